# Optimizing a Trainium2 kernel written in Bass

```python
import math
import jax, jax.numpy as jnp
from jax import lax
import numpy as np

D_MODEL = 1024
BATCH = 1
SEQ = 16384
DEPTH = 4

CHUNK = 64
N_A_LAYERS = DEPTH // 2
N_B_LAYERS = DEPTH - N_A_LAYERS

SSM_EXPAND = 2
D_INNER = SSM_EXPAND * D_MODEL
SSM_HEAD_DIM = 64
SSM_HEADS = D_INNER // SSM_HEAD_DIM
SSM_GROUPS = 8
SSM_HEADS_PER_GROUP = SSM_HEADS // SSM_GROUPS
D_STATE = 128
D_CONV = 4
CONV_DIM = D_INNER + 2 * SSM_GROUPS * D_STATE
D_IN_PROJ = 2 * D_INNER + 2 * SSM_GROUPS * D_STATE + SSM_HEADS
DT_MIN = 1e-3
DT_MAX = 1e-1

SB_HEADS = 16
SB_HEAD_DIM = D_MODEL // SB_HEADS
SB_WIDTH = SB_HEADS * SB_HEAD_DIM
Q_BLOCK = 128

D_FF = 4 * D_MODEL

DEEPNORM_ALPHA = (2 * DEPTH) ** 0.25
DEEPNORM_BETA = (8 * DEPTH) ** -0.25
LN_EPS = 1e-5
RMS_EPS = 1e-5

kernel_name = "ssd_stickbreak_yoco_deepnorm_trunk"


def layer_norm(x, g, b):
    xf = x.astype(jnp.float32)
    mu = jnp.mean(xf, axis=-1, keepdims=True)
    var = jnp.mean(jnp.square(xf - mu), axis=-1, keepdims=True)
    return ((xf - mu) * lax.rsqrt(var + LN_EPS) * g.astype(jnp.float32) + b.astype(jnp.float32)).astype(x.dtype)


def causal_depthwise_conv(u, w, b):
    c = u.shape[-1]
    y = lax.conv_general_dilated(
        u, w[:, None, :], window_strides=(1,), padding=[(D_CONV - 1, 0)],
        dimension_numbers=("NWC", "WIO", "NWC"), feature_group_count=c)
    return y + b


def segsum(a):
    t = a.shape[-1]
    cs = jnp.cumsum(a, axis=-1)
    diff = cs[..., :, None] - cs[..., None, :]
    mask = jnp.tril(jnp.ones((t, t), dtype=bool))
    return jnp.where(mask, diff, -jnp.inf)


def ssd_chunked_scan(x, da, bm, cm):
    b, l, h, p = x.shape
    c = l // CHUNK
    g, r, n = SSM_GROUPS, SSM_HEADS_PER_GROUP, D_STATE
    xc = x.reshape(b, c, CHUNK, g, r, p)
    bc = bm.reshape(b, c, CHUNK, g, n)
    cc = cm.reshape(b, c, CHUNK, g, n)
    a = da.astype(jnp.float32).reshape(b, c, CHUNK, g, r).transpose(0, 1, 3, 4, 2)
    a_cs = jnp.cumsum(a, axis=-1)
    decay_mat = jnp.exp(segsum(a))
    cb = jnp.einsum("bclgn,bcsgn->bcgls", cc, bc)
    y_diag = jnp.einsum("bcgls,bcgrls,bcsgrp->bclgrp", cb, decay_mat, xc)
    decay_states = jnp.exp(a_cs[..., -1:] - a_cs)
    states = jnp.einsum("bclgn,bcgrl,bclgrp->bcgrpn", bc, decay_states, xc).astype(jnp.float32)
    block_decay = jnp.exp(a_cs[..., -1])

    def step(carry, inp):
        s_c, d_c = inp
        return carry * d_c[..., None, None] + s_c, carry

    init = jnp.zeros((b, g, r, p, n), jnp.float32)
    _, prev = lax.scan(step, init, (jnp.moveaxis(states, 1, 0), jnp.moveaxis(block_decay, 1, 0)))
    prev = jnp.moveaxis(prev, 0, 1)
    y_off = jnp.einsum("bclgn,bcgrpn,bcgrl->bclgrp", cc, prev, jnp.exp(a_cs))
    return (y_diag + y_off).reshape(b, l, h, p).astype(x.dtype)


def mamba2_mixer(u, w_in, conv_w, conv_b, dt_bias, a_log, d_skip, norm_w, w_out):
    b, l, _ = u.shape
    zxbcdt = u @ w_in
    z = zxbcdt[..., :D_INNER]
    xbc = zxbcdt[..., D_INNER:D_INNER + CONV_DIM]
    dt_raw = zxbcdt[..., D_INNER + CONV_DIM:]
    xbc = jax.nn.silu(causal_depthwise_conv(xbc, conv_w, conv_b))
    xs = xbc[..., :D_INNER]
    bm = xbc[..., D_INNER:D_INNER + SSM_GROUPS * D_STATE].reshape(b, l, SSM_GROUPS, D_STATE)
    cm = xbc[..., D_INNER + SSM_GROUPS * D_STATE:].reshape(b, l, SSM_GROUPS, D_STATE)
    dt = jax.nn.softplus((dt_raw + dt_bias).astype(jnp.float32))
    a = -jnp.exp(a_log.astype(jnp.float32))
    xh = xs.reshape(b, l, SSM_HEADS, SSM_HEAD_DIM)
    y = ssd_chunked_scan(xh * dt[..., None].astype(xh.dtype), dt * a, bm, cm)
    y = (y + xh * d_skip[:, None]).reshape(b, l, D_INNER)
    yg = (y * jax.nn.silu(z)).astype(jnp.float32).reshape(b, l, SSM_GROUPS, D_INNER // SSM_GROUPS)
    yg = yg * lax.rsqrt(jnp.mean(jnp.square(yg), axis=-1, keepdims=True) + RMS_EPS)
    y = (yg.reshape(b, l, D_INNER) * norm_w.astype(jnp.float32)).astype(u.dtype)
    return y @ w_out


def shared_kv(h, w_k, w_v):
    b, l, _ = h.shape
    k = (h @ w_k).reshape(b, l, SB_HEADS, SB_HEAD_DIM).transpose(0, 2, 1, 3)
    v = (h @ w_v).reshape(b, l, SB_HEADS, SB_HEAD_DIM).transpose(0, 2, 1, 3)
    return k, v


def stick_breaking_attention(u, w_q, k, v, w_o):
    b, l, _ = u.shape
    q = (u @ w_q).reshape(b, l, SB_HEADS, SB_HEAD_DIM).transpose(0, 2, 1, 3) * (SB_HEAD_DIM ** -0.5)
    nb = l // Q_BLOCK
    kb = jnp.moveaxis(k.reshape(b, SB_HEADS, nb, Q_BLOCK, SB_HEAD_DIM), 2, 0)
    vb = jnp.moveaxis(v.reshape(b, SB_HEADS, nb, Q_BLOCK, SB_HEAD_DIM), 2, 0)
    idx = jnp.arange(Q_BLOCK)
    diag_mask = idx[None, :] < idx[:, None]
    later = (idx[:, None] > idx[None, :]).astype(jnp.float32)

    outs = []
    for i in range(nb):
        qi = q[:, :, i * Q_BLOCK:(i + 1) * Q_BLOCK]

        def step(carry, inp, qi=qi):
            acc, o = carry
            kj, vj, is_diag = inp
            z = jnp.einsum("bhqd,bhkd->bhqk", qi, kj).astype(jnp.float32)
            valid = jnp.logical_or(jnp.logical_not(is_diag), diag_mask)
            lo = jnp.where(valid, jax.nn.log_sigmoid(-z), 0.0)
            after = jnp.einsum("bhqj,js->bhqs", lo, later) + acc[..., None]
            logw = jnp.where(valid, z + lo + after, -jnp.inf)
            o = o + jnp.einsum("bhqk,bhkd->bhqd", jnp.exp(logw).astype(vj.dtype), vj).astype(jnp.float32)
            acc = acc + jnp.sum(lo, axis=-1)
            return (acc, o), None

        init = (jnp.zeros((b, SB_HEADS, Q_BLOCK), jnp.float32),
                jnp.zeros((b, SB_HEADS, Q_BLOCK, SB_HEAD_DIM), jnp.float32))
        xs = (kb[:i + 1][::-1], vb[:i + 1][::-1], jnp.arange(i + 1) == 0)
        (_, o_i), _ = lax.scan(step, init, xs)
        outs.append(o_i.astype(u.dtype))
    o = jnp.concatenate(outs, axis=2)
    o = o.transpose(0, 2, 1, 3).reshape(b, l, SB_WIDTH)
    return o @ w_o


def squared_relu_mlp(h, w1, w2):
    return jnp.square(jax.nn.relu(h @ w1)) @ w2


def setup_inputs(seed: int = 0) -> dict:
    key = jax.random.key(seed)
    ks = jax.random.split(key, 20)
    f32 = jnp.float32

    def nrm(k, shape, scale):
        return jax.random.normal(k, shape, f32) * scale

    x = nrm(ks[0], (BATCH, SEQ, D_MODEL), 1.0)
    ssm_w_in = nrm(ks[1], (N_A_LAYERS, D_MODEL, D_IN_PROJ), D_MODEL ** -0.5)
    ssm_conv_w = jax.random.uniform(ks[2], (N_A_LAYERS, D_CONV, CONV_DIM), f32, -0.5, 0.5)
    ssm_conv_b = nrm(ks[3], (N_A_LAYERS, CONV_DIM), 0.02)
    dt = jnp.exp(jax.random.uniform(ks[4], (N_A_LAYERS, SSM_HEADS), f32)
                 * (math.log(DT_MAX) - math.log(DT_MIN)) + math.log(DT_MIN))
    dt = jnp.maximum(dt, 1e-4)
    ssm_dt_bias = dt + jnp.log(-jnp.expm1(-dt))
    ssm_a_log = jnp.log(jax.random.uniform(ks[5], (N_A_LAYERS, SSM_HEADS), f32, 1.0, 16.0))
    ssm_d = 1.0 + nrm(ks[6], (N_A_LAYERS, SSM_HEADS), 0.1)
    ssm_norm_w = 1.0 + nrm(ks[7], (N_A_LAYERS, D_INNER), 0.1)
    ssm_w_out = nrm(ks[8], (N_A_LAYERS, D_INNER, D_MODEL), D_INNER ** -0.5 * DEEPNORM_BETA)
    sb_w_k = nrm(ks[9], (D_MODEL, SB_WIDTH), D_MODEL ** -0.5)
    sb_w_v = nrm(ks[10], (D_MODEL, SB_WIDTH), D_MODEL ** -0.5 * DEEPNORM_BETA)
    sb_w_q = nrm(ks[11], (N_B_LAYERS, D_MODEL, SB_WIDTH), D_MODEL ** -0.5)
    sb_w_o = nrm(ks[12], (N_B_LAYERS, SB_WIDTH, D_MODEL), SB_WIDTH ** -0.5 * DEEPNORM_BETA)
    mlp_w1 = nrm(ks[13], (DEPTH, D_MODEL, D_FF), D_MODEL ** -0.5)
    mlp_w2 = nrm(ks[14], (DEPTH, D_FF, D_MODEL), D_FF ** -0.5 * DEEPNORM_BETA)
    ln_mix_g = 1.0 + nrm(ks[15], (DEPTH, D_MODEL), 0.05)
    ln_mix_b = nrm(ks[16], (DEPTH, D_MODEL), 0.02)
    ln_mlp_g = 1.0 + nrm(ks[17], (DEPTH, D_MODEL), 0.05)
    ln_mlp_b = nrm(ks[18], (DEPTH, D_MODEL), 0.02)
    return {"x": x, "ssm_w_in": ssm_w_in, "ssm_conv_w": ssm_conv_w, "ssm_conv_b": ssm_conv_b,
            "ssm_dt_bias": ssm_dt_bias, "ssm_a_log": ssm_a_log, "ssm_d": ssm_d,
            "ssm_norm_w": ssm_norm_w, "ssm_w_out": ssm_w_out, "sb_w_k": sb_w_k, "sb_w_v": sb_w_v,
            "sb_w_q": sb_w_q, "sb_w_o": sb_w_o, "mlp_w1": mlp_w1, "mlp_w2": mlp_w2,
            "ln_mix_g": ln_mix_g, "ln_mix_b": ln_mix_b, "ln_mlp_g": ln_mlp_g, "ln_mlp_b": ln_mlp_b}


def reference(x, ssm_w_in, ssm_conv_w, ssm_conv_b, ssm_dt_bias, ssm_a_log, ssm_d, ssm_norm_w,
              ssm_w_out, sb_w_k, sb_w_v, sb_w_q, sb_w_o, mlp_w1, mlp_w2,
              ln_mix_g, ln_mix_b, ln_mlp_g, ln_mlp_b):
    h = x
    k = v = None
    for layer in range(DEPTH):
        if layer < N_A_LAYERS:
            i = layer
            mix = mamba2_mixer(h, ssm_w_in[i], ssm_conv_w[i], ssm_conv_b[i], ssm_dt_bias[i],
                               ssm_a_log[i], ssm_d[i], ssm_norm_w[i], ssm_w_out[i])
        else:
            if layer == N_A_LAYERS:
                k, v = shared_kv(h, sb_w_k, sb_w_v)
            j = layer - N_A_LAYERS
            mix = stick_breaking_attention(h, sb_w_q[j], k, v, sb_w_o[j])
        h = layer_norm(DEEPNORM_ALPHA * h + mix, ln_mix_g[layer], ln_mix_b[layer])
        h = layer_norm(DEEPNORM_ALPHA * h + squared_relu_mlp(h, mlp_w1[layer], mlp_w2[layer]),
                       ln_mlp_g[layer], ln_mlp_b[layer])
    return h
```

```python
import math
import numpy as np
import ml_dtypes
import concourse.bass as bass
import concourse.mybir as mybir
from concourse.bass_utils import run_bass_kernel_spmd

F32 = mybir.dt.float32
BF16 = mybir.dt.bfloat16
AF = mybir.ActivationFunctionType
ALU = mybir.AluOpType
AX = mybir.AxisListType

NCORES = 8
SEQ = 16384
D = 1024
DEPTH = 4
TOK = SEQ // NCORES
ALPHA = (2 * DEPTH) ** 0.25
LN_EPS = 1e-5
RMS_EPS = 1e-5
D_INNER = 2048
D_FF = 4096
NBLK = SEQ // 128

ENGS = ("pe", "act", "dve", "pool", "sp")
NRING = 6


class Tok:
    __slots__ = ("name", "w", "rs", "rd", "excl")

    def __init__(self, name="", excl=False):
        self.name = name
        self.w = None
        self.rs = {}
        self.rd = []
        self.excl = excl


class Op:
    __slots__ = ("eng", "fn", "deps", "dma", "sig", "sem", "val", "prev")


class Prog:
    def __init__(self, nc):
        self.nc = nc
        self.ops = {e: [] for e in ENGS}

    def add(self, eng, fn, r=(), w=(), dma=False):
        op = Op()
        op.eng, op.fn, op.dma, op.sig, op.sem, op.val, op.prev = eng, fn, dma, False, None, 0, 0
        deps = set()
        for t in r:
            if t.w is not None:
                deps.add(t.w)
            if t.excl:
                deps.update(o for en, o in t.rs.items() if en != eng)
        for t in w:
            if t.w is not None:
                deps.add(t.w)
            deps.update(t.rs.values())
            deps.update(t.rd)
        if eng == "pe" and not dma:
            deps = {d for d in deps if d.dma or d.eng != "pe"}
        deps.discard(op)
        op.deps = deps
        for t in r:
            if dma:
                t.rd.append(op)
            else:
                t.rs[eng] = op
        for t in w:
            t.w = op
            t.rs = {}
            t.rd = []
        self.ops[eng].append(op)
        return op

    def emit(self):
        nc = self.nc
        for e in ENGS:
            for op in self.ops[e]:
                for d in op.deps:
                    d.sig = True
        csem = {e: nc.alloc_semaphore(name=f"c_{e}") for e in ENGS if e != "sp"}
        ring = {e: [nc.alloc_semaphore(name=f"r_{e}{i}") for i in range(NRING)] for e in ENGS}
        final = {}
        for e in ENGS:
            cnt = 0
            nd = 0
            for op in self.ops[e]:
                if op.dma:
                    op.sem = ring[e][nd % NRING]
                    op.prev = 16 * (nd // NRING)
                    op.val = op.prev + 16
                    final[op.sem] = op.val
                    nd += 1
                elif op.sig:
                    cnt += 1
                    op.sem = csem[e]
                    op.val = cnt
        engobj = {"pe": "tensor", "act": "scalar", "dve": "vector", "pool": "gpsimd", "sp": "sync"}
        ops = self.ops

        def run(e, eng):
            waited = {}
            for op in ops[e]:
                needs = {}
                for d in op.deps:
                    if needs.get(d.sem, 0) < d.val:
                        needs[d.sem] = d.val
                if op.dma and op.prev > 0 and needs.get(op.sem, 0) < op.prev:
                    needs[op.sem] = op.prev
                for sem, val in needs.items():
                    if waited.get(sem, 0) < val:
                        eng.wait_ge(sem, val)
                        waited[sem] = val
                ins = op.fn(eng)
                if op.dma:
                    ins.then_inc(op.sem, 16)
                elif op.sig:
                    ins.then_inc(op.sem, 1)
            if e == "sp":
                for sem, val in final.items():
                    if waited.get(sem, 0) < val:
                        eng.wait_ge(sem, val)

        with nc.Block() as block:
            @block.tensor
            def _(eng):
                run("pe", eng)

            @block.scalar
            def _(eng):
                run("act", eng)

            @block.vector
            def _(eng):
                run("dve", eng)

            @block.gpsimd
            def _(eng):
                run("pool", eng)

            @block.sync
            def _(eng):
                run("sp", eng)


class PProxy:
    def __init__(self, real):
        self.real = real
        self.rec = None

    def add(self, *a, **k):
        if self.rec is not None:
            self.rec.append((a, k))
            return None
        return self.real.add(*a, **k)

    def record(self, fn, *args):
        self.rec = []
        fn(*args)
        items, self.rec = self.rec, None
        return items

    def merge(self, *lists):
        total = max(len(l) for l in lists)
        pos = [0] * len(lists)
        for step in range(1, total + 1):
            for i, l in enumerate(lists):
                upto = (len(l) * step) // total
                while pos[i] < upto:
                    a, k = l[pos[i]]
                    self.real.add(*a, **k)
                    pos[i] += 1

    def emit(self):
        self.real.emit()


class Ctx:
    def __init__(self, nbanks=8):
        self.nc = bass.Bass("TRN2", target_bir_lowering=False)
        self.p = Prog(self.nc)
        self.nb = nbanks
        self.ps = self.nc.alloc_psum_tensor("psum_all", [128, nbanks, 512], F32)
        self.ps_tok = [Tok(f"ps{i}", excl=True) for i in range(nbanks)]
        self.ps_next = 0
        self.nbuf = 0

    def dbuf(self, name, shape, dt, n=2):
        bufs = [self.nc.alloc_sbuf_tensor(f"{name}_{i}", shape, dt) for i in range(n)]
        toks = [Tok(f"{name}_{i}") for i in range(n)]
        return bufs, toks

    def sb(self, name, shape, dt):
        return self.nc.alloc_sbuf_tensor(name, shape, dt)

    def bank(self):
        b = self.ps_next
        self.ps_next = (b + 1) % self.nb
        return self.ps[:, b, :], self.ps_tok[b]

    def dram_in(self, name, shape, dt):
        return self.nc.dram_tensor(name, list(shape), dt, kind="ExternalInput").ap()

    def dram_out(self, name, shape, dt):
        return self.nc.dram_tensor(name, list(shape), dt, kind="ExternalOutput").ap()

    def dma(self, out, in_, r=(), w=(), eng=None):
        if eng is None:
            eng = "sp"
        return self.p.add(eng, lambda e, o=out, i=in_: e.dma_start(out=o, in_=i), r=r, w=w, dma=True)

    def dma_cast(self, out, in_, r=(), w=()):
        return self.p.add("pool", lambda e, o=out, i=in_: e.dma_start(out=o, in_=i), r=r, w=w, dma=True)


def load_cast_rows(cx, dst, src, nk, ncols, wtok, chunk=2048):
    v = src.rearrange("(k p) n -> p k n", p=128)
    for k in range(nk):
        for c0 in range(0, ncols, chunk):
            c1 = min(ncols, c0 + chunk)
            cx.dma_cast(dst[:, k, c0:c1], v[:, k, c0:c1], w=[wtok])


def layer_norm_tile(cx, r_sb, r_tok, out_sb, out_tok, g_sb, b_sb, eps_sb, scr, gtok):
    p = cx.p
    st, mv, rstd, st_tok = scr
    for c in range(2):
        p.add("dve", lambda e, c=c: e.bn_stats(out=st[:, c, :], in_=r_sb[:, c * 512:(c + 1) * 512]),
              r=[r_tok], w=[st_tok])
    p.add("dve", lambda e: e.bn_aggr(out=mv[:, :], in_=st[:, :, :].rearrange("p a b -> p (a b)")), r=[st_tok], w=[st_tok])
    p.add("act", lambda e: e.activation(out=rstd[:, :], in_=mv[:, 1:2], func=AF.Sqrt, bias=eps_sb[:, 0:1], scale=1.0),
          r=[st_tok], w=[st_tok])
    p.add("dve", lambda e: e.reciprocal(out=rstd[:, :], in_=rstd[:, :]), r=[st_tok], w=[st_tok])
    p.add("dve", lambda e: e.tensor_scalar(out=out_sb, in0=r_sb, scalar1=mv[:, 0:1], scalar2=rstd[:, 0:1],
                                           op0=ALU.subtract, op1=ALU.mult), r=[r_tok, st_tok], w=[out_tok])
    p.add("pool", lambda e: e.tensor_tensor(out=out_sb, in0=out_sb, in1=g_sb, op=ALU.mult), r=[out_tok, gtok], w=[out_tok])
    p.add("pool", lambda e: e.tensor_tensor(out=out_sb, in0=out_sb, in1=b_sb, op=ALU.add), r=[out_tok, gtok], w=[out_tok])


def build_tphase(kin):
    cx = Ctx()
    nc, p = cx.nc, cx.p
    KC = kin // 128
    G = 128
    CPB = 512 // G
    NG = TOK // G
    yT = cx.dram_in("yT", [kin, TOK], BF16)
    h_in = cx.dram_in("h", [TOK, D], F32)
    w_o = cx.dram_in("w_o", [kin, D], F32)
    w1 = cx.dram_in("w1", [D, D_FF], F32)
    w2 = cx.dram_in("w2", [D_FF, D], F32)
    lnp = cx.dram_in("lnp", [128, 4, D], F32)
    h_out = cx.dram_out("h_out", [TOK, D], F32)

    wo_sb = cx.sb("wo_sb", [128, KC, D], BF16)
    w1_sb = cx.sb("w1_sb", [128, 8, D_FF], BF16)
    w2_sb = cx.sb("w2_sb", [128, 32, D], BF16)
    lnp_sb = cx.sb("lnp_sb", [128, 4, D], F32)
    ident = cx.sb("ident_sb", [128, 128], F32)
    eps_sb = cx.sb("eps_sb", [128, 1], F32)
    t_wo, t_w1, t_w2, t_lnp, t_const = Tok("wo"), Tok("w1"), Tok("w2"), Tok("lnp"), Tok("const")

    ident_d = cx.dram_in("ident", [128, 128], F32)
    cx.dma(ident[:, :], ident_d[:, :], w=[t_const])
    p.add("pool", lambda e: e.memset(eps_sb[:, :], LN_EPS), w=[t_const])
    cx.dma(lnp_sb[:, :, :], lnp[:, :, :], w=[t_lnp])

    yT_sb = cx.sb("yT_sb", [128, KC, G], BF16)
    h_sb = cx.sb("h_sb", [128, D], F32)
    t_yT, t_h = Tok("yT"), Tok("h")
    rbuf = [cx.sb(f"rbuf{i}", [128, D], F32) for i in range(2)]
    t_rb = [Tok("rb0"), Tok("rb1")]
    h1T = cx.sb("h1T", [128, 8, G], BF16)
    t_h1T = Tok("h1T")
    uT = cx.sb("uT", [128, 32, G], BF16)
    t_uT = [Tok(f"uT{i}") for i in range(32 // CPB)]
    utmp = [cx.sb(f"utmp{i}", [128, CPB * G], F32) for i in range(2)]
    t_utmp = [Tok("utmp0"), Tok("utmp1")]
    st = cx.sb("ln_st", [128, 2, 6], F32)
    mv = cx.sb("ln_mv", [128, 2], F32)
    rstd = cx.sb("ln_rstd", [128, 1], F32)
    scr = (st, mv, rstd, Tok("lnscr"))

    yT_v = yT.rearrange("(k p) t -> p k t", p=128)
    h_v = h_in.rearrange("(n p) d -> p n d", p=128)
    ho_v = h_out.rearrange("(n p) d -> p n d", p=128)

    def load_group(g):
        cx.dma(yT_sb[:, :, :], yT_v[:, :, g * G:(g + 1) * G], w=[t_yT])
        cx.dma(h_sb[:, :], h_v[:, g, :], w=[t_h])

    load_group(0)
    load_cast_rows(cx, wo_sb, w_o, KC, D, t_wo)
    load_cast_rows(cx, w1_sb, w1, 8, D_FF, t_w1)
    load_cast_rows(cx, w2_sb, w2, 32, D, t_w2)

    for g in range(NG):
        rb, trb = rbuf[g % 2], t_rb[g % 2]
        for n in range(2):
            bank, bt = cx.bank()
            for k in range(KC):
                p.add("pe", lambda e, bank=bank, k=k, n=n: e.matmul(
                    bank, lhsT=yT_sb[:, k, :], rhs=wo_sb[:, k, n * 512:(n + 1) * 512],
                    start=(k == 0), stop=(k == KC - 1)), r=[t_yT, t_wo], w=[bt])
            p.add("dve", lambda e, bank=bank, n=n, rb=rb: e.scalar_tensor_tensor(
                out=rb[:, n * 512:(n + 1) * 512], in0=h_sb[:, n * 512:(n + 1) * 512], scalar=ALPHA,
                in1=bank, op0=ALU.mult, op1=ALU.add), r=[bt, t_h], w=[trb])
        if g + 1 < NG:
            load_group(g + 1)
        layer_norm_tile(cx, rb[:, :], trb, rb[:, :], trb, lnp_sb[:, 0, :], lnp_sb[:, 1, :], eps_sb, scr, t_lnp)
        for q in range(2):
            bank, bt = cx.bank()
            for j in range(4):
                f = q * 4 + j
                p.add("pe", lambda e, bank=bank, j=j, f=f, rb=rb: e.transpose(
                    bank[:, j * 128:(j + 1) * 128], rb[:, f * 128:(f + 1) * 128], ident[:, :]),
                    r=[trb, t_const], w=[bt])
            p.add("act", lambda e, bank=bank, q=q: e.activation(
                out=h1T[:, q * 4:(q + 1) * 4, :], in_=bank.rearrange("p (j c) -> p j c", j=4), func=AF.Copy),
                r=[bt], w=[t_h1T])
        for cp in range(32 // CPB):
            bank, bt = cx.bank()
            for j in range(CPB):
                ch = cp * CPB + j
                for k in range(8):
                    p.add("pe", lambda e, bank=bank, j=j, ch=ch, k=k: e.matmul(
                        bank[:, j * G:(j + 1) * G], lhsT=w1_sb[:, k, ch * 128:(ch + 1) * 128], rhs=h1T[:, k, :],
                        start=(k == 0), stop=(k == 7)), r=[t_w1, t_h1T], w=[bt])
            ub = cp % 2
            p.add("act", lambda e, bank=bank, ub=ub: e.activation(out=utmp[ub][:, :], in_=bank[:, 0:CPB * G], func=AF.Relu),
                  r=[bt], w=[t_utmp[ub]])
            p.add("pool", lambda e, cp=cp, ub=ub: e.tensor_tensor(
                out=uT[:, CPB * cp:CPB * cp + CPB, :], in0=utmp[ub][:, :].rearrange("p (j c) -> p j c", j=CPB),
                in1=utmp[ub][:, :].rearrange("p (j c) -> p j c", j=CPB), op=ALU.mult), r=[t_utmp[ub]], w=[t_uT[cp]])
        for n in range(2):
            bank, bt = cx.bank()
            for k in range(32):
                p.add("pe", lambda e, bank=bank, k=k, n=n: e.matmul(
                    bank, lhsT=uT[:, k, :], rhs=w2_sb[:, k, n * 512:(n + 1) * 512],
                    start=(k == 0), stop=(k == 31)), r=[t_uT[k // CPB], t_w2], w=[bt])
            p.add("dve", lambda e, bank=bank, n=n, rb=rb: e.scalar_tensor_tensor(
                out=rb[:, n * 512:(n + 1) * 512], in0=rb[:, n * 512:(n + 1) * 512], scalar=ALPHA,
                in1=bank, op0=ALU.mult, op1=ALU.add), r=[bt, trb], w=[trb])
        layer_norm_tile(cx, rb[:, :], trb, rb[:, :], trb, lnp_sb[:, 2, :], lnp_sb[:, 3, :], eps_sb, scr, t_lnp)
        cx.dma(ho_v[:, g, :], rb[:, :], r=[trb])
    p.emit()
    return nc


def build_ssd(nblocks=SEQ // 512):
    cx = Ctx(nbanks=5)
    cx.p = PProxy(cx.p)
    pool_ctr = {"t": 0, "b": 0}

    def tbank():
        i = pool_ctr["t"] % 3
        pool_ctr["t"] += 1
        return cx.ps[:, i, :], cx.ps_tok[i]

    def bbank():
        i = 3 + pool_ctr["b"] % 2
        pool_ctr["b"] += 1
        return cx.ps[:, i, :], cx.ps_tok[i]

    nc, p = cx.nc, cx.p
    BLK = 512
    seq = nblocks * BLK
    psB = nc.alloc_psum_tensor("psum_bf", [128, 1024], BF16)
    t_psB = Tok("psB", excl=True)
    psYS = nc.alloc_psum_tensor("psum_ys", [128, 2, 512], F32)
    t_YS = [Tok("YS0", excl=True), Tok("YS1", excl=True)]

    hT = cx.dram_in("hT", [D, seq], F32)
    wfm = cx.dram_in("wfm", [D, 512], F32)
    wtm = cx.dram_in("wtm", [D, 260], F32)
    cw = cx.dram_in("cw", [128, 4, 4], F32)
    cb = cx.dram_in("cb", [128, 4], F32)
    hp = cx.dram_in("hp", [128, 3, 4], F32)
    nw = cx.dram_in("nw", [128, 256], F32)
    cst_f = cx.dram_in("cst_f", [128, 2, 128], F32)
    cst_b = cx.dram_in("cst_b", [128, 128 + 512], BF16)
    yn = cx.dram_out("yn", [seq, 256], BF16)

    wfm_sb = cx.sb("wfm_sb", [128, 8, 512], BF16)
    wtm_sb = cx.sb("wtm_sb", [128, 8, 260], BF16)
    cw_sb = cx.sb("cw_sb", [128, 4, 4], F32)
    cb_sb = cx.sb("cb_sb", [128, 4], F32)
    hp_sb = cx.sb("hp_sb", [128, 3, 4], F32)
    nw_sb = cx.sb("nw_sb", [128, 256], F32)
    cf_sb = cx.sb("cf_sb", [128, 2, 128], F32)
    cbf_sb = cx.sb("cbf_sb", [128, 640], BF16)
    aneg = cx.sb("aneg", [128, 4], F32)
    onecol = cx.sb("onecol", [128, 1], F32)
    neghalf = cx.sb("neghalf", [128, 1], F32)
    S = cx.sb("S_state", [128, 256], F32)
    S_bf = cx.sb("S_bf", [128, 256], BF16)
    t_w, t_c, t_S, t_Sbf = Tok("w"), Tok("c"), Tok("S"), Tok("Sbf")
    triu, ones_f = cf_sb[:, 0, :], cf_sb[:, 1, :]
    ident_b, negmask = cbf_sb[:, 0:128], cbf_sb[:, 128:640]

    for dst, src in ((cw_sb, cw), (cb_sb, cb), (hp_sb, hp), (nw_sb, nw), (cf_sb, cst_f), (cbf_sb, cst_b)):
        nd = len(dst.shape)
        idx = tuple([slice(None)] * nd)
        cx.dma(dst[idx], src[idx], w=[t_c])
    load_cast_rows(cx, wfm_sb, wfm, 8, 512, t_w)
    load_cast_rows(cx, wtm_sb, wtm, 8, 260, t_w)
    p.add("pool", lambda e: e.memset(onecol[:, :], 1.0), w=[t_c])
    p.add("pool", lambda e: e.memset(neghalf[:, :], -0.5), w=[t_c])
    p.add("pool", lambda e: e.memset(S[:, :], 0.0), w=[t_S])
    p.add("pool", lambda e: e.memset(S_bf[:, :], 0.0), w=[t_Sbf])
    p.add("act", lambda e: e.activation(out=aneg[:, :], in_=hp_sb[:, 1, :], func=AF.Exp), r=[t_c], w=[t_c])
    p.add("dve", lambda e: e.tensor_scalar(out=aneg[:, :], in0=aneg[:, :], scalar1=-1.0, scalar2=None, op0=ALU.mult),
          r=[t_c], w=[t_c])

    hTf, t_hTf = cx.dbuf("hTf", [128, 8, BLK], F32)
    hTb, t_hTb = cx.dbuf("hTb", [128, 8, BLK], BF16, n=3)
    xpre, t_xpre = cx.dbuf("xpre", [128, 4, 3 + BLK], F32)
    acc, t_acc = cx.dbuf("cacc", [128, BLK], F32)
    xbcT, t_xbcT = cx.dbuf("xbcT", [128, 4, BLK], BF16)
    zs, t_zs = cx.dbuf("zs", [128, 4, 256], F32)
    dtr, t_dtr = cx.dbuf("dtr", [128, 4, 4], F32)
    dte, t_dte = cx.dbuf("dte", [128, 4, 4], F32)
    dts, t_dts = cx.dbuf("dts", [128, 4, 4], F32)
    das, t_das = cx.dbuf("das", [128, 4, 4], F32)
    xdt, t_xdt = cx.dbuf("xdt", [128, 256], BF16)
    ysk, t_ysk = cx.dbuf("ysk", [128, 256], F32)
    Btm, t_Btm = cx.dbuf("Btm", [128, 128], BF16)
    daT, t_daT = cx.dbuf("daT", [128, 4, 128], F32)
    nacs, t_nacs = cx.dbuf("nacs", [128, 4], F32)
    e_sb, t_e = cx.dbuf("e_sb", [128, 4], F32)
    w_sb, t_wd = cx.dbuf("w_sb", [128, 4], F32)
    dec, t_dec = cx.dbuf("dec", [128, 4], F32)
    Dm, t_Dm = cx.dbuf("Dm", [128, 4, 128], F32)
    Gm, t_G = cx.dbuf("Gm", [128, 4, 128], BF16)
    yc, t_yc = cx.dbuf("yc", [128, 256], F32)
    sq, t_sq = cx.dbuf("sq", [128, 256], F32)
    ss, t_ss = cx.dbuf("ss", [128, 2], F32)
    yno, t_yno = cx.dbuf("yno", [128, 256], BF16)
    xdw, t_xdw = cx.dbuf("xdw", [128, 256], BF16)

    hT_v = hT.rearrange("(k p) t -> p k t", p=128)
    p.add("pool", lambda e: e.memset(xpre[1][:, :, :], 0.0), w=[t_xpre[1]])

    def load_block(b):
        fb, hb = b % 2, b % 3
        for half in range(2):
            cx.dma(hTf[fb][:, half * 4:(half + 1) * 4, :], hT_v[:, half * 4:(half + 1) * 4, b * BLK:(b + 1) * BLK],
                   w=[t_hTf[fb]])
        p.add("act", lambda e, fb=fb, hb=hb: e.activation(out=hTb[hb][:, 0:4, :], in_=hTf[fb][:, 0:4, :], func=AF.Copy),
              r=[t_hTf[fb]], w=[t_hTb[hb]])
        p.add("dve", lambda e, fb=fb, hb=hb: e.tensor_copy(out=hTb[hb][:, 4:8, :], in_=hTf[fb][:, 4:8, :]),
              r=[t_hTf[fb]], w=[t_hTb[hb]])

    load_block(0)
    if nblocks > 1:
        load_block(1)
    bc4 = lambda ap: ap.unsqueeze(2).broadcast_to([128, 4, 64])
    v464 = lambda ap: ap.rearrange("p (h c) -> p h c", h=4)
    def blockpro(b):
        pb = b % 2
        hb = b % 3
        if b + 2 < nblocks:
            load_block(b + 2)
        p.add("pool", lambda e, pb=pb: e.tensor_copy(out=xpre[pb][:, :, 0:3], in_=xpre[1 - pb][:, :, BLK:BLK + 3]),
              r=[t_xpre[1 - pb]], w=[t_xpre[pb]])
        for ct in range(4):
            bank, bt = bbank()
            for k in range(8):
                p.add("pe", lambda e, bank=bank, k=k, ct=ct, hb=hb: e.matmul(
                    bank, lhsT=wfm_sb[:, k, ct * 128:(ct + 1) * 128], rhs=hTb[hb][:, k, :],
                    start=(k == 0), stop=(k == 7)), r=[t_w, t_hTb[hb]], w=[bt])
            p.add("act", lambda e, bank=bank, ct=ct, pb=pb: e.activation(
                out=xpre[pb][:, ct, 3:3 + BLK], in_=bank, func=AF.Copy), r=[bt], w=[t_xpre[pb]])
        for ct in range(4):
            a = ct % 2
            p.add("act", lambda e, ct=ct, pb=pb, a=a: e.activation(
                out=acc[a][:, :], in_=xpre[pb][:, ct, 0:BLK], func=AF.Identity,
                scale=cw_sb[:, ct, 0:1], bias=cb_sb[:, ct:ct + 1]), r=[t_xpre[pb], t_c], w=[t_acc[a]])
            for k in range(1, 4):
                p.add("dve", lambda e, ct=ct, pb=pb, a=a, k=k: e.scalar_tensor_tensor(
                    out=acc[a][:, :], in0=xpre[pb][:, ct, k:k + BLK], scalar=cw_sb[:, ct, k:k + 1], in1=acc[a][:, :],
                    op0=ALU.mult, op1=ALU.add), r=[t_xpre[pb], t_acc[a], t_c], w=[t_acc[a]])
            p.add("act", lambda e, ct=ct, pb=pb, a=a: e.activation(
                out=xbcT[pb][:, ct, :], in_=acc[a][:, :], func=AF.Silu), r=[t_acc[a]], w=[t_xbcT[pb]])
        for t in range(4):
            bank, bt = bbank()
            for k in range(8):
                p.add("pe", lambda e, bank=bank, k=k, t=t, hb=hb: e.matmul(
                    bank[:, 0:260], lhsT=hTb[hb][:, k, t * 128:(t + 1) * 128], rhs=wtm_sb[:, k, :],
                    start=(k == 0), stop=(k == 7)), r=[t_w, t_hTb[hb]], w=[bt])
            p.add("act", lambda e, bank=bank, t=t, pb=pb: e.activation(
                out=zs[pb][:, t, :], in_=bank[:, 0:256], func=AF.Silu), r=[bt], w=[t_zs[pb]])
            p.add("dve", lambda e, bank=bank, t=t, pb=pb: e.tensor_tensor(
                out=dtr[pb][:, t, :], in0=bank[:, 256:260], in1=hp_sb[:, 0, :], op=ALU.add),
                r=[bt, t_c], w=[t_dtr[pb]])
        p.add("act", lambda e, pb=pb: e.activation(out=dte[pb][:, :, :], in_=dtr[pb][:, :, :], func=AF.Exp),
              r=[t_dtr[pb]], w=[t_dte[pb]])
        p.add("act", lambda e, pb=pb: e.activation(out=dts[pb][:, :, :], in_=dte[pb][:, :, :], func=AF.Ln,
                                                   bias=onecol[:, 0:1], scale=1.0), r=[t_dte[pb], t_c], w=[t_dts[pb]])
        p.add("dve", lambda e, pb=pb: e.tensor_tensor(
            out=das[pb][:, :, :], in0=dts[pb][:, :, :], in1=aneg[:, :].unsqueeze(1).broadcast_to([128, 4, 4]),
            op=ALU.mult), r=[t_dts[pb], t_c], w=[t_das[pb]])
    def front(n):
        if True:
            b, t = divmod(n, 4)
            pb, q, c0 = b % 2, n % 2, t * 128
            for j in range(3):
                p.add("pe", lambda e, j=j, pb=pb, c0=c0: e.transpose(
                    psB[:, j * 128:(j + 1) * 128], xbcT[pb][:, j, c0:c0 + 128], ident_b),
                    r=[t_xbcT[pb], t_c], w=[t_psB])
            p.add("dve", lambda e, q=q, pb=pb, t=t: e.tensor_tensor(
                out=v464(xdt[q][:, :]), in0=v464(psB[:, 0:256]), in1=bc4(dts[pb][:, t, :]), op=ALU.mult),
                r=[t_psB, t_dts[pb]], w=[t_xdt[q]])
            p.add("dve", lambda e, q=q: e.tensor_tensor(
                out=v464(ysk[q][:, :]), in0=v464(psB[:, 0:256]), in1=bc4(hp_sb[:, 2, :]), op=ALU.mult),
                r=[t_psB, t_c], w=[t_ysk[q]])
            p.add("act", lambda e, q=q: e.activation(out=Btm[q][:, :], in_=psB[:, 256:384], func=AF.Copy),
                  r=[t_psB], w=[t_Btm[q]])
            bankA, btA = tbank()
            p.add("pe", lambda e, bankA=bankA, pb=pb, t=t: e.matmul(
                bankA[:, 0:4], lhsT=triu, rhs=das[pb][:, t, :], start=True, stop=True), r=[t_c, t_das[pb]], w=[btA])
            p.add("dve", lambda e, q=q, pb=pb, t=t: e.tensor_tensor(
                out=daT[q][:, :, :], in0=triu.unsqueeze(1).broadcast_to([128, 4, 128]),
                in1=das[pb][:, t, :].unsqueeze(2).broadcast_to([128, 4, 128]), op=ALU.mult),
                r=[t_c, t_das[pb]], w=[t_daT[q]])
            bankR, btR = tbank()
            p.add("pe", lambda e, bankR=bankR, q=q: e.matmul(
                bankR, lhsT=ones_f, rhs=daT[q][:, :, :].rearrange("p h l -> p (h l)"), start=True, stop=False),
                r=[t_c, t_daT[q]], w=[btR])
            p.add("pe", lambda e, bankR=bankR: e.matmul(bankR, lhsT=ident_b, rhs=negmask, start=False, stop=True),
                  r=[t_c], w=[btR])
            p.add("act", lambda e, bankA=bankA, q=q: e.activation(
                out=nacs[q][:, :], in_=bankA[:, 0:4], func=AF.Identity, scale=-1.0), r=[btA], w=[t_nacs[q]])
            p.add("act", lambda e, bankA=bankA, q=q: e.activation(out=e_sb[q][:, :], in_=bankA[:, 0:4], func=AF.Exp),
                  r=[btA], w=[t_e[q]])
            Rv = bankR.rearrange("p (h l) -> p h l", h=4)
            for h in range(4):
                p.add("act", lambda e, bankR=bankR, q=q, h=h: e.activation(
                    out=Dm[q][:, h, :], in_=bankR[:, h * 128:(h + 1) * 128], func=AF.Exp,
                    bias=nacs[q][:, h:h + 1], scale=1.0), r=[btR, t_nacs[q]], w=[t_Dm[q]])
            p.add("dve", lambda e, Rv=Rv, q=q: e.tensor_tensor(
                out=w_sb[q][:, :], in0=Rv[:, :, 127], in1=nacs[q][:, :], op=ALU.add),
                r=[btR, t_nacs[q]], w=[t_wd[q]])
            p.add("act", lambda e, q=q: e.activation(out=w_sb[q][:, :], in_=w_sb[q][:, :], func=AF.Exp),
                  r=[t_wd[q]], w=[t_wd[q]])
            p.add("act", lambda e, Rv=Rv, q=q: e.activation(out=dec[q][:, :], in_=Rv[:, :, 127], func=AF.Exp),
                  r=[btR], w=[t_dec[q]])
            bankC, btC = bankA[:, 128:256], btA
            p.add("pe", lambda e, bankC=bankC, pb=pb, c0=c0: e.matmul(
                bankC, lhsT=xbcT[pb][:, 2, c0:c0 + 128], rhs=xbcT[pb][:, 3, c0:c0 + 128],
                start=True, stop=True), r=[t_xbcT[pb]], w=[btC])
            p.add("dve", lambda e, bankC=bankC, q=q: e.tensor_tensor(
                out=Gm[q][:, :, :], in0=Dm[q][:, :, :], in1=bankC.unsqueeze(1).broadcast_to([128, 4, 128]),
                op=ALU.mult), r=[btC, t_Dm[q]], w=[t_G[q]])
            bankY, btY = psYS[:, q, :], t_YS[q]
            for h in range(4):
                p.add("pe", lambda e, bankY=bankY, q=q, h=h: e.matmul(
                    bankY[:, h * 64:(h + 1) * 64], lhsT=Gm[q][:, h, :], rhs=xdt[q][:, h * 64:(h + 1) * 64],
                    start=True, stop=True), r=[t_G[q], t_xdt[q]], w=[btY])
            p.add("pool", lambda e, q=q: e.tensor_tensor(
                out=v464(xdw[q][:, :]), in0=v464(xdt[q][:, :]), in1=bc4(w_sb[q][:, :]), op=ALU.mult),
                r=[t_xdt[q], t_wd[q]], w=[t_xdw[q]])
            p.add("pe", lambda e, q=q: e.matmul(
                psYS[:, q, 256:512], lhsT=Btm[q][:, :], rhs=xdw[q][:, :], start=True, stop=True),
                r=[t_Btm[q], t_xdw[q]], w=[t_YS[q]])
    def back(n):
        if True:
            b, t = divmod(n, 4)
            pb, q, c0 = b % 2, n % 2, t * 128
            row0 = b * BLK + c0
            bankY, btY = psYS[:, q, :], t_YS[q]
            bankO, btO = tbank()
            p.add("pe", lambda e, bankO=bankO, pb=pb, c0=c0: e.matmul(
                bankO[:, 0:256], lhsT=xbcT[pb][:, 3, c0:c0 + 128], rhs=S_bf[:, :], start=True, stop=True),
                r=[t_xbcT[pb], t_Sbf], w=[btO])
            p.add("dve", lambda e, bankO=bankO, q=q: e.tensor_tensor(
                out=v464(yc[q][:, :]), in0=v464(bankO[:, 0:256]), in1=bc4(e_sb[q][:, :]), op=ALU.mult),
                r=[btO, t_e[q]], w=[t_yc[q]])
            p.add("dve", lambda e, bankY=bankY, q=q: e.tensor_tensor(
                out=yc[q][:, :], in0=yc[q][:, :], in1=bankY[:, 0:256], op=ALU.add), r=[btY, t_yc[q]], w=[t_yc[q]])
            p.add("pool", lambda e, q=q: e.tensor_tensor(out=yc[q][:, :], in0=yc[q][:, :], in1=ysk[q][:, :], op=ALU.add),
                  r=[t_yc[q], t_ysk[q]], w=[t_yc[q]])
            p.add("pool", lambda e, q=q, pb=pb, t=t: e.tensor_tensor(
                out=yc[q][:, :], in0=yc[q][:, :], in1=zs[pb][:, t, :], op=ALU.mult),
                r=[t_yc[q], t_zs[pb]], w=[t_yc[q]])
            p.add("act", lambda e, q=q: e.activation(out=sq[q][:, :], in_=yc[q][:, :], func=AF.Square,
                                                     accum_out=ss[q][:, 0:1]), r=[t_yc[q]], w=[t_sq[q], t_ss[q]])
            p.add("dve", lambda e, q=q: e.tensor_scalar(out=ss[q][:, 1:2], in0=ss[q][:, 0:1], scalar1=1.0 / 256.0,
                                                        scalar2=RMS_EPS, op0=ALU.mult, op1=ALU.add),
                  r=[t_ss[q]], w=[t_ss[q]])
            p.add("pool", lambda e, q=q: e.tensor_tensor(out=ss[q][:, 1:2], in0=ss[q][:, 1:2], in1=neghalf[:, 0:1],
                                                         op=ALU.pow), r=[t_ss[q], t_c], w=[t_ss[q]])
            p.add("dve", lambda e, q=q: e.scalar_tensor_tensor(
                out=yno[q][:, :], in0=yc[q][:, :], scalar=ss[q][:, 1:2], in1=nw_sb[:, :], op0=ALU.mult, op1=ALU.mult),
                r=[t_yc[q], t_ss[q], t_c], w=[t_yno[q]])
            cx.dma(yn[row0:row0 + 128, :], yno[q][:, :], r=[t_yno[q]])
            p.add("pool", lambda e, q=q: e.tensor_tensor(
                out=v464(S[:, :]), in0=v464(S[:, :]), in1=bc4(dec[q][:, :]), op=ALU.mult),
                r=[t_S, t_dec[q]], w=[t_S])
            p.add("dve", lambda e, q=q: e.tensor_tensor(out=S[:, :], in0=S[:, :], in1=psYS[:, q, 256:512], op=ALU.add),
                  r=[t_S, t_YS[q]], w=[t_S])
            p.add("act", lambda e: e.activation(out=S_bf[:, :], in_=S[:, :], func=AF.Copy), r=[t_S], w=[t_Sbf])
    ntiles = nblocks * 4
    blockpro(0)
    front(0)
    chunks = []
    for n in range(ntiles):
        b, t = divmod(n, 4)
        lists = []
        if t == 0 and b + 1 < nblocks:
            bp = p.record(blockpro, b + 1)
            c = (len(bp) + 2) // 3
            chunks = [bp[0:c], bp[c:2 * c], bp[2 * c:]]
        if t < 3 and chunks:
            lists.append(chunks[t])
            if t == 2:
                chunks = []
        if n + 1 < ntiles:
            lists.append(p.record(front, n + 1))
        lists.append(p.record(back, n))
        p.merge(*lists)
    p.emit()
    return nc


def ssd_inputs(hT, w_in, conv_w, conv_b, dt_bias, a_log, d_skip, norm_w, g):
    xo, bo, co, do = 2048, 4096, 5120, 6144
    wfm = np.concatenate([w_in[:, xo + g * 256: xo + (g + 1) * 256], w_in[:, bo + g * 128: bo + (g + 1) * 128],
                          w_in[:, co + g * 128: co + (g + 1) * 128]], axis=1)
    wtm = np.concatenate([w_in[:, g * 256:(g + 1) * 256], w_in[:, do + 4 * g: do + 4 * g + 4]], axis=1)
    cidx = np.concatenate([np.arange(g * 256, (g + 1) * 256), 2048 + np.arange(g * 128, (g + 1) * 128),
                           3072 + np.arange(g * 128, (g + 1) * 128)])
    cw = np.ascontiguousarray(conv_w[:, cidx].reshape(4, 4, 128).transpose(2, 1, 0))
    cb = np.ascontiguousarray(conv_b[cidx].reshape(4, 128).T)
    hp = np.stack([dt_bias[4 * g:4 * g + 4], a_log[4 * g:4 * g + 4], d_skip[4 * g:4 * g + 4]])
    hp = np.ascontiguousarray(np.broadcast_to(hp[None], (128, 3, 4)))
    nw = np.ascontiguousarray(np.broadcast_to(norm_w[None, g * 256:(g + 1) * 256], (128, 256)))
    return {"hT": hT, "wfm": np.ascontiguousarray(wfm), "wtm": np.ascontiguousarray(wtm), "cw": cw, "cb": cb,
            "hp": hp.astype(np.float32), "nw": nw.astype(np.float32), "cst_f": SSD_CST_F, "cst_b": SSD_CST_B}


def _ssd_consts():
    k = np.arange(128)
    triu = (k[:, None] <= k[None, :]).astype(np.float32)
    cst_f = np.ascontiguousarray(np.stack([triu, np.ones((128, 128), np.float32)], axis=1))
    neg = np.where(k[:, None] > k[None, :], -30000.0, 0.0).astype(np.float32)
    cst_b = np.concatenate([np.eye(128, dtype=np.float32), np.tile(neg, (1, 4))], axis=1).astype(ml_dtypes.bfloat16)
    return cst_f, np.ascontiguousarray(cst_b)


SSD_CST_F, SSD_CST_B = _ssd_consts()


def build_attn(seq=SEQ, same_kv=True, nd1=0, nd2=0):
    cx = Ctx(nbanks=8)
    nc, p = cx.nc, cx.p
    BLK = 512
    nblocks = seq // BLK
    nkb = seq // 128
    QG = 8
    nqg = nkb // QG

    hTq = cx.dram_in("hTq", [D, seq], F32)
    hTkv = hTq if same_kv else cx.dram_in("hTkv", [D, seq], F32)
    wq = cx.dram_in("wq", [D, 128], F32)
    wk = cx.dram_in("wk", [D, 128], F32)
    wv = cx.dram_in("wv", [D, 128], F32)
    cst = cx.dram_in("acst", [128, 1024], BF16)
    negm_d = cx.dram_in("negm", [64, 1], F32)
    oT = cx.dram_out("oT", [128, seq], BF16)

    wq_sb = cx.sb("wq_sb", [128, 8, 128], BF16)
    wk_sb = cx.sb("wk_sb", [128, 8, 128], BF16)
    wv_sb = cx.sb("wv_sb", [128, 8, 128], BF16)
    cst_sb = cx.sb("cst_sb", [128, 1024], BF16)
    negm = cx.sb("negm_sb", [64, 1], F32)
    qT = cx.sb("qT", [128, seq], BF16)
    kT = cx.sb("kT", [128, seq], BF16)
    V = cx.sb("V", [128, nkb, 128], BF16)
    t_w, t_c, t_q, t_k, t_v = Tok("w"), Tok("c"), Tok("q"), Tok("k"), Tok("v")
    negMinc, diagmask = cst_sb[:, 0:128], cst_sb[:, 128:256]
    ones_b, ident_b, zero_b = cst_sb[:, 256:384], cst_sb[:, 384:512], cst_sb[:, 512:1024]
    ones_col = cx.sb("ones_col", [128, 1], F32)
    negones = cx.sb("negones", [128, 2], BF16)
    p.add("pool", lambda e: e.memset(ones_col[:, :], 1.0), w=[t_c])
    p.add("pool", lambda e: e.memset(negones[:, :], -1.0), w=[t_c])

    cx.dma(cst_sb[:, :], cst[:, :], w=[t_c])
    cx.dma(negm[:, :], negm_d[:, :], w=[t_c])
    load_cast_rows(cx, wq_sb, wq, 8, 128, t_w)
    load_cast_rows(cx, wk_sb, wk, 8, 128, t_w)
    load_cast_rows(cx, wv_sb, wv, 8, 128, t_w)

    hTf, t_hTf = cx.dbuf("hTf", [128, 8, BLK], F32)
    hTb, t_hTb = cx.dbuf("hTb", [128, 8, BLK], BF16)

    passes = [("qkv", hTq)] if same_kv else [("kv", hTkv), ("q", hTq)]
    cnt = 0
    for what, src in passes:
        src_v = src.rearrange("(k p) t -> p k t", p=128)

        def load_block(b, cnt, src_v=src_v):
            pb = cnt % 2
            for half in range(2):
                cx.dma(hTf[pb][:, half * 4:(half + 1) * 4, :], src_v[:, half * 4:(half + 1) * 4, b * BLK:(b + 1) * BLK],
                       w=[t_hTf[pb]])

        load_block(0, cnt)
        for b in range(nblocks):
            pb = cnt % 2
            if b + 1 < nblocks:
                load_block(b + 1, cnt + 1)
            cnt += 1
            p.add("act", lambda e, pb=pb: e.activation(out=hTb[pb][:, 0:3, :], in_=hTf[pb][:, 0:3, :], func=AF.Copy),
                  r=[t_hTf[pb]], w=[t_hTb[pb]])
            p.add("dve", lambda e, pb=pb: e.tensor_copy(out=hTb[pb][:, 3:6, :], in_=hTf[pb][:, 3:6, :]),
                  r=[t_hTf[pb]], w=[t_hTb[pb]])
            p.add("pool", lambda e, pb=pb: e.tensor_copy(out=hTb[pb][:, 6:8, :], in_=hTf[pb][:, 6:8, :]),
                  r=[t_hTf[pb]], w=[t_hTb[pb]])
            cols = slice(b * BLK, (b + 1) * BLK)
            if "q" in what:
                bank, bt = cx.bank()
                for k in range(8):
                    p.add("pe", lambda e, bank=bank, k=k, pb=pb: e.matmul(
                        bank, lhsT=wq_sb[:, k, :], rhs=hTb[pb][:, k, :], start=(k == 0), stop=(k == 7)),
                        r=[t_w, t_hTb[pb]], w=[bt])
                p.add("act", lambda e, bank=bank, cols=cols: e.activation(out=qT[:, cols], in_=bank, func=AF.Copy, scale=0.125),
                      r=[bt], w=[t_q])
            if "k" in what:
                bank, bt = cx.bank()
                for k in range(8):
                    p.add("pe", lambda e, bank=bank, k=k, pb=pb: e.matmul(
                        bank, lhsT=wk_sb[:, k, :], rhs=hTb[pb][:, k, :], start=(k == 0), stop=(k == 7)),
                        r=[t_w, t_hTb[pb]], w=[bt])
                p.add("dve", lambda e, bank=bank, cols=cols: e.tensor_copy(out=kT[:, cols], in_=bank), r=[bt], w=[t_k])
                bank, bt = cx.bank()
                for t in range(4):
                    for k in range(8):
                        p.add("pe", lambda e, bank=bank, k=k, t=t, pb=pb: e.matmul(
                            bank[:, t * 128:(t + 1) * 128], lhsT=hTb[pb][:, k, t * 128:(t + 1) * 128], rhs=wv_sb[:, k, :],
                            start=(k == 0), stop=(k == 7)), r=[t_w, t_hTb[pb]], w=[bt])
                p.add("act", lambda e, bank=bank, b=b: e.activation(
                    out=V[:, 4 * b:4 * b + 4, :], in_=bank.rearrange("p (t c) -> p t c", t=4), func=AF.Copy),
                    r=[bt], w=[t_v])

    NQ = QG * 128
    zb = [cx.ps[:, 0:2, :].rearrange("p b c -> p (b c)"), cx.ps[:, 2:4, :].rearrange("p b c -> p (b c)")]
    ob = cx.ps[:, 4:6, :].rearrange("p b c -> p (b c)")
    accb = cx.ps[:, 6:8, :].rearrange("p b c -> p (b c)")
    t_z = [Tok("zA", excl=True), Tok("zB", excl=True)]
    t_o, t_acc = Tok("o", excl=True), [Tok("accA", excl=True), Tok("accB", excl=True)]
    for hd in range(2):
        for bnk in (2 * hd, 2 * hd + 1):
            t_z[hd].w = cx.ps_tok[bnk].w if t_z[hd].w is None else t_z[hd].w
            t_z[hd].rs.update(cx.ps_tok[bnk].rs)
    for bnk in (4, 5):
        t_o.rs.update(cx.ps_tok[bnk].rs)
    for hd in range(2):
        for bnk in (6, 7):
            t_acc[hd].rs.update(cx.ps_tok[bnk].rs)
    E_sb, t_E = cx.dbuf("E_sb", [128, NQ], F32)
    L_sb, t_L = cx.dbuf("L_sb", [128, NQ], BF16)
    W_sb, t_W = cx.dbuf("W_sb", [128, NQ], BF16)
    hi_t = cx.sb("hi_t", [64, NQ], BF16)
    acc_hl = cx.sb("acc_hl", [64, NQ], BF16)
    t_hi, t_hl = [Tok("hiA"), Tok("hiB")], [Tok("hlA"), Tok("hlB")]
    obuf, t_ob = cx.dbuf("obuf", [128, NQ], BF16)
    rows = [slice(0, 2), slice(32, 34)]

    def dummies(n):
        for i in range(n):
            c0 = 512 * (i % 2)
            p.add("pe", lambda e, c0=c0: e.matmul(ob[:, c0:c0 + 512], lhsT=zero_b[:, 0:128], rhs=zero_b[:, 0:512],
                                                  start=False, stop=False, skip_group_check=True), r=[t_c], w=[t_o])

    def pieces(lo):
        out = []
        for c0 in (0, 512):
            a, bnd = max(lo, c0), c0 + 512
            if a < bnd:
                out.append((a, bnd))
        return out

    for qg in range(nqg):
        i0 = qg * QG
        qcol0 = i0 * 128
        for c0 in (0, 512):
            p.add("pe", lambda e, c0=c0: e.matmul(ob[:, c0:c0 + 512], lhsT=zero_b[:, 0:128], rhs=zero_b[:, 0:512],
                                                  start=True, stop=False, skip_group_check=True), r=[t_c], w=[t_o])
            for hd in range(2):
                p.add("pe", lambda e, c0=c0, hd=hd: e.matmul(
                    accb[rows[hd], c0:c0 + 512], lhsT=zero_b[:, 0:2], rhs=zero_b[:, 0:512],
                    start=True, stop=False, skip_group_check=True), r=[t_c], w=[t_acc[hd]])
        for hd in range(2):
            p.add("pool", lambda e, hd=hd: e.memset(acc_hl[rows[hd], :], 0.0), w=[t_hl[hd]])
        def emit_z(j, hd):
            lo = max(j - i0, 0) * 128
            ks = slice(j * 128, (j + 1) * 128)
            hp_ = slice(hd * 64, (hd + 1) * 64)
            for (a, bnd) in pieces(lo):
                p.add("pe", lambda e, hd=hd, hp_=hp_, a=a, bnd=bnd, ks=ks, qcol0=qcol0: e.matmul(
                    zb[hd][:, a:bnd], lhsT=kT[hp_, ks], rhs=qT[hp_, qcol0 + a:qcol0 + bnd],
                    start=True, stop=False, skip_group_check=True), r=[t_k, t_q], w=[t_z[hd]])
            if j >= i0:
                p.add("pe", lambda e, hd=hd, lo=lo: e.matmul(
                    zb[hd][:, lo:lo + 128], lhsT=ident_b, rhs=diagmask, start=False, stop=False,
                    skip_group_check=True), r=[t_c], w=[t_z[hd]])

        jtop = i0 + QG - 1
        for hd in range(2):
            emit_z(jtop, hd)
        for j in range(jtop, -1, -1):
            lo = max(j - i0, 0) * 128
            pcs = pieces(lo)
            for hd in range(2):
                p.add("act", lambda e, hd=hd, lo=lo: e.activation(out=E_sb[hd][:, lo:NQ], in_=zb[hd][:, lo:NQ], func=AF.Exp),
                      r=[t_z[hd]], w=[t_E[hd]])
                p.add("act", lambda e, hd=hd, lo=lo: e.activation(out=L_sb[hd][:, lo:NQ], in_=E_sb[hd][:, lo:NQ], func=AF.Ln,
                                                                  bias=ones_col[:, 0:1], scale=1.0),
                      r=[t_E[hd], t_c], w=[t_L[hd]])
            for hd in range(2):
                for (a, bnd) in pcs:
                    p.add("pe", lambda e, hd=hd, a=a, bnd=bnd: e.matmul(
                        zb[hd][:, a:bnd], lhsT=negMinc, rhs=L_sb[hd][:, a:bnd], start=False, stop=False,
                        skip_group_check=True), r=[t_L[hd], t_c], w=[t_z[hd]])
                    p.add("pe", lambda e, hd=hd, a=a, bnd=bnd: e.matmul(
                        zb[hd][:, a:bnd], lhsT=ones_b[rows[hd], :], rhs=acc_hl[rows[hd], a:bnd], start=False, stop=True,
                        skip_group_check=True), r=[t_hl[hd], t_c], w=[t_z[hd]])
            for hd in range(2):
                p.add("act", lambda e, hd=hd, lo=lo: e.activation(out=W_sb[hd][:, lo:NQ], in_=zb[hd][:, lo:NQ], func=AF.Exp),
                      r=[t_z[hd]], w=[t_W[hd]])
            for hd in range(2):
                for (a, bnd) in pcs:
                    p.add("pe", lambda e, hd=hd, a=a, bnd=bnd: e.matmul(
                        accb[rows[hd], a:bnd], lhsT=negones[:, 0:2], rhs=L_sb[hd][:, a:bnd], start=False, stop=False,
                        skip_group_check=True), r=[t_L[hd], t_c], w=[t_acc[hd]])
                if j > 0:
                    p.add("dve", lambda e, hd=hd, lo=lo: e.tensor_copy(out=hi_t[rows[hd], lo:NQ], in_=accb[rows[hd], lo:NQ]),
                          r=[t_acc[hd]], w=[t_hi[hd]])
                    p.add("dve", lambda e, hd=hd, lo=lo: e.scalar_tensor_tensor(
                        out=acc_hl[rows[hd], lo:NQ], in0=hi_t[rows[hd], lo:NQ], scalar=negm[rows[hd], 0:1],
                        in1=accb[rows[hd], lo:NQ], op0=ALU.mult, op1=ALU.add), r=[t_hi[hd], t_acc[hd], t_c], w=[t_hl[hd]])
            for hd in range(2):
                if j > 0:
                    emit_z(j - 1, hd)
                for (a, bnd) in pcs:
                    p.add("pe", lambda e, hd=hd, a=a, bnd=bnd, j=j: e.matmul(
                        ob[hd * 64:(hd + 1) * 64, a:bnd], lhsT=V[:, j, hd * 64:(hd + 1) * 64], rhs=W_sb[hd][:, a:bnd],
                        start=False, stop=False, skip_group_check=True), r=[t_W[hd], t_v], w=[t_o])
        ob_i = qg % 2
        p.add("dve", lambda e, ob_i=ob_i: e.tensor_copy(out=obuf[ob_i][:, :], in_=ob[:, :]), r=[t_o], w=[t_ob[ob_i]])
        cx.dma(oT[:, qcol0:qcol0 + NQ], obuf[ob_i][:, :], r=[t_ob[ob_i]])
    p.emit()
    return nc


def _attn_consts():
    k = np.arange(128)
    neg_minc = np.where(k[:, None] >= k[None, :], -1.0, 0.0)
    diag = np.where(k[:, None] >= k[None, :], -30000.0, 0.0)
    cst = np.concatenate([neg_minc, diag, np.ones((128, 128)), np.eye(128), np.zeros((128, 512))], axis=1)
    negm = np.zeros((64, 1), np.float32)
    negm[1, 0] = -1.0
    negm[33, 0] = -1.0
    return np.ascontiguousarray(cst.astype(ml_dtypes.bfloat16)), negm


ATT_CST, ATT_NEGM = _attn_consts()


_PROGS = {}


def _prog(key, builder):
    if key not in _PROGS:
        _PROGS[key] = builder()
    return _PROGS[key]


def _run(nc, in_maps):
    return run_bass_kernel_spmd(nc, in_maps, core_ids=list(range(NCORES))).results


def kernel(x, ssm_w_in, ssm_conv_w, ssm_conv_b, ssm_dt_bias, ssm_a_log, ssm_d, ssm_norm_w, ssm_w_out,
           sb_w_k, sb_w_v, sb_w_q, sb_w_o, mlp_w1, mlp_w2, ln_mix_g, ln_mix_b, ln_mlp_g, ln_mlp_b):
    f32 = lambda a: np.ascontiguousarray(np.asarray(a, dtype=np.float32))
    h = f32(x)[0]
    ident = np.eye(128, dtype=np.float32)
    hT_kv = None
    for layer in range(DEPTH):
        hT = np.ascontiguousarray(h.T)
        if layer < 2:
            nc = _prog("ssd", build_ssd)
            ins = [ssd_inputs(hT, f32(ssm_w_in[layer]), f32(ssm_conv_w[layer]), f32(ssm_conv_b[layer]),
                              f32(ssm_dt_bias[layer]), f32(ssm_a_log[layer]), f32(ssm_d[layer]),
                              f32(ssm_norm_w[layer]), g) for g in range(NCORES)]
            res = _run(nc, ins)
            Y = np.concatenate([np.asarray(res[g]["yn"]) for g in range(NCORES)], axis=1)
            yTs = [np.ascontiguousarray(Y[c * TOK:(c + 1) * TOK].T) for c in range(NCORES)]
            w_o, kin = f32(ssm_w_out[layer]), D_INNER
        else:
            j = layer - 2
            same = layer == 2
            if same:
                hT_kv = hT
            nc = _prog(("attn", same), lambda: build_attn(SEQ, same_kv=same))
            ins = []
            for c in range(NCORES):
                sl = slice(c * 128, (c + 1) * 128)
                d = {"hTq": hT, "wq": f32(sb_w_q[j][:, sl]), "wk": f32(sb_w_k[:, sl]), "wv": f32(sb_w_v[:, sl]),
                     "acst": ATT_CST, "negm": ATT_NEGM}
                if not same:
                    d["hTkv"] = hT_kv
                ins.append(d)
            res = _run(nc, ins)
            OT = np.concatenate([np.asarray(res[c]["oT"]) for c in range(NCORES)], axis=0)
            yTs = [np.ascontiguousarray(OT[:, c * TOK:(c + 1) * TOK]) for c in range(NCORES)]
            w_o, kin = f32(sb_w_o[j]), D
        nc = _prog(("t", kin), lambda: build_tphase(kin))
        lnp = np.stack([f32(ln_mix_g[layer]), f32(ln_mix_b[layer]), f32(ln_mlp_g[layer]), f32(ln_mlp_b[layer])])
        lnp = np.ascontiguousarray(np.broadcast_to(lnp[None], (128, 4, D)))
        w1, w2 = f32(mlp_w1[layer]), f32(mlp_w2[layer])
        ins = [{"yT": yTs[c], "h": np.ascontiguousarray(h[c * TOK:(c + 1) * TOK]), "w_o": w_o, "w1": w1, "w2": w2,
                "lnp": lnp, "ident": ident} for c in range(NCORES)]
        res = _run(nc, ins)
        h = np.concatenate([np.asarray(res[c]["h_out"]) for c in range(NCORES)], axis=0)
    return h[None].astype(np.float32)
```

```python
import math
import numpy as np
import ml_dtypes
import concourse.bass as bass
import concourse.mybir as mybir
from concourse.bass_utils import run_bass_kernel_spmd

F32 = mybir.dt.float32
BF16 = mybir.dt.bfloat16
AF = mybir.ActivationFunctionType
ALU = mybir.AluOpType
AX = mybir.AxisListType

NCORES = 8
SEQ = 16384
D = 1024
DEPTH = 4
TOK = SEQ // NCORES
ALPHA = (2 * DEPTH) ** 0.25
LN_EPS = 1e-5
RMS_EPS = 1e-5
D_INNER = 2048
D_FF = 4096
NBLK = SEQ // 128

ENGS = ("pe", "act", "dve", "pool", "sp")
NRING = 6


class Tok:
    __slots__ = ("name", "w", "rs", "rd", "excl")

    def __init__(self, name="", excl=False):
        self.name = name
        self.w = None
        self.rs = {}
        self.rd = []
        self.excl = excl


class Op:
    __slots__ = ("eng", "fn", "deps", "dma", "sig", "sem", "val", "prev")


class Prog:
    def __init__(self, nc):
        self.nc = nc
        self.ops = {e: [] for e in ENGS}

    def add(self, eng, fn, r=(), w=(), dma=False):
        op = Op()
        op.eng, op.fn, op.dma, op.sig, op.sem, op.val, op.prev = eng, fn, dma, False, None, 0, 0
        deps = set()
        for t in r:
            if t.w is not None:
                deps.add(t.w)
            if t.excl:
                deps.update(o for en, o in t.rs.items() if en != eng)
        for t in w:
            if t.w is not None:
                deps.add(t.w)
            deps.update(t.rs.values())
            deps.update(t.rd)
        if eng == "pe" and not dma:
            deps = {d for d in deps if d.dma or d.eng != "pe"}
        deps.discard(op)
        op.deps = deps
        for t in r:
            if dma:
                t.rd.append(op)
            else:
                t.rs[eng] = op
        for t in w:
            t.w = op
            t.rs = {}
            t.rd = []
        self.ops[eng].append(op)
        return op

    def emit(self):
        nc = self.nc
        for e in ENGS:
            for op in self.ops[e]:
                for d in op.deps:
                    d.sig = True
        csem = {e: nc.alloc_semaphore(name=f"c_{e}") for e in ENGS if e != "sp"}
        ring = {e: [nc.alloc_semaphore(name=f"r_{e}{i}") for i in range(NRING)] for e in ENGS}
        final = {}
        for e in ENGS:
            cnt = 0
            nd = 0
            for op in self.ops[e]:
                if op.dma:
                    op.sem = ring[e][nd % NRING]
                    op.prev = 16 * (nd // NRING)
                    op.val = op.prev + 16
                    final[op.sem] = op.val
                    nd += 1
                elif op.sig:
                    cnt += 1
                    op.sem = csem[e]
                    op.val = cnt
        engobj = {"pe": "tensor", "act": "scalar", "dve": "vector", "pool": "gpsimd", "sp": "sync"}
        ops = self.ops

        def run(e, eng):
            waited = {}
            for op in ops[e]:
                needs = {}
                for d in op.deps:
                    if needs.get(d.sem, 0) < d.val:
                        needs[d.sem] = d.val
                if op.dma and op.prev > 0 and needs.get(op.sem, 0) < op.prev:
                    needs[op.sem] = op.prev
                for sem, val in needs.items():
                    if waited.get(sem, 0) < val:
                        eng.wait_ge(sem, val)
                        waited[sem] = val
                ins = op.fn(eng)
                if op.dma:
                    ins.then_inc(op.sem, 16)
                elif op.sig:
                    ins.then_inc(op.sem, 1)
            if e == "sp":
                for sem, val in final.items():
                    if waited.get(sem, 0) < val:
                        eng.wait_ge(sem, val)

        with nc.Block() as block:
            @block.tensor
            def _(eng):
                run("pe", eng)

            @block.scalar
            def _(eng):
                run("act", eng)

            @block.vector
            def _(eng):
                run("dve", eng)

            @block.gpsimd
            def _(eng):
                run("pool", eng)

            @block.sync
            def _(eng):
                run("sp", eng)


class PProxy:
    def __init__(self, real):
        self.real = real
        self.rec = None

    def add(self, *a, **k):
        if self.rec is not None:
            self.rec.append((a, k))
            return None
        return self.real.add(*a, **k)

    def record(self, fn, *args):
        self.rec = []
        fn(*args)
        items, self.rec = self.rec, None
        return items

    def merge(self, *lists):
        total = max(len(l) for l in lists)
        pos = [0] * len(lists)
        for step in range(1, total + 1):
            for i, l in enumerate(lists):
                upto = (len(l) * step) // total
                while pos[i] < upto:
                    a, k = l[pos[i]]
                    self.real.add(*a, **k)
                    pos[i] += 1

    def emit(self):
        self.real.emit()


class Ctx:
    def __init__(self, nbanks=8):
        self.nc = bass.Bass("TRN2", target_bir_lowering=False)
        self.p = Prog(self.nc)
        self.nb = nbanks
        self.ps = self.nc.alloc_psum_tensor("psum_all", [128, nbanks, 512], F32)
        self.ps_tok = [Tok(f"ps{i}", excl=True) for i in range(nbanks)]
        self.ps_next = 0
        self.nbuf = 0

    def dbuf(self, name, shape, dt, n=2):
        bufs = [self.nc.alloc_sbuf_tensor(f"{name}_{i}", shape, dt) for i in range(n)]
        toks = [Tok(f"{name}_{i}") for i in range(n)]
        return bufs, toks

    def sb(self, name, shape, dt):
        return self.nc.alloc_sbuf_tensor(name, shape, dt)

    def bank(self):
        b = self.ps_next
        self.ps_next = (b + 1) % self.nb
        return self.ps[:, b, :], self.ps_tok[b]

    def dram_in(self, name, shape, dt):
        return self.nc.dram_tensor(name, list(shape), dt, kind="ExternalInput").ap()

    def dram_out(self, name, shape, dt):
        return self.nc.dram_tensor(name, list(shape), dt, kind="ExternalOutput").ap()

    def dma(self, out, in_, r=(), w=(), eng=None):
        if eng is None:
            eng = "sp"
        return self.p.add(eng, lambda e, o=out, i=in_: e.dma_start(out=o, in_=i), r=r, w=w, dma=True)

    def dma_cast(self, out, in_, r=(), w=()):
        return self.p.add("pool", lambda e, o=out, i=in_: e.dma_start(out=o, in_=i), r=r, w=w, dma=True)


def load_cast_rows(cx, dst, src, nk, ncols, wtok, chunk=2048):
    v = src.rearrange("(k p) n -> p k n", p=128)
    for k in range(nk):
        for c0 in range(0, ncols, chunk):
            c1 = min(ncols, c0 + chunk)
            cx.dma_cast(dst[:, k, c0:c1], v[:, k, c0:c1], w=[wtok])


def layer_norm_tile(cx, r_sb, r_tok, out_sb, out_tok, g_sb, b_sb, eps_sb, scr, gtok):
    p = cx.p
    st, mv, rstd, st_tok = scr
    for c in range(2):
        p.add("dve", lambda e, c=c: e.bn_stats(out=st[:, c, :], in_=r_sb[:, c * 512:(c + 1) * 512]),
              r=[r_tok], w=[st_tok])
    p.add("dve", lambda e: e.bn_aggr(out=mv[:, :], in_=st[:, :, :].rearrange("p a b -> p (a b)")), r=[st_tok], w=[st_tok])
    p.add("act", lambda e: e.activation(out=rstd[:, :], in_=mv[:, 1:2], func=AF.Sqrt, bias=eps_sb[:, 0:1], scale=1.0),
          r=[st_tok], w=[st_tok])
    p.add("dve", lambda e: e.reciprocal(out=rstd[:, :], in_=rstd[:, :]), r=[st_tok], w=[st_tok])
    p.add("dve", lambda e: e.tensor_scalar(out=out_sb, in0=r_sb, scalar1=mv[:, 0:1], scalar2=rstd[:, 0:1],
                                           op0=ALU.subtract, op1=ALU.mult), r=[r_tok, st_tok], w=[out_tok])
    p.add("pool", lambda e: e.tensor_tensor(out=out_sb, in0=out_sb, in1=g_sb, op=ALU.mult), r=[out_tok, gtok], w=[out_tok])
    p.add("pool", lambda e: e.tensor_tensor(out=out_sb, in0=out_sb, in1=b_sb, op=ALU.add), r=[out_tok, gtok], w=[out_tok])


def build_tphase(kin):
    cx = Ctx()
    nc, p = cx.nc, cx.p
    KC = kin // 128
    G = 128
    CPB = 512 // G
    NG = TOK // G
    yT = cx.dram_in("yT", [kin, TOK], BF16)
    h_in = cx.dram_in("h", [TOK, D], F32)
    w_o = cx.dram_in("w_o", [kin, D], F32)
    w1 = cx.dram_in("w1", [D, D_FF], F32)
    w2 = cx.dram_in("w2", [D_FF, D], F32)
    lnp = cx.dram_in("lnp", [128, 4, D], F32)
    h_out = cx.dram_out("h_out", [TOK, D], F32)

    wo_sb = cx.sb("wo_sb", [128, KC, D], BF16)
    w1_sb = cx.sb("w1_sb", [128, 8, D_FF], BF16)
    w2_sb = cx.sb("w2_sb", [128, 32, D], BF16)
    lnp_sb = cx.sb("lnp_sb", [128, 4, D], F32)
    ident = cx.sb("ident_sb", [128, 128], F32)
    eps_sb = cx.sb("eps_sb", [128, 1], F32)
    t_wo, t_w1, t_w2, t_lnp, t_const = Tok("wo"), Tok("w1"), Tok("w2"), Tok("lnp"), Tok("const")

    ident_d = cx.dram_in("ident", [128, 128], F32)
    cx.dma(ident[:, :], ident_d[:, :], w=[t_const])
    p.add("pool", lambda e: e.memset(eps_sb[:, :], LN_EPS), w=[t_const])
    cx.dma(lnp_sb[:, :, :], lnp[:, :, :], w=[t_lnp])

    yT_sb = cx.sb("yT_sb", [128, KC, G], BF16)
    h_sb = cx.sb("h_sb", [128, D], F32)
    t_yT, t_h = Tok("yT"), Tok("h")
    rbuf = [cx.sb(f"rbuf{i}", [128, D], F32) for i in range(2)]
    t_rb = [Tok("rb0"), Tok("rb1")]
    h1T = cx.sb("h1T", [128, 8, G], BF16)
    t_h1T = Tok("h1T")
    uT = cx.sb("uT", [128, 32, G], BF16)
    t_uT = [Tok(f"uT{i}") for i in range(32 // CPB)]
    utmp = [cx.sb(f"utmp{i}", [128, CPB * G], F32) for i in range(2)]
    t_utmp = [Tok("utmp0"), Tok("utmp1")]
    st = cx.sb("ln_st", [128, 2, 6], F32)
    mv = cx.sb("ln_mv", [128, 2], F32)
    rstd = cx.sb("ln_rstd", [128, 1], F32)
    scr = (st, mv, rstd, Tok("lnscr"))

    yT_v = yT.rearrange("(k p) t -> p k t", p=128)
    h_v = h_in.rearrange("(n p) d -> p n d", p=128)
    ho_v = h_out.rearrange("(n p) d -> p n d", p=128)

    def load_group(g):
        cx.dma(yT_sb[:, :, :], yT_v[:, :, g * G:(g + 1) * G], w=[t_yT])
        cx.dma(h_sb[:, :], h_v[:, g, :], w=[t_h])

    load_group(0)
    load_cast_rows(cx, wo_sb, w_o, KC, D, t_wo)
    load_cast_rows(cx, w1_sb, w1, 8, D_FF, t_w1)
    load_cast_rows(cx, w2_sb, w2, 32, D, t_w2)

    for g in range(NG):
        rb, trb = rbuf[g % 2], t_rb[g % 2]
        for n in range(2):
            bank, bt = cx.bank()
            for k in range(KC):
                p.add("pe", lambda e, bank=bank, k=k, n=n: e.matmul(
                    bank, lhsT=yT_sb[:, k, :], rhs=wo_sb[:, k, n * 512:(n + 1) * 512],
                    start=(k == 0), stop=(k == KC - 1)), r=[t_yT, t_wo], w=[bt])
            p.add("dve", lambda e, bank=bank, n=n, rb=rb: e.scalar_tensor_tensor(
                out=rb[:, n * 512:(n + 1) * 512], in0=h_sb[:, n * 512:(n + 1) * 512], scalar=ALPHA,
                in1=bank, op0=ALU.mult, op1=ALU.add), r=[bt, t_h], w=[trb])
        if g + 1 < NG:
            load_group(g + 1)
        layer_norm_tile(cx, rb[:, :], trb, rb[:, :], trb, lnp_sb[:, 0, :], lnp_sb[:, 1, :], eps_sb, scr, t_lnp)
        for q in range(2):
            bank, bt = cx.bank()
            for j in range(4):
                f = q * 4 + j
                p.add("pe", lambda e, bank=bank, j=j, f=f, rb=rb: e.transpose(
                    bank[:, j * 128:(j + 1) * 128], rb[:, f * 128:(f + 1) * 128], ident[:, :]),
                    r=[trb, t_const], w=[bt])
            p.add("act", lambda e, bank=bank, q=q: e.activation(
                out=h1T[:, q * 4:(q + 1) * 4, :], in_=bank.rearrange("p (j c) -> p j c", j=4), func=AF.Copy),
                r=[bt], w=[t_h1T])
        for cp in range(32 // CPB):
            bank, bt = cx.bank()
            for j in range(CPB):
                ch = cp * CPB + j
                for k in range(8):
                    p.add("pe", lambda e, bank=bank, j=j, ch=ch, k=k: e.matmul(
                        bank[:, j * G:(j + 1) * G], lhsT=w1_sb[:, k, ch * 128:(ch + 1) * 128], rhs=h1T[:, k, :],
                        start=(k == 0), stop=(k == 7)), r=[t_w1, t_h1T], w=[bt])
            ub = cp % 2
            p.add("act", lambda e, bank=bank, ub=ub: e.activation(out=utmp[ub][:, :], in_=bank[:, 0:CPB * G], func=AF.Relu),
                  r=[bt], w=[t_utmp[ub]])
            p.add("pool", lambda e, cp=cp, ub=ub: e.tensor_tensor(
                out=uT[:, CPB * cp:CPB * cp + CPB, :], in0=utmp[ub][:, :].rearrange("p (j c) -> p j c", j=CPB),
                in1=utmp[ub][:, :].rearrange("p (j c) -> p j c", j=CPB), op=ALU.mult), r=[t_utmp[ub]], w=[t_uT[cp]])
        for n in range(2):
            bank, bt = cx.bank()
            for k in range(32):
                p.add("pe", lambda e, bank=bank, k=k, n=n: e.matmul(
                    bank, lhsT=uT[:, k, :], rhs=w2_sb[:, k, n * 512:(n + 1) * 512],
                    start=(k == 0), stop=(k == 31)), r=[t_uT[k // CPB], t_w2], w=[bt])
            p.add("dve", lambda e, bank=bank, n=n, rb=rb: e.scalar_tensor_tensor(
                out=rb[:, n * 512:(n + 1) * 512], in0=rb[:, n * 512:(n + 1) * 512], scalar=ALPHA,
                in1=bank, op0=ALU.mult, op1=ALU.add), r=[bt, trb], w=[trb])
        layer_norm_tile(cx, rb[:, :], trb, rb[:, :], trb, lnp_sb[:, 2, :], lnp_sb[:, 3, :], eps_sb, scr, t_lnp)
        cx.dma(ho_v[:, g, :], rb[:, :], r=[trb])
    p.emit()
    return nc


def build_ssd(nblocks=SEQ // 512):
    cx = Ctx(nbanks=5)
    cx.p = PProxy(cx.p)
    pool_ctr = {"t": 0, "b": 0}

    def tbank():
        i = pool_ctr["t"] % 3
        pool_ctr["t"] += 1
        return cx.ps[:, i, :], cx.ps_tok[i]

    def bbank():
        i = 3 + pool_ctr["b"] % 2
        pool_ctr["b"] += 1
        return cx.ps[:, i, :], cx.ps_tok[i]

    nc, p = cx.nc, cx.p
    BLK = 512
    seq = nblocks * BLK
    psB = nc.alloc_psum_tensor("psum_bf", [128, 1024], BF16)
    t_psB = Tok("psB", excl=True)
    psYS = nc.alloc_psum_tensor("psum_ys", [128, 2, 512], F32)
    t_YS = [Tok("YS0", excl=True), Tok("YS1", excl=True)]

    hT = cx.dram_in("hT", [D, seq], F32)
    wfm = cx.dram_in("wfm", [D, 512], F32)
    wtm = cx.dram_in("wtm", [D, 260], F32)
    cw = cx.dram_in("cw", [128, 4, 4], F32)
    cb = cx.dram_in("cb", [128, 4], F32)
    hp = cx.dram_in("hp", [128, 3, 4], F32)
    nw = cx.dram_in("nw", [128, 256], F32)
    cst_f = cx.dram_in("cst_f", [128, 2, 128], F32)
    cst_b = cx.dram_in("cst_b", [128, 128 + 512], BF16)
    yn = cx.dram_out("yn", [seq, 256], BF16)

    wfm_sb = cx.sb("wfm_sb", [128, 8, 512], BF16)
    wtm_sb = cx.sb("wtm_sb", [128, 8, 260], BF16)
    cw_sb = cx.sb("cw_sb", [128, 4, 4], F32)
    cb_sb = cx.sb("cb_sb", [128, 4], F32)
    hp_sb = cx.sb("hp_sb", [128, 3, 4], F32)
    nw_sb = cx.sb("nw_sb", [128, 256], F32)
    cf_sb = cx.sb("cf_sb", [128, 2, 128], F32)
    cbf_sb = cx.sb("cbf_sb", [128, 640], BF16)
    aneg = cx.sb("aneg", [128, 4], F32)
    onecol = cx.sb("onecol", [128, 1], F32)
    neghalf = cx.sb("neghalf", [128, 1], F32)
    S = cx.sb("S_state", [128, 256], F32)
    S_bf = cx.sb("S_bf", [128, 256], BF16)
    t_w, t_c, t_S, t_Sbf = Tok("w"), Tok("c"), Tok("S"), Tok("Sbf")
    triu, ones_f = cf_sb[:, 0, :], cf_sb[:, 1, :]
    ident_b, negmask = cbf_sb[:, 0:128], cbf_sb[:, 128:640]

    for dst, src in ((cw_sb, cw), (cb_sb, cb), (hp_sb, hp), (nw_sb, nw), (cf_sb, cst_f), (cbf_sb, cst_b)):
        nd = len(dst.shape)
        idx = tuple([slice(None)] * nd)
        cx.dma(dst[idx], src[idx], w=[t_c])
    load_cast_rows(cx, wfm_sb, wfm, 8, 512, t_w)
    load_cast_rows(cx, wtm_sb, wtm, 8, 260, t_w)
    p.add("pool", lambda e: e.memset(onecol[:, :], 1.0), w=[t_c])
    p.add("pool", lambda e: e.memset(neghalf[:, :], -0.5), w=[t_c])
    p.add("pool", lambda e: e.memset(S[:, :], 0.0), w=[t_S])
    p.add("pool", lambda e: e.memset(S_bf[:, :], 0.0), w=[t_Sbf])
    p.add("act", lambda e: e.activation(out=aneg[:, :], in_=hp_sb[:, 1, :], func=AF.Exp), r=[t_c], w=[t_c])
    p.add("dve", lambda e: e.tensor_scalar(out=aneg[:, :], in0=aneg[:, :], scalar1=-1.0, scalar2=None, op0=ALU.mult),
          r=[t_c], w=[t_c])

    hTf, t_hTf = cx.dbuf("hTf", [128, 8, BLK], F32)
    hTb, t_hTb = cx.dbuf("hTb", [128, 8, BLK], BF16, n=3)
    xpre, t_xpre = cx.dbuf("xpre", [128, 4, 3 + BLK], F32)
    acc, t_acc = cx.dbuf("cacc", [128, BLK], F32)
    xbcT, t_xbcT = cx.dbuf("xbcT", [128, 4, BLK], BF16)
    zs, t_zs = cx.dbuf("zs", [128, 4, 256], F32)
    dtr, t_dtr = cx.dbuf("dtr", [128, 4, 4], F32)
    dte, t_dte = cx.dbuf("dte", [128, 4, 4], F32)
    dts, t_dts = cx.dbuf("dts", [128, 4, 4], F32)
    das, t_das = cx.dbuf("das", [128, 4, 4], F32)
    xdt, t_xdt = cx.dbuf("xdt", [128, 256], BF16)
    ysk, t_ysk = cx.dbuf("ysk", [128, 256], F32)
    Btm, t_Btm = cx.dbuf("Btm", [128, 128], BF16)
    daT, t_daT = cx.dbuf("daT", [128, 4, 128], F32)
    nacs, t_nacs = cx.dbuf("nacs", [128, 4], F32)
    e_sb, t_e = cx.dbuf("e_sb", [128, 4], F32)
    w_sb, t_wd = cx.dbuf("w_sb", [128, 4], F32)
    dec, t_dec = cx.dbuf("dec", [128, 4], F32)
    Dm, t_Dm = cx.dbuf("Dm", [128, 4, 128], F32)
    Gm, t_G = cx.dbuf("Gm", [128, 4, 128], BF16)
    yc, t_yc = cx.dbuf("yc", [128, 256], F32)
    sq, t_sq = cx.dbuf("sq", [128, 256], F32)
    ss, t_ss = cx.dbuf("ss", [128, 2], F32)
    yno, t_yno = cx.dbuf("yno", [128, 256], BF16)
    xdw, t_xdw = cx.dbuf("xdw", [128, 256], BF16)

    hT_v = hT.rearrange("(k p) t -> p k t", p=128)
    p.add("pool", lambda e: e.memset(xpre[1][:, :, :], 0.0), w=[t_xpre[1]])

    def load_block(b):
        fb, hb = b % 2, b % 3
        for half in range(2):
            cx.dma(hTf[fb][:, half * 4:(half + 1) * 4, :], hT_v[:, half * 4:(half + 1) * 4, b * BLK:(b + 1) * BLK],
                   w=[t_hTf[fb]])
        p.add("act", lambda e, fb=fb, hb=hb: e.activation(out=hTb[hb][:, 0:4, :], in_=hTf[fb][:, 0:4, :], func=AF.Copy),
              r=[t_hTf[fb]], w=[t_hTb[hb]])
        p.add("dve", lambda e, fb=fb, hb=hb: e.tensor_copy(out=hTb[hb][:, 4:8, :], in_=hTf[fb][:, 4:8, :]),
              r=[t_hTf[fb]], w=[t_hTb[hb]])

    load_block(0)
    if nblocks > 1:
        load_block(1)
    bc4 = lambda ap: ap.unsqueeze(2).broadcast_to([128, 4, 64])
    v464 = lambda ap: ap.rearrange("p (h c) -> p h c", h=4)
    def blockpro(b):
        pb = b % 2
        hb = b % 3
        if b + 2 < nblocks:
            load_block(b + 2)
        p.add("pool", lambda e, pb=pb: e.tensor_copy(out=xpre[pb][:, :, 0:3], in_=xpre[1 - pb][:, :, BLK:BLK + 3]),
              r=[t_xpre[1 - pb]], w=[t_xpre[pb]])
        for ct in range(4):
            bank, bt = bbank()
            for k in range(8):
                p.add("pe", lambda e, bank=bank, k=k, ct=ct, hb=hb: e.matmul(
                    bank, lhsT=wfm_sb[:, k, ct * 128:(ct + 1) * 128], rhs=hTb[hb][:, k, :],
                    start=(k == 0), stop=(k == 7)), r=[t_w, t_hTb[hb]], w=[bt])
            p.add("act", lambda e, bank=bank, ct=ct, pb=pb: e.activation(
                out=xpre[pb][:, ct, 3:3 + BLK], in_=bank, func=AF.Copy), r=[bt], w=[t_xpre[pb]])
        for ct in range(4):
            a = ct % 2
            p.add("act", lambda e, ct=ct, pb=pb, a=a: e.activation(
                out=acc[a][:, :], in_=xpre[pb][:, ct, 0:BLK], func=AF.Identity,
                scale=cw_sb[:, ct, 0:1], bias=cb_sb[:, ct:ct + 1]), r=[t_xpre[pb], t_c], w=[t_acc[a]])
            for k in range(1, 4):
                p.add("dve", lambda e, ct=ct, pb=pb, a=a, k=k: e.scalar_tensor_tensor(
                    out=acc[a][:, :], in0=xpre[pb][:, ct, k:k + BLK], scalar=cw_sb[:, ct, k:k + 1], in1=acc[a][:, :],
                    op0=ALU.mult, op1=ALU.add), r=[t_xpre[pb], t_acc[a], t_c], w=[t_acc[a]])
            p.add("act", lambda e, ct=ct, pb=pb, a=a: e.activation(
                out=xbcT[pb][:, ct, :], in_=acc[a][:, :], func=AF.Silu), r=[t_acc[a]], w=[t_xbcT[pb]])
        for t in range(4):
            bank, bt = bbank()
            for k in range(8):
                p.add("pe", lambda e, bank=bank, k=k, t=t, hb=hb: e.matmul(
                    bank[:, 0:260], lhsT=hTb[hb][:, k, t * 128:(t + 1) * 128], rhs=wtm_sb[:, k, :],
                    start=(k == 0), stop=(k == 7)), r=[t_w, t_hTb[hb]], w=[bt])
            p.add("act", lambda e, bank=bank, t=t, pb=pb: e.activation(
                out=zs[pb][:, t, :], in_=bank[:, 0:256], func=AF.Silu), r=[bt], w=[t_zs[pb]])
            p.add("dve", lambda e, bank=bank, t=t, pb=pb: e.tensor_tensor(
                out=dtr[pb][:, t, :], in0=bank[:, 256:260], in1=hp_sb[:, 0, :], op=ALU.add),
                r=[bt, t_c], w=[t_dtr[pb]])
        p.add("act", lambda e, pb=pb: e.activation(out=dte[pb][:, :, :], in_=dtr[pb][:, :, :], func=AF.Exp),
              r=[t_dtr[pb]], w=[t_dte[pb]])
        p.add("act", lambda e, pb=pb: e.activation(out=dts[pb][:, :, :], in_=dte[pb][:, :, :], func=AF.Ln,
                                                   bias=onecol[:, 0:1], scale=1.0), r=[t_dte[pb], t_c], w=[t_dts[pb]])
        p.add("dve", lambda e, pb=pb: e.tensor_tensor(
            out=das[pb][:, :, :], in0=dts[pb][:, :, :], in1=aneg[:, :].unsqueeze(1).broadcast_to([128, 4, 4]),
            op=ALU.mult), r=[t_dts[pb], t_c], w=[t_das[pb]])
    def front(n):
        if True:
            b, t = divmod(n, 4)
            pb, q, c0 = b % 2, n % 2, t * 128
            for j in range(3):
                p.add("pe", lambda e, j=j, pb=pb, c0=c0: e.transpose(
                    psB[:, j * 128:(j + 1) * 128], xbcT[pb][:, j, c0:c0 + 128], ident_b),
                    r=[t_xbcT[pb], t_c], w=[t_psB])
            p.add("dve", lambda e, q=q, pb=pb, t=t: e.tensor_tensor(
                out=v464(xdt[q][:, :]), in0=v464(psB[:, 0:256]), in1=bc4(dts[pb][:, t, :]), op=ALU.mult),
                r=[t_psB, t_dts[pb]], w=[t_xdt[q]])
            p.add("dve", lambda e, q=q: e.tensor_tensor(
                out=v464(ysk[q][:, :]), in0=v464(psB[:, 0:256]), in1=bc4(hp_sb[:, 2, :]), op=ALU.mult),
                r=[t_psB, t_c], w=[t_ysk[q]])
            p.add("act", lambda e, q=q: e.activation(out=Btm[q][:, :], in_=psB[:, 256:384], func=AF.Copy),
                  r=[t_psB], w=[t_Btm[q]])
            bankA, btA = tbank()
            p.add("pe", lambda e, bankA=bankA, pb=pb, t=t: e.matmul(
                bankA[:, 0:4], lhsT=triu, rhs=das[pb][:, t, :], start=True, stop=True), r=[t_c, t_das[pb]], w=[btA])
            p.add("dve", lambda e, q=q, pb=pb, t=t: e.tensor_tensor(
                out=daT[q][:, :, :], in0=triu.unsqueeze(1).broadcast_to([128, 4, 128]),
                in1=das[pb][:, t, :].unsqueeze(2).broadcast_to([128, 4, 128]), op=ALU.mult),
                r=[t_c, t_das[pb]], w=[t_daT[q]])
            bankR, btR = tbank()
            p.add("pe", lambda e, bankR=bankR, q=q: e.matmul(
                bankR, lhsT=ones_f, rhs=daT[q][:, :, :].rearrange("p h l -> p (h l)"), start=True, stop=False),
                r=[t_c, t_daT[q]], w=[btR])
            p.add("pe", lambda e, bankR=bankR: e.matmul(bankR, lhsT=ident_b, rhs=negmask, start=False, stop=True),
                  r=[t_c], w=[btR])
            p.add("act", lambda e, bankA=bankA, q=q: e.activation(
                out=nacs[q][:, :], in_=bankA[:, 0:4], func=AF.Identity, scale=-1.0), r=[btA], w=[t_nacs[q]])
            p.add("act", lambda e, bankA=bankA, q=q: e.activation(out=e_sb[q][:, :], in_=bankA[:, 0:4], func=AF.Exp),
                  r=[btA], w=[t_e[q]])
            Rv = bankR.rearrange("p (h l) -> p h l", h=4)
            for h in range(4):
                p.add("act", lambda e, bankR=bankR, q=q, h=h: e.activation(
                    out=Dm[q][:, h, :], in_=bankR[:, h * 128:(h + 1) * 128], func=AF.Exp,
                    bias=nacs[q][:, h:h + 1], scale=1.0), r=[btR, t_nacs[q]], w=[t_Dm[q]])
            p.add("dve", lambda e, Rv=Rv, q=q: e.tensor_tensor(
                out=w_sb[q][:, :], in0=Rv[:, :, 127], in1=nacs[q][:, :], op=ALU.add),
                r=[btR, t_nacs[q]], w=[t_wd[q]])
            p.add("act", lambda e, q=q: e.activation(out=w_sb[q][:, :], in_=w_sb[q][:, :], func=AF.Exp),
                  r=[t_wd[q]], w=[t_wd[q]])
            p.add("act", lambda e, Rv=Rv, q=q: e.activation(out=dec[q][:, :], in_=Rv[:, :, 127], func=AF.Exp),
                  r=[btR], w=[t_dec[q]])
            bankC, btC = bankA[:, 128:256], btA
            p.add("pe", lambda e, bankC=bankC, pb=pb, c0=c0: e.matmul(
                bankC, lhsT=xbcT[pb][:, 2, c0:c0 + 128], rhs=xbcT[pb][:, 3, c0:c0 + 128],
                start=True, stop=True), r=[t_xbcT[pb]], w=[btC])
            p.add("dve", lambda e, bankC=bankC, q=q: e.tensor_tensor(
                out=Gm[q][:, :, :], in0=Dm[q][:, :, :], in1=bankC.unsqueeze(1).broadcast_to([128, 4, 128]),
                op=ALU.mult), r=[btC, t_Dm[q]], w=[t_G[q]])
            bankY, btY = psYS[:, q, :], t_YS[q]
            for h in range(4):
                p.add("pe", lambda e, bankY=bankY, q=q, h=h: e.matmul(
                    bankY[:, h * 64:(h + 1) * 64], lhsT=Gm[q][:, h, :], rhs=xdt[q][:, h * 64:(h + 1) * 64],
                    start=True, stop=True), r=[t_G[q], t_xdt[q]], w=[btY])
            p.add("pool", lambda e, q=q: e.tensor_tensor(
                out=v464(xdw[q][:, :]), in0=v464(xdt[q][:, :]), in1=bc4(w_sb[q][:, :]), op=ALU.mult),
                r=[t_xdt[q], t_wd[q]], w=[t_xdw[q]])
            p.add("pe", lambda e, q=q: e.matmul(
                psYS[:, q, 256:512], lhsT=Btm[q][:, :], rhs=xdw[q][:, :], start=True, stop=True),
                r=[t_Btm[q], t_xdw[q]], w=[t_YS[q]])
    def back(n):
        if True:
            b, t = divmod(n, 4)
            pb, q, c0 = b % 2, n % 2, t * 128
            row0 = b * BLK + c0
            bankY, btY = psYS[:, q, :], t_YS[q]
            bankO, btO = tbank()
            p.add("pe", lambda e, bankO=bankO, pb=pb, c0=c0: e.matmul(
                bankO[:, 0:256], lhsT=xbcT[pb][:, 3, c0:c0 + 128], rhs=S_bf[:, :], start=True, stop=True),
                r=[t_xbcT[pb], t_Sbf], w=[btO])
            p.add("dve", lambda e, bankO=bankO, q=q: e.tensor_tensor(
                out=v464(yc[q][:, :]), in0=v464(bankO[:, 0:256]), in1=bc4(e_sb[q][:, :]), op=ALU.mult),
                r=[btO, t_e[q]], w=[t_yc[q]])
            p.add("dve", lambda e, bankY=bankY, q=q: e.tensor_tensor(
                out=yc[q][:, :], in0=yc[q][:, :], in1=bankY[:, 0:256], op=ALU.add), r=[btY, t_yc[q]], w=[t_yc[q]])
            p.add("pool", lambda e, q=q: e.tensor_tensor(out=yc[q][:, :], in0=yc[q][:, :], in1=ysk[q][:, :], op=ALU.add),
                  r=[t_yc[q], t_ysk[q]], w=[t_yc[q]])
            p.add("pool", lambda e, q=q, pb=pb, t=t: e.tensor_tensor(
                out=yc[q][:, :], in0=yc[q][:, :], in1=zs[pb][:, t, :], op=ALU.mult),
                r=[t_yc[q], t_zs[pb]], w=[t_yc[q]])
            p.add("act", lambda e, q=q: e.activation(out=sq[q][:, :], in_=yc[q][:, :], func=AF.Square,
                                                     accum_out=ss[q][:, 0:1]), r=[t_yc[q]], w=[t_sq[q], t_ss[q]])
            p.add("dve", lambda e, q=q: e.tensor_scalar(out=ss[q][:, 1:2], in0=ss[q][:, 0:1], scalar1=1.0 / 256.0,
                                                        scalar2=RMS_EPS, op0=ALU.mult, op1=ALU.add),
                  r=[t_ss[q]], w=[t_ss[q]])
            p.add("pool", lambda e, q=q: e.tensor_tensor(out=ss[q][:, 1:2], in0=ss[q][:, 1:2], in1=neghalf[:, 0:1],
                                                         op=ALU.pow), r=[t_ss[q], t_c], w=[t_ss[q]])
            p.add("dve", lambda e, q=q: e.scalar_tensor_tensor(
                out=yno[q][:, :], in0=yc[q][:, :], scalar=ss[q][:, 1:2], in1=nw_sb[:, :], op0=ALU.mult, op1=ALU.mult),
                r=[t_yc[q], t_ss[q], t_c], w=[t_yno[q]])
            cx.dma(yn[row0:row0 + 128, :], yno[q][:, :], r=[t_yno[q]])
            p.add("pool", lambda e, q=q: e.tensor_tensor(
                out=v464(S[:, :]), in0=v464(S[:, :]), in1=bc4(dec[q][:, :]), op=ALU.mult),
                r=[t_S, t_dec[q]], w=[t_S])
            p.add("dve", lambda e, q=q: e.tensor_tensor(out=S[:, :], in0=S[:, :], in1=psYS[:, q, 256:512], op=ALU.add),
                  r=[t_S, t_YS[q]], w=[t_S])
            p.add("act", lambda e: e.activation(out=S_bf[:, :], in_=S[:, :], func=AF.Copy), r=[t_S], w=[t_Sbf])
    ntiles = nblocks * 4
    blockpro(0)
    front(0)
    chunks = []
    for n in range(ntiles):
        b, t = divmod(n, 4)
        lists = []
        if t == 0 and b + 1 < nblocks:
            bp = p.record(blockpro, b + 1)
            c = (len(bp) + 2) // 3
            chunks = [bp[0:c], bp[c:2 * c], bp[2 * c:]]
        if t < 3 and chunks:
            lists.append(chunks[t])
            if t == 2:
                chunks = []
        if n + 1 < ntiles:
            lists.append(p.record(front, n + 1))
        lists.append(p.record(back, n))
        p.merge(*lists)
    p.emit()
    return nc


def ssd_inputs(hT, w_in, conv_w, conv_b, dt_bias, a_log, d_skip, norm_w, g):
    xo, bo, co, do = 2048, 4096, 5120, 6144
    wfm = np.concatenate([w_in[:, xo + g * 256: xo + (g + 1) * 256], w_in[:, bo + g * 128: bo + (g + 1) * 128],
                          w_in[:, co + g * 128: co + (g + 1) * 128]], axis=1)
    wtm = np.concatenate([w_in[:, g * 256:(g + 1) * 256], w_in[:, do + 4 * g: do + 4 * g + 4]], axis=1)
    cidx = np.concatenate([np.arange(g * 256, (g + 1) * 256), 2048 + np.arange(g * 128, (g + 1) * 128),
                           3072 + np.arange(g * 128, (g + 1) * 128)])
    cw = np.ascontiguousarray(conv_w[:, cidx].reshape(4, 4, 128).transpose(2, 1, 0))
    cb = np.ascontiguousarray(conv_b[cidx].reshape(4, 128).T)
    hp = np.stack([dt_bias[4 * g:4 * g + 4], a_log[4 * g:4 * g + 4], d_skip[4 * g:4 * g + 4]])
    hp = np.ascontiguousarray(np.broadcast_to(hp[None], (128, 3, 4)))
    nw = np.ascontiguousarray(np.broadcast_to(norm_w[None, g * 256:(g + 1) * 256], (128, 256)))
    return {"hT": hT, "wfm": np.ascontiguousarray(wfm), "wtm": np.ascontiguousarray(wtm), "cw": cw, "cb": cb,
            "hp": hp.astype(np.float32), "nw": nw.astype(np.float32), "cst_f": SSD_CST_F, "cst_b": SSD_CST_B}


def _ssd_consts():
    k = np.arange(128)
    triu = (k[:, None] <= k[None, :]).astype(np.float32)
    cst_f = np.ascontiguousarray(np.stack([triu, np.ones((128, 128), np.float32)], axis=1))
    neg = np.where(k[:, None] > k[None, :], -30000.0, 0.0).astype(np.float32)
    cst_b = np.concatenate([np.eye(128, dtype=np.float32), np.tile(neg, (1, 4))], axis=1).astype(ml_dtypes.bfloat16)
    return cst_f, np.ascontiguousarray(cst_b)


SSD_CST_F, SSD_CST_B = _ssd_consts()


def build_attn(seq=SEQ, same_kv=True, nd1=0, nd2=0):
    cx = Ctx(nbanks=8)
    nc, p = cx.nc, cx.p
    BLK = 512
    nblocks = seq // BLK
    nkb = seq // 128
    QG = 8
    nqg = nkb // QG

    hTq = cx.dram_in("hTq", [D, seq], F32)
    hTkv = hTq if same_kv else cx.dram_in("hTkv", [D, seq], F32)
    wq = cx.dram_in("wq", [D, 128], F32)
    wk = cx.dram_in("wk", [D, 128], F32)
    wv = cx.dram_in("wv", [D, 128], F32)
    cst = cx.dram_in("acst", [128, 1024], BF16)
    negm_d = cx.dram_in("negm", [64, 1], F32)
    oT = cx.dram_out("oT", [128, seq], BF16)

    wq_sb = cx.sb("wq_sb", [128, 8, 128], BF16)
    wk_sb = cx.sb("wk_sb", [128, 8, 128], BF16)
    wv_sb = cx.sb("wv_sb", [128, 8, 128], BF16)
    cst_sb = cx.sb("cst_sb", [128, 1024], BF16)
    negm = cx.sb("negm_sb", [64, 1], F32)
    qT = cx.sb("qT", [128, seq], BF16)
    kT = cx.sb("kT", [128, seq], BF16)
    V = cx.sb("V", [128, nkb, 128], BF16)
    t_w, t_c, t_q, t_k, t_v = Tok("w"), Tok("c"), Tok("q"), Tok("k"), Tok("v")
    negMinc, diagmask = cst_sb[:, 0:128], cst_sb[:, 128:256]
    ones_b, ident_b, zero_b = cst_sb[:, 256:384], cst_sb[:, 384:512], cst_sb[:, 512:1024]
    ones_col = cx.sb("ones_col", [128, 1], F32)
    negones = cx.sb("negones", [128, 2], BF16)
    p.add("pool", lambda e: e.memset(ones_col[:, :], 1.0), w=[t_c])
    p.add("pool", lambda e: e.memset(negones[:, :], -1.0), w=[t_c])

    cx.dma(cst_sb[:, :], cst[:, :], w=[t_c])
    cx.dma(negm[:, :], negm_d[:, :], w=[t_c])
    load_cast_rows(cx, wq_sb, wq, 8, 128, t_w)
    load_cast_rows(cx, wk_sb, wk, 8, 128, t_w)
    load_cast_rows(cx, wv_sb, wv, 8, 128, t_w)

    hTf, t_hTf = cx.dbuf("hTf", [128, 8, BLK], F32)
    hTb, t_hTb = cx.dbuf("hTb", [128, 8, BLK], BF16)

    passes = [("qkv", hTq)] if same_kv else [("kv", hTkv), ("q", hTq)]
    cnt = 0
    for what, src in passes:
        src_v = src.rearrange("(k p) t -> p k t", p=128)

        def load_block(b, cnt, src_v=src_v):
            pb = cnt % 2
            for half in range(2):
                cx.dma(hTf[pb][:, half * 4:(half + 1) * 4, :], src_v[:, half * 4:(half + 1) * 4, b * BLK:(b + 1) * BLK],
                       w=[t_hTf[pb]])

        load_block(0, cnt)
        for b in range(nblocks):
            pb = cnt % 2
            if b + 1 < nblocks:
                load_block(b + 1, cnt + 1)
            cnt += 1
            p.add("act", lambda e, pb=pb: e.activation(out=hTb[pb][:, 0:3, :], in_=hTf[pb][:, 0:3, :], func=AF.Copy),
                  r=[t_hTf[pb]], w=[t_hTb[pb]])
            p.add("dve", lambda e, pb=pb: e.tensor_copy(out=hTb[pb][:, 3:6, :], in_=hTf[pb][:, 3:6, :]),
                  r=[t_hTf[pb]], w=[t_hTb[pb]])
            p.add("pool", lambda e, pb=pb: e.tensor_copy(out=hTb[pb][:, 6:8, :], in_=hTf[pb][:, 6:8, :]),
                  r=[t_hTf[pb]], w=[t_hTb[pb]])
            cols = slice(b * BLK, (b + 1) * BLK)
            if "q" in what:
                bank, bt = cx.bank()
                for k in range(8):
                    p.add("pe", lambda e, bank=bank, k=k, pb=pb: e.matmul(
                        bank, lhsT=wq_sb[:, k, :], rhs=hTb[pb][:, k, :], start=(k == 0), stop=(k == 7)),
                        r=[t_w, t_hTb[pb]], w=[bt])
                p.add("act", lambda e, bank=bank, cols=cols: e.activation(out=qT[:, cols], in_=bank, func=AF.Copy, scale=0.125),
                      r=[bt], w=[t_q])
            if "k" in what:
                bank, bt = cx.bank()
                for k in range(8):
                    p.add("pe", lambda e, bank=bank, k=k, pb=pb: e.matmul(
                        bank, lhsT=wk_sb[:, k, :], rhs=hTb[pb][:, k, :], start=(k == 0), stop=(k == 7)),
                        r=[t_w, t_hTb[pb]], w=[bt])
                p.add("dve", lambda e, bank=bank, cols=cols: e.tensor_copy(out=kT[:, cols], in_=bank), r=[bt], w=[t_k])
                bank, bt = cx.bank()
                for t in range(4):
                    for k in range(8):
                        p.add("pe", lambda e, bank=bank, k=k, t=t, pb=pb: e.matmul(
                            bank[:, t * 128:(t + 1) * 128], lhsT=hTb[pb][:, k, t * 128:(t + 1) * 128], rhs=wv_sb[:, k, :],
                            start=(k == 0), stop=(k == 7)), r=[t_w, t_hTb[pb]], w=[bt])
                p.add("act", lambda e, bank=bank, b=b: e.activation(
                    out=V[:, 4 * b:4 * b + 4, :], in_=bank.rearrange("p (t c) -> p t c", t=4), func=AF.Copy),
                    r=[bt], w=[t_v])

    NQ = QG * 128
    zb = [cx.ps[:, 0:2, :].rearrange("p b c -> p (b c)"), cx.ps[:, 2:4, :].rearrange("p b c -> p (b c)")]
    ob = cx.ps[:, 4:6, :].rearrange("p b c -> p (b c)")
    accb = cx.ps[:, 6:8, :].rearrange("p b c -> p (b c)")
    t_z = [Tok("zA", excl=True), Tok("zB", excl=True)]
    t_zp = [[Tok(f"z{h}{i}", excl=True) for i in range(2)] for h in range(2)]
    t_Wp = [[Tok(f"W{h}{i}") for i in range(2)] for h in range(2)]
    t_o, t_acc = Tok("o", excl=True), [Tok("accA", excl=True), Tok("accB", excl=True)]
    for hd in range(2):
        for bnk in (2 * hd, 2 * hd + 1):
            t_zp[hd][bnk - 2 * hd].rs.update(cx.ps_tok[bnk].rs)
    for bnk in (4, 5):
        t_o.rs.update(cx.ps_tok[bnk].rs)
    for hd in range(2):
        for bnk in (6, 7):
            t_acc[hd].rs.update(cx.ps_tok[bnk].rs)
    E_sb, t_E = cx.dbuf("E_sb", [128, NQ], F32)
    L_sb, t_L = cx.dbuf("L_sb", [128, NQ], BF16)
    W_sb, t_W = cx.dbuf("W_sb", [128, NQ], BF16)
    hi_t = cx.sb("hi_t", [64, NQ], BF16)
    acc_hl = cx.sb("acc_hl", [64, NQ], BF16)
    t_hi, t_hl = [Tok("hiA"), Tok("hiB")], [Tok("hlA"), Tok("hlB")]
    obuf, t_ob = cx.dbuf("obuf", [128, NQ], BF16)
    rows = [slice(0, 2), slice(32, 34)]

    def dummies(n):
        for i in range(n):
            c0 = 512 * (i % 2)
            p.add("pe", lambda e, c0=c0: e.matmul(ob[:, c0:c0 + 512], lhsT=zero_b[:, 0:128], rhs=zero_b[:, 0:512],
                                                  start=False, stop=False, skip_group_check=True), r=[t_c], w=[t_o])

    def pieces(lo):
        out = []
        for c0 in (0, 512):
            a, bnd = max(lo, c0), c0 + 512
            if a < bnd:
                out.append((a, bnd))
        return out

    for qg in range(nqg):
        i0 = qg * QG
        qcol0 = i0 * 128
        for c0 in (0, 512):
            p.add("pe", lambda e, c0=c0: e.matmul(ob[:, c0:c0 + 512], lhsT=zero_b[:, 0:128], rhs=zero_b[:, 0:512],
                                                  start=True, stop=False, skip_group_check=True), r=[t_c], w=[t_o])
            for hd in range(2):
                p.add("pe", lambda e, c0=c0, hd=hd: e.matmul(
                    accb[rows[hd], c0:c0 + 512], lhsT=zero_b[:, 0:2], rhs=zero_b[:, 0:512],
                    start=True, stop=False, skip_group_check=True), r=[t_c], w=[t_acc[hd]])
        for hd in range(2):
            p.add("pool", lambda e, hd=hd: e.memset(acc_hl[rows[hd], :], 0.0), w=[t_hl[hd]])
        def emit_z(j, hd):
            lo = max(j - i0, 0) * 128
            ks = slice(j * 128, (j + 1) * 128)
            hp_ = slice(hd * 64, (hd + 1) * 64)
            for (a, bnd) in pieces(lo):
                p.add("pe", lambda e, hd=hd, hp_=hp_, a=a, bnd=bnd, ks=ks, qcol0=qcol0: e.matmul(
                    zb[hd][:, a:bnd], lhsT=kT[hp_, ks], rhs=qT[hp_, qcol0 + a:qcol0 + bnd],
                    start=True, stop=False, skip_group_check=True), r=[t_k, t_q], w=[t_zp[hd][a // 512]])
            if j >= i0:
                p.add("pe", lambda e, hd=hd, lo=lo: e.matmul(
                    zb[hd][:, lo:lo + 128], lhsT=ident_b, rhs=diagmask, start=False, stop=False,
                    skip_group_check=True), r=[t_c], w=[t_zp[hd][lo // 512]])

        jtop = i0 + QG - 1
        for hd in range(2):
            emit_z(jtop, hd)
        for j in range(jtop, -1, -1):
            lo = max(j - i0, 0) * 128
            pcs = pieces(lo)
            for hd in range(2):
                p.add("act", lambda e, hd=hd, lo=lo: e.activation(out=E_sb[hd][:, lo:NQ], in_=zb[hd][:, lo:NQ], func=AF.Exp),
                      r=[t_z[hd]] + t_zp[hd], w=[t_E[hd]])
                p.add("act", lambda e, hd=hd, lo=lo: e.activation(out=L_sb[hd][:, lo:NQ], in_=E_sb[hd][:, lo:NQ], func=AF.Ln,
                                                                  bias=ones_col[:, 0:1], scale=1.0),
                      r=[t_E[hd], t_c], w=[t_L[hd]])
            for hd in range(2):
                for (a, bnd) in pcs:
                    p.add("pe", lambda e, hd=hd, a=a, bnd=bnd: e.matmul(
                        zb[hd][:, a:bnd], lhsT=negMinc, rhs=L_sb[hd][:, a:bnd], start=False, stop=False,
                        skip_group_check=True), r=[t_L[hd], t_c], w=[t_zp[hd][a // 512]])
                for (a, bnd) in pcs:
                    p.add("pe", lambda e, hd=hd, a=a, bnd=bnd: e.matmul(
                        zb[hd][:, a:bnd], lhsT=ones_b[rows[hd], :], rhs=acc_hl[rows[hd], a:bnd], start=False, stop=True,
                        skip_group_check=True), r=[t_hl[hd], t_c], w=[t_zp[hd][a // 512]])
            for hd in range(2):
                p.add("act", lambda e, hd=hd, lo=lo: e.activation(out=W_sb[hd][:, lo:NQ], in_=zb[hd][:, lo:NQ], func=AF.Exp),
                      r=t_zp[hd], w=t_Wp[hd])
            for (a, bnd) in pcs:
                for hd in range(2):
                    p.add("pe", lambda e, hd=hd, a=a, bnd=bnd: e.matmul(
                        accb[rows[hd], a:bnd], lhsT=negones[:, 0:2], rhs=L_sb[hd][:, a:bnd], start=False, stop=False,
                        skip_group_check=True), r=[t_L[hd], t_c], w=[t_acc[hd]])
            if j > 0:
                for hd in range(2):
                    p.add("dve", lambda e, hd=hd, lo=lo: e.tensor_copy(out=hi_t[rows[hd], lo:NQ], in_=accb[rows[hd], lo:NQ]),
                          r=[t_acc[hd]], w=[t_hi[hd]])
                    p.add("dve", lambda e, hd=hd, lo=lo: e.scalar_tensor_tensor(
                        out=acc_hl[rows[hd], lo:NQ], in0=hi_t[rows[hd], lo:NQ], scalar=negm[rows[hd], 0:1],
                        in1=accb[rows[hd], lo:NQ], op0=ALU.mult, op1=ALU.add), r=[t_hi[hd], t_acc[hd], t_c], w=[t_hl[hd]])
                for hd in range(2):
                    emit_z(j - 1, hd)
            for (a, bnd) in pcs:
                for hd in range(2):
                    p.add("pe", lambda e, hd=hd, a=a, bnd=bnd, j=j: e.matmul(
                        ob[hd * 64:(hd + 1) * 64, a:bnd], lhsT=V[:, j, hd * 64:(hd + 1) * 64], rhs=W_sb[hd][:, a:bnd],
                        start=False, stop=False, skip_group_check=True), r=[t_Wp[hd][a // 512], t_v], w=[t_o])
        ob_i = qg % 2
        p.add("dve", lambda e, ob_i=ob_i: e.tensor_copy(out=obuf[ob_i][:, :], in_=ob[:, :]), r=[t_o], w=[t_ob[ob_i]])
        cx.dma(oT[:, qcol0:qcol0 + NQ], obuf[ob_i][:, :], r=[t_ob[ob_i]])
    p.emit()
    return nc


def _attn_consts():
    k = np.arange(128)
    neg_minc = np.where(k[:, None] >= k[None, :], -1.0, 0.0)
    diag = np.where(k[:, None] >= k[None, :], -30000.0, 0.0)
    cst = np.concatenate([neg_minc, diag, np.ones((128, 128)), np.eye(128), np.zeros((128, 512))], axis=1)
    negm = np.zeros((64, 1), np.float32)
    negm[1, 0] = -1.0
    negm[33, 0] = -1.0
    return np.ascontiguousarray(cst.astype(ml_dtypes.bfloat16)), negm


ATT_CST, ATT_NEGM = _attn_consts()


_PROGS = {}


def _prog(key, builder):
    if key not in _PROGS:
        _PROGS[key] = builder()
    return _PROGS[key]


def _run(nc, in_maps):
    return run_bass_kernel_spmd(nc, in_maps, core_ids=list(range(NCORES))).results


def kernel(x, ssm_w_in, ssm_conv_w, ssm_conv_b, ssm_dt_bias, ssm_a_log, ssm_d, ssm_norm_w, ssm_w_out,
           sb_w_k, sb_w_v, sb_w_q, sb_w_o, mlp_w1, mlp_w2, ln_mix_g, ln_mix_b, ln_mlp_g, ln_mlp_b):
    f32 = lambda a: np.ascontiguousarray(np.asarray(a, dtype=np.float32))
    h = f32(x)[0]
    ident = np.eye(128, dtype=np.float32)
    hT_kv = None
    for layer in range(DEPTH):
        hT = np.ascontiguousarray(h.T)
        if layer < 2:
            nc = _prog("ssd", build_ssd)
            ins = [ssd_inputs(hT, f32(ssm_w_in[layer]), f32(ssm_conv_w[layer]), f32(ssm_conv_b[layer]),
                              f32(ssm_dt_bias[layer]), f32(ssm_a_log[layer]), f32(ssm_d[layer]),
                              f32(ssm_norm_w[layer]), g) for g in range(NCORES)]
            res = _run(nc, ins)
            Y = np.concatenate([np.asarray(res[g]["yn"]) for g in range(NCORES)], axis=1)
            yTs = [np.ascontiguousarray(Y[c * TOK:(c + 1) * TOK].T) for c in range(NCORES)]
            w_o, kin = f32(ssm_w_out[layer]), D_INNER
        else:
            j = layer - 2
            same = layer == 2
            if same:
                hT_kv = hT
            nc = _prog(("attn", same), lambda: build_attn(SEQ, same_kv=same))
            ins = []
            for c in range(NCORES):
                sl = slice(c * 128, (c + 1) * 128)
                d = {"hTq": hT, "wq": f32(sb_w_q[j][:, sl]), "wk": f32(sb_w_k[:, sl]), "wv": f32(sb_w_v[:, sl]),
                     "acst": ATT_CST, "negm": ATT_NEGM}
                if not same:
                    d["hTkv"] = hT_kv
                ins.append(d)
            res = _run(nc, ins)
            OT = np.concatenate([np.asarray(res[c]["oT"]) for c in range(NCORES)], axis=0)
            yTs = [np.ascontiguousarray(OT[:, c * TOK:(c + 1) * TOK]) for c in range(NCORES)]
            w_o, kin = f32(sb_w_o[j]), D
        nc = _prog(("t", kin), lambda: build_tphase(kin))
        lnp = np.stack([f32(ln_mix_g[layer]), f32(ln_mix_b[layer]), f32(ln_mlp_g[layer]), f32(ln_mlp_b[layer])])
        lnp = np.ascontiguousarray(np.broadcast_to(lnp[None], (128, 4, D)))
        w1, w2 = f32(mlp_w1[layer]), f32(mlp_w2[layer])
        ins = [{"yT": yTs[c], "h": np.ascontiguousarray(h[c * TOK:(c + 1) * TOK]), "w_o": w_o, "w1": w1, "w2": w2,
                "lnp": lnp, "ident": ident} for c in range(NCORES)]
        res = _run(nc, ins)
        h = np.concatenate([np.asarray(res[c]["h_out"]) for c in range(NCORES)], axis=0)
    return h[None].astype(np.float32)
```

```python
import math
import numpy as np
import ml_dtypes
import concourse.bass as bass
import concourse.mybir as mybir
from concourse.bass_utils import run_bass_kernel_spmd

F32 = mybir.dt.float32
BF16 = mybir.dt.bfloat16
AF = mybir.ActivationFunctionType
ALU = mybir.AluOpType
AX = mybir.AxisListType

NCORES = 8
SEQ = 16384
D = 1024
DEPTH = 4
TOK = SEQ // NCORES
ALPHA = (2 * DEPTH) ** 0.25
LN_EPS = 1e-5
RMS_EPS = 1e-5
D_INNER = 2048
D_FF = 4096
NBLK = SEQ // 128

ENGS = ("pe", "act", "dve", "pool", "sp")
NRING = 6


class Tok:
    __slots__ = ("name", "w", "rs", "rd", "excl")

    def __init__(self, name="", excl=False):
        self.name = name
        self.w = None
        self.rs = {}
        self.rd = []
        self.excl = excl


class Op:
    __slots__ = ("eng", "fn", "deps", "dma", "sig", "sem", "val", "prev")


class Prog:
    def __init__(self, nc):
        self.nc = nc
        self.ops = {e: [] for e in ENGS}

    def add(self, eng, fn, r=(), w=(), dma=False):
        op = Op()
        op.eng, op.fn, op.dma, op.sig, op.sem, op.val, op.prev = eng, fn, dma, False, None, 0, 0
        deps = set()
        for t in r:
            if t.w is not None:
                deps.add(t.w)
            if t.excl:
                deps.update(o for en, o in t.rs.items() if en != eng)
        for t in w:
            if t.w is not None:
                deps.add(t.w)
            deps.update(t.rs.values())
            deps.update(t.rd)
        if eng == "pe" and not dma:
            deps = {d for d in deps if d.dma or d.eng != "pe"}
        deps.discard(op)
        op.deps = deps
        for t in r:
            if dma:
                t.rd.append(op)
            else:
                t.rs[eng] = op
        for t in w:
            t.w = op
            t.rs = {}
            t.rd = []
        self.ops[eng].append(op)
        return op

    def emit(self):
        nc = self.nc
        for e in ENGS:
            for op in self.ops[e]:
                for d in op.deps:
                    d.sig = True
        csem = {e: nc.alloc_semaphore(name=f"c_{e}") for e in ENGS if e != "sp"}
        ring = {e: [nc.alloc_semaphore(name=f"r_{e}{i}") for i in range(NRING)] for e in ENGS}
        final = {}
        for e in ENGS:
            cnt = 0
            nd = 0
            for op in self.ops[e]:
                if op.dma:
                    op.sem = ring[e][nd % NRING]
                    op.prev = 16 * (nd // NRING)
                    op.val = op.prev + 16
                    final[op.sem] = op.val
                    nd += 1
                elif op.sig:
                    cnt += 1
                    op.sem = csem[e]
                    op.val = cnt
        engobj = {"pe": "tensor", "act": "scalar", "dve": "vector", "pool": "gpsimd", "sp": "sync"}
        ops = self.ops

        def run(e, eng):
            waited = {}
            for op in ops[e]:
                needs = {}
                for d in op.deps:
                    if needs.get(d.sem, 0) < d.val:
                        needs[d.sem] = d.val
                if op.dma and op.prev > 0 and needs.get(op.sem, 0) < op.prev:
                    needs[op.sem] = op.prev
                for sem, val in needs.items():
                    if waited.get(sem, 0) < val:
                        eng.wait_ge(sem, val)
                        waited[sem] = val
                ins = op.fn(eng)
                if op.dma:
                    ins.then_inc(op.sem, 16)
                elif op.sig:
                    ins.then_inc(op.sem, 1)
            if e == "sp":
                for sem, val in final.items():
                    if waited.get(sem, 0) < val:
                        eng.wait_ge(sem, val)

        with nc.Block() as block:
            @block.tensor
            def _(eng):
                run("pe", eng)

            @block.scalar
            def _(eng):
                run("act", eng)

            @block.vector
            def _(eng):
                run("dve", eng)

            @block.gpsimd
            def _(eng):
                run("pool", eng)

            @block.sync
            def _(eng):
                run("sp", eng)


class PProxy:
    def __init__(self, real):
        self.real = real
        self.rec = None

    def add(self, *a, **k):
        if self.rec is not None:
            self.rec.append((a, k))
            return None
        return self.real.add(*a, **k)

    def record(self, fn, *args):
        self.rec = []
        fn(*args)
        items, self.rec = self.rec, None
        return items

    def merge(self, *lists):
        total = max(len(l) for l in lists)
        pos = [0] * len(lists)
        for step in range(1, total + 1):
            for i, l in enumerate(lists):
                upto = (len(l) * step) // total
                while pos[i] < upto:
                    a, k = l[pos[i]]
                    self.real.add(*a, **k)
                    pos[i] += 1

    def emit(self):
        self.real.emit()


class Ctx:
    def __init__(self, nbanks=8):
        self.nc = bass.Bass("TRN2", target_bir_lowering=False)
        self.p = Prog(self.nc)
        self.nb = nbanks
        self.ps = self.nc.alloc_psum_tensor("psum_all", [128, nbanks, 512], F32)
        self.ps_tok = [Tok(f"ps{i}", excl=True) for i in range(nbanks)]
        self.ps_next = 0
        self.nbuf = 0

    def dbuf(self, name, shape, dt, n=2):
        bufs = [self.nc.alloc_sbuf_tensor(f"{name}_{i}", shape, dt) for i in range(n)]
        toks = [Tok(f"{name}_{i}") for i in range(n)]
        return bufs, toks

    def sb(self, name, shape, dt):
        return self.nc.alloc_sbuf_tensor(name, shape, dt)

    def bank(self):
        b = self.ps_next
        self.ps_next = (b + 1) % self.nb
        return self.ps[:, b, :], self.ps_tok[b]

    def dram_in(self, name, shape, dt):
        return self.nc.dram_tensor(name, list(shape), dt, kind="ExternalInput").ap()

    def dram_out(self, name, shape, dt):
        return self.nc.dram_tensor(name, list(shape), dt, kind="ExternalOutput").ap()

    def dma(self, out, in_, r=(), w=(), eng=None):
        if eng is None:
            eng = "sp"
        return self.p.add(eng, lambda e, o=out, i=in_: e.dma_start(out=o, in_=i), r=r, w=w, dma=True)

    def dma_cast(self, out, in_, r=(), w=()):
        return self.p.add("pool", lambda e, o=out, i=in_: e.dma_start(out=o, in_=i), r=r, w=w, dma=True)


def load_cast_rows(cx, dst, src, nk, ncols, wtok, chunk=2048):
    v = src.rearrange("(k p) n -> p k n", p=128)
    for k in range(nk):
        for c0 in range(0, ncols, chunk):
            c1 = min(ncols, c0 + chunk)
            cx.dma_cast(dst[:, k, c0:c1], v[:, k, c0:c1], w=[wtok])


def layer_norm_tile(cx, r_sb, r_tok, out_sb, out_tok, g_sb, b_sb, eps_sb, scr, gtok):
    p = cx.p
    st, mv, rstd, st_tok = scr
    for c in range(2):
        p.add("dve", lambda e, c=c: e.bn_stats(out=st[:, c, :], in_=r_sb[:, c * 512:(c + 1) * 512]),
              r=[r_tok], w=[st_tok])
    p.add("dve", lambda e: e.bn_aggr(out=mv[:, :], in_=st[:, :, :].rearrange("p a b -> p (a b)")), r=[st_tok], w=[st_tok])
    p.add("act", lambda e: e.activation(out=rstd[:, :], in_=mv[:, 1:2], func=AF.Sqrt, bias=eps_sb[:, 0:1], scale=1.0),
          r=[st_tok], w=[st_tok])
    p.add("dve", lambda e: e.reciprocal(out=rstd[:, :], in_=rstd[:, :]), r=[st_tok], w=[st_tok])
    p.add("dve", lambda e: e.tensor_scalar(out=out_sb, in0=r_sb, scalar1=mv[:, 0:1], scalar2=rstd[:, 0:1],
                                           op0=ALU.subtract, op1=ALU.mult), r=[r_tok, st_tok], w=[out_tok])
    p.add("pool", lambda e: e.tensor_tensor(out=out_sb, in0=out_sb, in1=g_sb, op=ALU.mult), r=[out_tok, gtok], w=[out_tok])
    p.add("pool", lambda e: e.tensor_tensor(out=out_sb, in0=out_sb, in1=b_sb, op=ALU.add), r=[out_tok, gtok], w=[out_tok])


def build_tphase(kin):
    cx = Ctx()
    nc, p = cx.nc, cx.p
    KC = kin // 128
    G = 128
    CPB = 512 // G
    NG = TOK // G
    yT = cx.dram_in("yT", [kin, TOK], BF16)
    h_in = cx.dram_in("h", [TOK, D], F32)
    w_o = cx.dram_in("w_o", [kin, D], F32)
    w1 = cx.dram_in("w1", [D, D_FF], F32)
    w2 = cx.dram_in("w2", [D_FF, D], F32)
    lnp = cx.dram_in("lnp", [128, 4, D], F32)
    h_out = cx.dram_out("h_out", [TOK, D], F32)

    wo_sb = cx.sb("wo_sb", [128, KC, D], BF16)
    w1_sb = cx.sb("w1_sb", [128, 8, D_FF], BF16)
    w2_sb = cx.sb("w2_sb", [128, 32, D], BF16)
    lnp_sb = cx.sb("lnp_sb", [128, 4, D], F32)
    ident = cx.sb("ident_sb", [128, 128], F32)
    eps_sb = cx.sb("eps_sb", [128, 1], F32)
    t_wo, t_w1, t_w2, t_lnp, t_const = Tok("wo"), Tok("w1"), Tok("w2"), Tok("lnp"), Tok("const")

    ident_d = cx.dram_in("ident", [128, 128], F32)
    cx.dma(ident[:, :], ident_d[:, :], w=[t_const])
    p.add("pool", lambda e: e.memset(eps_sb[:, :], LN_EPS), w=[t_const])
    cx.dma(lnp_sb[:, :, :], lnp[:, :, :], w=[t_lnp])

    yT_sb = cx.sb("yT_sb", [128, KC, G], BF16)
    h_sb = cx.sb("h_sb", [128, D], F32)
    t_yT, t_h = Tok("yT"), Tok("h")
    rbuf = [cx.sb(f"rbuf{i}", [128, D], F32) for i in range(2)]
    t_rb = [Tok("rb0"), Tok("rb1")]
    h1T = cx.sb("h1T", [128, 8, G], BF16)
    t_h1T = Tok("h1T")
    uT = cx.sb("uT", [128, 32, G], BF16)
    t_uT = [Tok(f"uT{i}") for i in range(32 // CPB)]
    utmp = [cx.sb(f"utmp{i}", [128, CPB * G], F32) for i in range(2)]
    t_utmp = [Tok("utmp0"), Tok("utmp1")]
    st = cx.sb("ln_st", [128, 2, 6], F32)
    mv = cx.sb("ln_mv", [128, 2], F32)
    rstd = cx.sb("ln_rstd", [128, 1], F32)
    scr = (st, mv, rstd, Tok("lnscr"))

    yT_v = yT.rearrange("(k p) t -> p k t", p=128)
    h_v = h_in.rearrange("(n p) d -> p n d", p=128)
    ho_v = h_out.rearrange("(n p) d -> p n d", p=128)

    def load_group(g):
        cx.dma(yT_sb[:, :, :], yT_v[:, :, g * G:(g + 1) * G], w=[t_yT])
        cx.dma(h_sb[:, :], h_v[:, g, :], w=[t_h])

    load_group(0)
    load_cast_rows(cx, wo_sb, w_o, KC, D, t_wo)
    load_cast_rows(cx, w1_sb, w1, 8, D_FF, t_w1)
    load_cast_rows(cx, w2_sb, w2, 32, D, t_w2)

    for g in range(NG):
        rb, trb = rbuf[g % 2], t_rb[g % 2]
        for n in range(2):
            bank, bt = cx.bank()
            for k in range(KC):
                p.add("pe", lambda e, bank=bank, k=k, n=n: e.matmul(
                    bank, lhsT=yT_sb[:, k, :], rhs=wo_sb[:, k, n * 512:(n + 1) * 512],
                    start=(k == 0), stop=(k == KC - 1)), r=[t_yT, t_wo], w=[bt])
            p.add("dve", lambda e, bank=bank, n=n, rb=rb: e.scalar_tensor_tensor(
                out=rb[:, n * 512:(n + 1) * 512], in0=h_sb[:, n * 512:(n + 1) * 512], scalar=ALPHA,
                in1=bank, op0=ALU.mult, op1=ALU.add), r=[bt, t_h], w=[trb])
        if g + 1 < NG:
            load_group(g + 1)
        layer_norm_tile(cx, rb[:, :], trb, rb[:, :], trb, lnp_sb[:, 0, :], lnp_sb[:, 1, :], eps_sb, scr, t_lnp)
        for q in range(2):
            bank, bt = cx.bank()
            for j in range(4):
                f = q * 4 + j
                p.add("pe", lambda e, bank=bank, j=j, f=f, rb=rb: e.transpose(
                    bank[:, j * 128:(j + 1) * 128], rb[:, f * 128:(f + 1) * 128], ident[:, :]),
                    r=[trb, t_const], w=[bt])
            p.add("act", lambda e, bank=bank, q=q: e.activation(
                out=h1T[:, q * 4:(q + 1) * 4, :], in_=bank.rearrange("p (j c) -> p j c", j=4), func=AF.Copy),
                r=[bt], w=[t_h1T])
        for cp in range(32 // CPB):
            bank, bt = cx.bank()
            for j in range(CPB):
                ch = cp * CPB + j
                for k in range(8):
                    p.add("pe", lambda e, bank=bank, j=j, ch=ch, k=k: e.matmul(
                        bank[:, j * G:(j + 1) * G], lhsT=w1_sb[:, k, ch * 128:(ch + 1) * 128], rhs=h1T[:, k, :],
                        start=(k == 0), stop=(k == 7)), r=[t_w1, t_h1T], w=[bt])
            ub = cp % 2
            p.add("act", lambda e, bank=bank, ub=ub: e.activation(out=utmp[ub][:, :], in_=bank[:, 0:CPB * G], func=AF.Relu),
                  r=[bt], w=[t_utmp[ub]])
            p.add("pool", lambda e, cp=cp, ub=ub: e.tensor_tensor(
                out=uT[:, CPB * cp:CPB * cp + CPB, :], in0=utmp[ub][:, :].rearrange("p (j c) -> p j c", j=CPB),
                in1=utmp[ub][:, :].rearrange("p (j c) -> p j c", j=CPB), op=ALU.mult), r=[t_utmp[ub]], w=[t_uT[cp]])
        for n in range(2):
            bank, bt = cx.bank()
            for k in range(32):
                p.add("pe", lambda e, bank=bank, k=k, n=n: e.matmul(
                    bank, lhsT=uT[:, k, :], rhs=w2_sb[:, k, n * 512:(n + 1) * 512],
                    start=(k == 0), stop=(k == 31)), r=[t_uT[k // CPB], t_w2], w=[bt])
            p.add("dve", lambda e, bank=bank, n=n, rb=rb: e.scalar_tensor_tensor(
                out=rb[:, n * 512:(n + 1) * 512], in0=rb[:, n * 512:(n + 1) * 512], scalar=ALPHA,
                in1=bank, op0=ALU.mult, op1=ALU.add), r=[bt, trb], w=[trb])
        layer_norm_tile(cx, rb[:, :], trb, rb[:, :], trb, lnp_sb[:, 2, :], lnp_sb[:, 3, :], eps_sb, scr, t_lnp)
        cx.dma(ho_v[:, g, :], rb[:, :], r=[trb])
    p.emit()
    return nc


def build_ssd(nblocks=SEQ // 512):
    cx = Ctx(nbanks=5)
    cx.p = PProxy(cx.p)
    pool_ctr = {"t": 0, "b": 0}

    def tbank():
        i = pool_ctr["t"] % 3
        pool_ctr["t"] += 1
        return cx.ps[:, i, :], cx.ps_tok[i]

    def bbank():
        i = 3 + pool_ctr["b"] % 2
        pool_ctr["b"] += 1
        return cx.ps[:, i, :], cx.ps_tok[i]

    nc, p = cx.nc, cx.p
    BLK = 512
    seq = nblocks * BLK
    psB = nc.alloc_psum_tensor("psum_bf", [128, 1024], BF16)
    t_psB = Tok("psB", excl=True)
    psYS = nc.alloc_psum_tensor("psum_ys", [128, 2, 512], F32)
    t_YS = [Tok("YS0", excl=True), Tok("YS1", excl=True)]

    hT = cx.dram_in("hT", [D, seq], F32)
    wfm = cx.dram_in("wfm", [D, 512], F32)
    wtm = cx.dram_in("wtm", [D, 260], F32)
    cw = cx.dram_in("cw", [128, 4, 4], F32)
    cb = cx.dram_in("cb", [128, 4], F32)
    hp = cx.dram_in("hp", [128, 3, 4], F32)
    nw = cx.dram_in("nw", [128, 256], F32)
    cst_f = cx.dram_in("cst_f", [128, 2, 128], F32)
    cst_b = cx.dram_in("cst_b", [128, 128 + 512], BF16)
    yn = cx.dram_out("yn", [seq, 256], BF16)

    wfm_sb = cx.sb("wfm_sb", [128, 8, 512], BF16)
    wtm_sb = cx.sb("wtm_sb", [128, 8, 260], BF16)
    cw_sb = cx.sb("cw_sb", [128, 4, 4], F32)
    cb_sb = cx.sb("cb_sb", [128, 4], F32)
    hp_sb = cx.sb("hp_sb", [128, 3, 4], F32)
    nw_sb = cx.sb("nw_sb", [128, 256], F32)
    cf_sb = cx.sb("cf_sb", [128, 2, 128], F32)
    cbf_sb = cx.sb("cbf_sb", [128, 640], BF16)
    aneg = cx.sb("aneg", [128, 4], F32)
    onecol = cx.sb("onecol", [128, 1], F32)
    neghalf = cx.sb("neghalf", [128, 1], F32)
    S = cx.sb("S_state", [128, 256], F32)
    S_bf = cx.sb("S_bf", [128, 256], BF16)
    t_w, t_c, t_S, t_Sbf = Tok("w"), Tok("c"), Tok("S"), Tok("Sbf")
    triu, ones_f = cf_sb[:, 0, :], cf_sb[:, 1, :]
    ident_b, negmask = cbf_sb[:, 0:128], cbf_sb[:, 128:640]

    for dst, src in ((cw_sb, cw), (cb_sb, cb), (hp_sb, hp), (nw_sb, nw), (cf_sb, cst_f), (cbf_sb, cst_b)):
        nd = len(dst.shape)
        idx = tuple([slice(None)] * nd)
        cx.dma(dst[idx], src[idx], w=[t_c])
    load_cast_rows(cx, wfm_sb, wfm, 8, 512, t_w)
    load_cast_rows(cx, wtm_sb, wtm, 8, 260, t_w)
    p.add("pool", lambda e: e.memset(onecol[:, :], 1.0), w=[t_c])
    p.add("pool", lambda e: e.memset(neghalf[:, :], -0.5), w=[t_c])
    p.add("pool", lambda e: e.memset(S[:, :], 0.0), w=[t_S])
    p.add("pool", lambda e: e.memset(S_bf[:, :], 0.0), w=[t_Sbf])
    p.add("act", lambda e: e.activation(out=aneg[:, :], in_=hp_sb[:, 1, :], func=AF.Exp), r=[t_c], w=[t_c])
    p.add("dve", lambda e: e.tensor_scalar(out=aneg[:, :], in0=aneg[:, :], scalar1=-1.0, scalar2=None, op0=ALU.mult),
          r=[t_c], w=[t_c])

    hTf, t_hTf = cx.dbuf("hTf", [128, 8, BLK], F32)
    hTb, t_hTb = cx.dbuf("hTb", [128, 8, BLK], BF16, n=3)
    xpre, t_xpre = cx.dbuf("xpre", [128, 4, 3 + BLK], F32)
    acc, t_acc = cx.dbuf("cacc", [128, BLK], F32)
    xbcT, t_xbcT = cx.dbuf("xbcT", [128, 4, BLK], BF16)
    zs, t_zs = cx.dbuf("zs", [128, 4, 256], F32)
    dtr, t_dtr = cx.dbuf("dtr", [128, 4, 4], F32)
    dte, t_dte = cx.dbuf("dte", [128, 4, 4], F32)
    dts, t_dts = cx.dbuf("dts", [128, 4, 4], F32)
    das, t_das = cx.dbuf("das", [128, 4, 4], F32)
    xdt, t_xdt = cx.dbuf("xdt", [128, 256], BF16)
    ysk, t_ysk = cx.dbuf("ysk", [128, 256], F32)
    Btm, t_Btm = cx.dbuf("Btm", [128, 128], BF16)
    daT, t_daT = cx.dbuf("daT", [128, 4, 128], F32)
    nacs, t_nacs = cx.dbuf("nacs", [128, 4], F32)
    e_sb, t_e = cx.dbuf("e_sb", [128, 4], F32)
    w_sb, t_wd = cx.dbuf("w_sb", [128, 4], F32)
    dec, t_dec = cx.dbuf("dec", [128, 4], F32)
    Dm, t_Dm = cx.dbuf("Dm", [128, 4, 128], F32)
    Gm, t_G = cx.dbuf("Gm", [128, 4, 128], BF16)
    yc, t_yc = cx.dbuf("yc", [128, 256], F32)
    sq, t_sq = cx.dbuf("sq", [128, 256], F32)
    ss, t_ss = cx.dbuf("ss", [128, 2], F32)
    yno, t_yno = cx.dbuf("yno", [128, 256], BF16)
    xdw, t_xdw = cx.dbuf("xdw", [128, 256], BF16)

    hT_v = hT.rearrange("(k p) t -> p k t", p=128)
    p.add("pool", lambda e: e.memset(xpre[1][:, :, :], 0.0), w=[t_xpre[1]])

    def load_block(b):
        fb, hb = b % 2, b % 3
        for half in range(2):
            cx.dma(hTf[fb][:, half * 4:(half + 1) * 4, :], hT_v[:, half * 4:(half + 1) * 4, b * BLK:(b + 1) * BLK],
                   w=[t_hTf[fb]])
        p.add("act", lambda e, fb=fb, hb=hb: e.activation(out=hTb[hb][:, 0:4, :], in_=hTf[fb][:, 0:4, :], func=AF.Copy),
              r=[t_hTf[fb]], w=[t_hTb[hb]])
        p.add("dve", lambda e, fb=fb, hb=hb: e.tensor_copy(out=hTb[hb][:, 4:8, :], in_=hTf[fb][:, 4:8, :]),
              r=[t_hTf[fb]], w=[t_hTb[hb]])

    load_block(0)
    if nblocks > 1:
        load_block(1)
    bc4 = lambda ap: ap.unsqueeze(2).broadcast_to([128, 4, 64])
    v464 = lambda ap: ap.rearrange("p (h c) -> p h c", h=4)
    def blockpro(b):
        pb = b % 2
        hb = b % 3
        if b + 2 < nblocks:
            load_block(b + 2)
        p.add("pool", lambda e, pb=pb: e.tensor_copy(out=xpre[pb][:, :, 0:3], in_=xpre[1 - pb][:, :, BLK:BLK + 3]),
              r=[t_xpre[1 - pb]], w=[t_xpre[pb]])
        for ct in range(4):
            bank, bt = bbank()
            for k in range(8):
                p.add("pe", lambda e, bank=bank, k=k, ct=ct, hb=hb: e.matmul(
                    bank, lhsT=wfm_sb[:, k, ct * 128:(ct + 1) * 128], rhs=hTb[hb][:, k, :],
                    start=(k == 0), stop=(k == 7)), r=[t_w, t_hTb[hb]], w=[bt])
            p.add("act", lambda e, bank=bank, ct=ct, pb=pb: e.activation(
                out=xpre[pb][:, ct, 3:3 + BLK], in_=bank, func=AF.Copy), r=[bt], w=[t_xpre[pb]])
        for ct in range(4):
            a = ct % 2
            p.add("act", lambda e, ct=ct, pb=pb, a=a: e.activation(
                out=acc[a][:, :], in_=xpre[pb][:, ct, 0:BLK], func=AF.Identity,
                scale=cw_sb[:, ct, 0:1], bias=cb_sb[:, ct:ct + 1]), r=[t_xpre[pb], t_c], w=[t_acc[a]])
            for k in range(1, 4):
                p.add("dve", lambda e, ct=ct, pb=pb, a=a, k=k: e.scalar_tensor_tensor(
                    out=acc[a][:, :], in0=xpre[pb][:, ct, k:k + BLK], scalar=cw_sb[:, ct, k:k + 1], in1=acc[a][:, :],
                    op0=ALU.mult, op1=ALU.add), r=[t_xpre[pb], t_acc[a], t_c], w=[t_acc[a]])
            p.add("act", lambda e, ct=ct, pb=pb, a=a: e.activation(
                out=xbcT[pb][:, ct, :], in_=acc[a][:, :], func=AF.Silu), r=[t_acc[a]], w=[t_xbcT[pb]])
        for t in range(4):
            bank, bt = bbank()
            for k in range(8):
                p.add("pe", lambda e, bank=bank, k=k, t=t, hb=hb: e.matmul(
                    bank[:, 0:260], lhsT=hTb[hb][:, k, t * 128:(t + 1) * 128], rhs=wtm_sb[:, k, :],
                    start=(k == 0), stop=(k == 7)), r=[t_w, t_hTb[hb]], w=[bt])
            p.add("act", lambda e, bank=bank, t=t, pb=pb: e.activation(
                out=zs[pb][:, t, :], in_=bank[:, 0:256], func=AF.Silu), r=[bt], w=[t_zs[pb]])
            p.add("dve", lambda e, bank=bank, t=t, pb=pb: e.tensor_tensor(
                out=dtr[pb][:, t, :], in0=bank[:, 256:260], in1=hp_sb[:, 0, :], op=ALU.add),
                r=[bt, t_c], w=[t_dtr[pb]])
        p.add("act", lambda e, pb=pb: e.activation(out=dte[pb][:, :, :], in_=dtr[pb][:, :, :], func=AF.Exp),
              r=[t_dtr[pb]], w=[t_dte[pb]])
        p.add("act", lambda e, pb=pb: e.activation(out=dts[pb][:, :, :], in_=dte[pb][:, :, :], func=AF.Ln,
                                                   bias=onecol[:, 0:1], scale=1.0), r=[t_dte[pb], t_c], w=[t_dts[pb]])
        p.add("dve", lambda e, pb=pb: e.tensor_tensor(
            out=das[pb][:, :, :], in0=dts[pb][:, :, :], in1=aneg[:, :].unsqueeze(1).broadcast_to([128, 4, 4]),
            op=ALU.mult), r=[t_dts[pb], t_c], w=[t_das[pb]])
    def front(n):
        if True:
            b, t = divmod(n, 4)
            pb, q, c0 = b % 2, n % 2, t * 128
            for j in range(3):
                p.add("pe", lambda e, j=j, pb=pb, c0=c0: e.transpose(
                    psB[:, j * 128:(j + 1) * 128], xbcT[pb][:, j, c0:c0 + 128], ident_b),
                    r=[t_xbcT[pb], t_c], w=[t_psB])
            p.add("dve", lambda e, q=q, pb=pb, t=t: e.tensor_tensor(
                out=v464(xdt[q][:, :]), in0=v464(psB[:, 0:256]), in1=bc4(dts[pb][:, t, :]), op=ALU.mult),
                r=[t_psB, t_dts[pb]], w=[t_xdt[q]])
            p.add("dve", lambda e, q=q: e.tensor_tensor(
                out=v464(ysk[q][:, :]), in0=v464(psB[:, 0:256]), in1=bc4(hp_sb[:, 2, :]), op=ALU.mult),
                r=[t_psB, t_c], w=[t_ysk[q]])
            p.add("act", lambda e, q=q: e.activation(out=Btm[q][:, :], in_=psB[:, 256:384], func=AF.Copy),
                  r=[t_psB], w=[t_Btm[q]])
            bankA, btA = tbank()
            p.add("pe", lambda e, bankA=bankA, pb=pb, t=t: e.matmul(
                bankA[:, 0:4], lhsT=triu, rhs=das[pb][:, t, :], start=True, stop=True), r=[t_c, t_das[pb]], w=[btA])
            p.add("dve", lambda e, q=q, pb=pb, t=t: e.tensor_tensor(
                out=daT[q][:, :, :], in0=triu.unsqueeze(1).broadcast_to([128, 4, 128]),
                in1=das[pb][:, t, :].unsqueeze(2).broadcast_to([128, 4, 128]), op=ALU.mult),
                r=[t_c, t_das[pb]], w=[t_daT[q]])
            bankR, btR = tbank()
            p.add("pe", lambda e, bankR=bankR, q=q: e.matmul(
                bankR, lhsT=ones_f, rhs=daT[q][:, :, :].rearrange("p h l -> p (h l)"), start=True, stop=False),
                r=[t_c, t_daT[q]], w=[btR])
            p.add("pe", lambda e, bankR=bankR: e.matmul(bankR, lhsT=ident_b, rhs=negmask, start=False, stop=True),
                  r=[t_c], w=[btR])
            p.add("act", lambda e, bankA=bankA, q=q: e.activation(
                out=nacs[q][:, :], in_=bankA[:, 0:4], func=AF.Identity, scale=-1.0), r=[btA], w=[t_nacs[q]])
            p.add("act", lambda e, bankA=bankA, q=q: e.activation(out=e_sb[q][:, :], in_=bankA[:, 0:4], func=AF.Exp),
                  r=[btA], w=[t_e[q]])
            Rv = bankR.rearrange("p (h l) -> p h l", h=4)
            for h in range(4):
                p.add("act", lambda e, bankR=bankR, q=q, h=h: e.activation(
                    out=Dm[q][:, h, :], in_=bankR[:, h * 128:(h + 1) * 128], func=AF.Exp,
                    bias=nacs[q][:, h:h + 1], scale=1.0), r=[btR, t_nacs[q]], w=[t_Dm[q]])
            p.add("dve", lambda e, Rv=Rv, q=q: e.tensor_tensor(
                out=w_sb[q][:, :], in0=Rv[:, :, 127], in1=nacs[q][:, :], op=ALU.add),
                r=[btR, t_nacs[q]], w=[t_wd[q]])
            p.add("act", lambda e, q=q: e.activation(out=w_sb[q][:, :], in_=w_sb[q][:, :], func=AF.Exp),
                  r=[t_wd[q]], w=[t_wd[q]])
            p.add("act", lambda e, Rv=Rv, q=q: e.activation(out=dec[q][:, :], in_=Rv[:, :, 127], func=AF.Exp),
                  r=[btR], w=[t_dec[q]])
            bankC, btC = bankA[:, 128:256], btA
            p.add("pe", lambda e, bankC=bankC, pb=pb, c0=c0: e.matmul(
                bankC, lhsT=xbcT[pb][:, 2, c0:c0 + 128], rhs=xbcT[pb][:, 3, c0:c0 + 128],
                start=True, stop=True), r=[t_xbcT[pb]], w=[btC])
            p.add("dve", lambda e, bankC=bankC, q=q: e.tensor_tensor(
                out=Gm[q][:, :, :], in0=Dm[q][:, :, :], in1=bankC.unsqueeze(1).broadcast_to([128, 4, 128]),
                op=ALU.mult), r=[btC, t_Dm[q]], w=[t_G[q]])
            bankY, btY = psYS[:, q, :], t_YS[q]
            for h in range(4):
                p.add("pe", lambda e, bankY=bankY, q=q, h=h: e.matmul(
                    bankY[:, h * 64:(h + 1) * 64], lhsT=Gm[q][:, h, :], rhs=xdt[q][:, h * 64:(h + 1) * 64],
                    start=True, stop=True), r=[t_G[q], t_xdt[q]], w=[btY])
            p.add("pool", lambda e, q=q: e.tensor_tensor(
                out=v464(xdw[q][:, :]), in0=v464(xdt[q][:, :]), in1=bc4(w_sb[q][:, :]), op=ALU.mult),
                r=[t_xdt[q], t_wd[q]], w=[t_xdw[q]])
            p.add("pe", lambda e, q=q: e.matmul(
                psYS[:, q, 256:512], lhsT=Btm[q][:, :], rhs=xdw[q][:, :], start=True, stop=True),
                r=[t_Btm[q], t_xdw[q]], w=[t_YS[q]])
    def back(n):
        if True:
            b, t = divmod(n, 4)
            pb, q, c0 = b % 2, n % 2, t * 128
            row0 = b * BLK + c0
            bankY, btY = psYS[:, q, :], t_YS[q]
            bankO, btO = tbank()
            p.add("pe", lambda e, bankO=bankO, pb=pb, c0=c0: e.matmul(
                bankO[:, 0:256], lhsT=xbcT[pb][:, 3, c0:c0 + 128], rhs=S_bf[:, :], start=True, stop=True),
                r=[t_xbcT[pb], t_Sbf], w=[btO])
            p.add("dve", lambda e, bankO=bankO, q=q: e.tensor_tensor(
                out=v464(yc[q][:, :]), in0=v464(bankO[:, 0:256]), in1=bc4(e_sb[q][:, :]), op=ALU.mult),
                r=[btO, t_e[q]], w=[t_yc[q]])
            p.add("dve", lambda e, bankY=bankY, q=q: e.tensor_tensor(
                out=yc[q][:, :], in0=yc[q][:, :], in1=bankY[:, 0:256], op=ALU.add), r=[btY, t_yc[q]], w=[t_yc[q]])
            p.add("pool", lambda e, q=q: e.tensor_tensor(out=yc[q][:, :], in0=yc[q][:, :], in1=ysk[q][:, :], op=ALU.add),
                  r=[t_yc[q], t_ysk[q]], w=[t_yc[q]])
            p.add("pool", lambda e, q=q, pb=pb, t=t: e.tensor_tensor(
                out=yc[q][:, :], in0=yc[q][:, :], in1=zs[pb][:, t, :], op=ALU.mult),
                r=[t_yc[q], t_zs[pb]], w=[t_yc[q]])
            p.add("act", lambda e, q=q: e.activation(out=sq[q][:, :], in_=yc[q][:, :], func=AF.Square,
                                                     accum_out=ss[q][:, 0:1]), r=[t_yc[q]], w=[t_sq[q], t_ss[q]])
            p.add("dve", lambda e, q=q: e.tensor_scalar(out=ss[q][:, 1:2], in0=ss[q][:, 0:1], scalar1=1.0 / 256.0,
                                                        scalar2=RMS_EPS, op0=ALU.mult, op1=ALU.add),
                  r=[t_ss[q]], w=[t_ss[q]])
            p.add("pool", lambda e, q=q: e.tensor_tensor(out=ss[q][:, 1:2], in0=ss[q][:, 1:2], in1=neghalf[:, 0:1],
                                                         op=ALU.pow), r=[t_ss[q], t_c], w=[t_ss[q]])
            p.add("dve", lambda e, q=q: e.scalar_tensor_tensor(
                out=yno[q][:, :], in0=yc[q][:, :], scalar=ss[q][:, 1:2], in1=nw_sb[:, :], op0=ALU.mult, op1=ALU.mult),
                r=[t_yc[q], t_ss[q], t_c], w=[t_yno[q]])
            cx.dma(yn[row0:row0 + 128, :], yno[q][:, :], r=[t_yno[q]])
            p.add("pool", lambda e, q=q: e.tensor_tensor(
                out=v464(S[:, :]), in0=v464(S[:, :]), in1=bc4(dec[q][:, :]), op=ALU.mult),
                r=[t_S, t_dec[q]], w=[t_S])
            p.add("dve", lambda e, q=q: e.tensor_tensor(out=S[:, :], in0=S[:, :], in1=psYS[:, q, 256:512], op=ALU.add),
                  r=[t_S, t_YS[q]], w=[t_S])
            p.add("act", lambda e: e.activation(out=S_bf[:, :], in_=S[:, :], func=AF.Copy), r=[t_S], w=[t_Sbf])
    ntiles = nblocks * 4
    blockpro(0)
    front(0)
    chunks = []
    for n in range(ntiles):
        b, t = divmod(n, 4)
        lists = []
        if t == 0 and b + 1 < nblocks:
            bp = p.record(blockpro, b + 1)
            c = (len(bp) + 2) // 3
            chunks = [bp[0:c], bp[c:2 * c], bp[2 * c:]]
        if t < 3 and chunks:
            lists.append(chunks[t])
            if t == 2:
                chunks = []
        if n + 1 < ntiles:
            lists.append(p.record(front, n + 1))
        lists.append(p.record(back, n))
        p.merge(*lists)
    p.emit()
    return nc


def ssd_inputs(hT, w_in, conv_w, conv_b, dt_bias, a_log, d_skip, norm_w, g):
    xo, bo, co, do = 2048, 4096, 5120, 6144
    wfm = np.concatenate([w_in[:, xo + g * 256: xo + (g + 1) * 256], w_in[:, bo + g * 128: bo + (g + 1) * 128],
                          w_in[:, co + g * 128: co + (g + 1) * 128]], axis=1)
    wtm = np.concatenate([w_in[:, g * 256:(g + 1) * 256], w_in[:, do + 4 * g: do + 4 * g + 4]], axis=1)
    cidx = np.concatenate([np.arange(g * 256, (g + 1) * 256), 2048 + np.arange(g * 128, (g + 1) * 128),
                           3072 + np.arange(g * 128, (g + 1) * 128)])
    cw = np.ascontiguousarray(conv_w[:, cidx].reshape(4, 4, 128).transpose(2, 1, 0))
    cb = np.ascontiguousarray(conv_b[cidx].reshape(4, 128).T)
    hp = np.stack([dt_bias[4 * g:4 * g + 4], a_log[4 * g:4 * g + 4], d_skip[4 * g:4 * g + 4]])
    hp = np.ascontiguousarray(np.broadcast_to(hp[None], (128, 3, 4)))
    nw = np.ascontiguousarray(np.broadcast_to(norm_w[None, g * 256:(g + 1) * 256], (128, 256)))
    return {"hT": hT, "wfm": np.ascontiguousarray(wfm), "wtm": np.ascontiguousarray(wtm), "cw": cw, "cb": cb,
            "hp": hp.astype(np.float32), "nw": nw.astype(np.float32), "cst_f": SSD_CST_F, "cst_b": SSD_CST_B}


def _ssd_consts():
    k = np.arange(128)
    triu = (k[:, None] <= k[None, :]).astype(np.float32)
    cst_f = np.ascontiguousarray(np.stack([triu, np.ones((128, 128), np.float32)], axis=1))
    neg = np.where(k[:, None] > k[None, :], -30000.0, 0.0).astype(np.float32)
    cst_b = np.concatenate([np.eye(128, dtype=np.float32), np.tile(neg, (1, 4))], axis=1).astype(ml_dtypes.bfloat16)
    return cst_f, np.ascontiguousarray(cst_b)


SSD_CST_F, SSD_CST_B = _ssd_consts()


def build_attn(seq=SEQ, same_kv=True, nd1=0, nd2=0):
    cx = Ctx(nbanks=8)
    nc, p = cx.nc, cx.p
    BLK = 512
    nblocks = seq // BLK
    nkb = seq // 128
    QG = 8
    nqg = nkb // QG

    hTq = cx.dram_in("hTq", [D, seq], F32)
    wqkv = cx.dram_in("wqkv", [D, 384], F32)
    if same_kv:
        kT_out = cx.dram_out("kT_out", [128, seq], BF16)
        V_out = cx.dram_out("V_out", [128, nkb * 128], BF16)
    else:
        kT_in = cx.dram_in("kT_in", [128, seq], BF16)
        V_in = cx.dram_in("V_in", [128, nkb * 128], BF16)
    cst = cx.dram_in("acst", [128, 1024], BF16)
    negm_d = cx.dram_in("negm", [64, 1], F32)
    oT = cx.dram_out("oT", [128, seq], BF16)

    wqkv_sb = cx.sb("wqkv_sb", [128, 8, 384], BF16)
    wq_sb, wk_sb, wv_sb = wqkv_sb[:, :, 0:128], wqkv_sb[:, :, 128:256], wqkv_sb[:, :, 256:384]
    cst_sb = cx.sb("cst_sb", [128, 1024], BF16)
    negm = cx.sb("negm_sb", [64, 1], F32)
    qT = cx.sb("qT", [128, seq], BF16)
    kT = cx.sb("kT", [128, seq], BF16)
    V = cx.sb("V", [128, nkb, 128], BF16)
    t_w, t_c, t_q, t_k, t_v = Tok("w"), Tok("c"), Tok("q"), Tok("k"), Tok("v")
    negMinc, diagmask = cst_sb[:, 0:128], cst_sb[:, 128:256]
    ones_b, ident_b, zero_b = cst_sb[:, 256:384], cst_sb[:, 384:512], cst_sb[:, 512:1024]
    ones_col = cx.sb("ones_col", [128, 1], F32)
    negones = cx.sb("negones", [128, 2], BF16)
    p.add("pool", lambda e: e.memset(ones_col[:, :], 1.0), w=[t_c])
    p.add("pool", lambda e: e.memset(negones[:, :], -1.0), w=[t_c])

    cx.dma(cst_sb[:, :], cst[:, :], w=[t_c])
    cx.dma(negm[:, :], negm_d[:, :], w=[t_c])
    wv_ = wqkv.rearrange("(k p) n -> p k n", p=128)
    for half in range(2):
        cx.dma_cast(wqkv_sb[:, half * 4:(half + 1) * 4, :], wv_[:, half * 4:(half + 1) * 4, :], w=[t_w])

    hTf, t_hTf = cx.dbuf("hTf", [128, 8, BLK], F32)
    hTb, t_hTb = cx.dbuf("hTb", [128, 8, BLK], BF16)

    passes = [("qkv", hTq)] if same_kv else [("q", hTq)]
    if not same_kv:
        for c4 in range(4):
            cs = slice(c4 * (seq // 4), (c4 + 1) * (seq // 4))
            cx.dma(kT[:, cs], kT_in[:, cs], w=[t_k])
        Vf = V[:, :, :].rearrange("p b c -> p (b c)")
        for c4 in range(4):
            cs = slice(c4 * (nkb * 32), (c4 + 1) * (nkb * 32))
            cx.dma(Vf[:, cs], V_in[:, cs], w=[t_v])
    cnt = 0
    for what, src in passes:
        src_v = src.rearrange("(k p) t -> p k t", p=128)

        def load_block(b, cnt, src_v=src_v):
            pb = cnt % 2
            for half in range(2):
                cx.dma(hTf[pb][:, half * 4:(half + 1) * 4, :], src_v[:, half * 4:(half + 1) * 4, b * BLK:(b + 1) * BLK],
                       w=[t_hTf[pb]])

        load_block(0, cnt)
        for b in range(nblocks):
            pb = cnt % 2
            if b + 1 < nblocks:
                load_block(b + 1, cnt + 1)
            cnt += 1
            p.add("act", lambda e, pb=pb: e.activation(out=hTb[pb][:, 0:3, :], in_=hTf[pb][:, 0:3, :], func=AF.Copy),
                  r=[t_hTf[pb]], w=[t_hTb[pb]])
            p.add("dve", lambda e, pb=pb: e.tensor_copy(out=hTb[pb][:, 3:6, :], in_=hTf[pb][:, 3:6, :]),
                  r=[t_hTf[pb]], w=[t_hTb[pb]])
            p.add("pool", lambda e, pb=pb: e.tensor_copy(out=hTb[pb][:, 6:8, :], in_=hTf[pb][:, 6:8, :]),
                  r=[t_hTf[pb]], w=[t_hTb[pb]])
            cols = slice(b * BLK, (b + 1) * BLK)
            if "q" in what:
                bank, bt = cx.bank()
                for k in range(8):
                    p.add("pe", lambda e, bank=bank, k=k, pb=pb: e.matmul(
                        bank, lhsT=wq_sb[:, k, :], rhs=hTb[pb][:, k, :], start=(k == 0), stop=(k == 7)),
                        r=[t_w, t_hTb[pb]], w=[bt])
                p.add("act", lambda e, bank=bank, cols=cols: e.activation(out=qT[:, cols], in_=bank, func=AF.Copy, scale=0.125),
                      r=[bt], w=[t_q])
            if "k" in what:
                bank, bt = cx.bank()
                for k in range(8):
                    p.add("pe", lambda e, bank=bank, k=k, pb=pb: e.matmul(
                        bank, lhsT=wk_sb[:, k, :], rhs=hTb[pb][:, k, :], start=(k == 0), stop=(k == 7)),
                        r=[t_w, t_hTb[pb]], w=[bt])
                p.add("dve", lambda e, bank=bank, cols=cols: e.tensor_copy(out=kT[:, cols], in_=bank), r=[bt], w=[t_k])
                bank, bt = cx.bank()
                for t in range(4):
                    for k in range(8):
                        p.add("pe", lambda e, bank=bank, k=k, t=t, pb=pb: e.matmul(
                            bank[:, t * 128:(t + 1) * 128], lhsT=hTb[pb][:, k, t * 128:(t + 1) * 128], rhs=wv_sb[:, k, :],
                            start=(k == 0), stop=(k == 7)), r=[t_w, t_hTb[pb]], w=[bt])
                p.add("act", lambda e, bank=bank, b=b: e.activation(
                    out=V[:, 4 * b:4 * b + 4, :], in_=bank.rearrange("p (t c) -> p t c", t=4), func=AF.Copy),
                    r=[bt], w=[t_v])

    if same_kv:
        Vf = V[:, :, :].rearrange("p b c -> p (b c)")
        for c4 in range(4):
            cs = slice(c4 * (seq // 4), (c4 + 1) * (seq // 4))
            cx.dma(kT_out[:, cs], kT[:, cs], r=[t_k])
            cs = slice(c4 * (nkb * 32), (c4 + 1) * (nkb * 32))
            cx.dma(V_out[:, cs], Vf[:, cs], r=[t_v])

    NQ = QG * 128
    zb = [cx.ps[:, 0:2, :].rearrange("p b c -> p (b c)"), cx.ps[:, 2:4, :].rearrange("p b c -> p (b c)")]
    ob = cx.ps[:, 4:6, :].rearrange("p b c -> p (b c)")
    accb = cx.ps[:, 6:8, :].rearrange("p b c -> p (b c)")
    t_z = [Tok("zA", excl=True), Tok("zB", excl=True)]
    t_zp = [[Tok(f"z{h}{i}", excl=True) for i in range(2)] for h in range(2)]
    t_Wp = [[Tok(f"W{h}{i}") for i in range(2)] for h in range(2)]
    t_Ep = [[Tok(f"E{h}{i}") for i in range(2)] for h in range(2)]
    t_Lp = [[Tok(f"L{h}{i}") for i in range(2)] for h in range(2)]
    t_o, t_acc = Tok("o", excl=True), [Tok("accA", excl=True), Tok("accB", excl=True)]
    for hd in range(2):
        for bnk in (2 * hd, 2 * hd + 1):
            t_zp[hd][bnk - 2 * hd].rs.update(cx.ps_tok[bnk].rs)
    for bnk in (4, 5):
        t_o.rs.update(cx.ps_tok[bnk].rs)
    for hd in range(2):
        for bnk in (6, 7):
            t_acc[hd].rs.update(cx.ps_tok[bnk].rs)
    E_sb, t_E = cx.dbuf("E_sb", [128, NQ], F32)
    L_sb, t_L = cx.dbuf("L_sb", [128, NQ], BF16)
    W_sb, t_W = cx.dbuf("W_sb", [128, NQ], BF16)
    hi_t = cx.sb("hi_t", [64, NQ], BF16)
    acc_hl = cx.sb("acc_hl", [64, NQ], BF16)
    t_him, t_hlm, t_accm = Tok("hi"), Tok("hl"), Tok("accm", excl=True)
    for bnk in (6, 7):
        t_accm.rs.update(cx.ps_tok[bnk].rs)
    obuf, t_ob = cx.dbuf("obuf", [128, NQ], BF16)
    rows = [slice(0, 2), slice(32, 34)]

    def dummies(n):
        for i in range(n):
            c0 = 512 * (i % 2)
            p.add("pe", lambda e, c0=c0: e.matmul(ob[:, c0:c0 + 512], lhsT=zero_b[:, 0:128], rhs=zero_b[:, 0:512],
                                                  start=False, stop=False, skip_group_check=True), r=[t_c], w=[t_o])

    def pieces(lo):
        out = []
        for c0 in (0, 512):
            a, bnd = max(lo, c0), c0 + 512
            if a < bnd:
                out.append((a, bnd))
        return out

    for qg in range(nqg):
        i0 = qg * QG
        qcol0 = i0 * 128
        for c0 in (0, 512):
            p.add("pe", lambda e, c0=c0: e.matmul(ob[:, c0:c0 + 512], lhsT=zero_b[:, 0:128], rhs=zero_b[:, 0:512],
                                                  start=True, stop=False, skip_group_check=True), r=[t_c], w=[t_o])
            p.add("pe", lambda e, c0=c0: e.matmul(
                accb[0:34, c0:c0 + 512], lhsT=zero_b[:, 0:34], rhs=zero_b[:, 0:512],
                start=True, stop=False, skip_group_check=True), r=[t_c], w=[t_accm])
        p.add("pool", lambda e: e.memset(acc_hl[0:34, :], 0.0), w=[t_hlm])
        def emit_z(j, hd):
            lo = max(j - i0, 0) * 128
            ks = slice(j * 128, (j + 1) * 128)
            hp_ = slice(hd * 64, (hd + 1) * 64)
            for (a, bnd) in pieces(lo):
                p.add("pe", lambda e, hd=hd, hp_=hp_, a=a, bnd=bnd, ks=ks, qcol0=qcol0: e.matmul(
                    zb[hd][:, a:bnd], lhsT=kT[hp_, ks], rhs=qT[hp_, qcol0 + a:qcol0 + bnd],
                    start=True, stop=False, skip_group_check=True), r=[t_k, t_q], w=[t_zp[hd][a // 512]])
            if j >= i0:
                p.add("pe", lambda e, hd=hd, lo=lo: e.matmul(
                    zb[hd][:, lo:lo + 128], lhsT=ident_b, rhs=diagmask, start=False, stop=False,
                    skip_group_check=True), r=[t_c], w=[t_zp[hd][lo // 512]])

        jtop = i0 + QG - 1
        for hd in range(2):
            emit_z(jtop, hd)
        for j in range(jtop, -1, -1):
            lo = max(j - i0, 0) * 128
            pcs = pieces(lo)
            for hd in range(2):
                for (a, bnd) in pcs:
                    h2 = a // 512
                    p.add("act", lambda e, hd=hd, a=a, bnd=bnd: e.activation(
                        out=E_sb[hd][:, a:bnd], in_=zb[hd][:, a:bnd], func=AF.Exp),
                        r=[t_zp[hd][h2]], w=[t_Ep[hd][h2]])
                    p.add("act", lambda e, hd=hd, a=a, bnd=bnd: e.activation(
                        out=L_sb[hd][:, a:bnd], in_=E_sb[hd][:, a:bnd], func=AF.Ln, bias=ones_col[:, 0:1], scale=1.0),
                        r=[t_Ep[hd][h2], t_c], w=[t_Lp[hd][h2]])
            for hd in range(2):
                for (a, bnd) in pcs:
                    h2 = a // 512
                    p.add("pe", lambda e, hd=hd, a=a, bnd=bnd: e.matmul(
                        zb[hd][:, a:bnd], lhsT=negMinc, rhs=L_sb[hd][:, a:bnd], start=False, stop=False,
                        skip_group_check=True), r=[t_Lp[hd][h2], t_c], w=[t_zp[hd][h2]])
                    p.add("pe", lambda e, hd=hd, a=a, bnd=bnd: e.matmul(
                        zb[hd][:, a:bnd], lhsT=ones_b[rows[hd], :], rhs=acc_hl[rows[hd], a:bnd], start=False, stop=True,
                        skip_group_check=True), r=[t_hlm, t_c], w=[t_zp[hd][h2]])
            for hd in range(2):
                for (a, bnd) in pcs:
                    h2 = a // 512
                    p.add("act", lambda e, hd=hd, a=a, bnd=bnd: e.activation(
                        out=W_sb[hd][:, a:bnd], in_=zb[hd][:, a:bnd], func=AF.Exp),
                        r=[t_zp[hd][h2]], w=[t_Wp[hd][h2]])
            for (a, bnd) in pcs:
                for hd in range(2):
                    p.add("pe", lambda e, hd=hd, a=a, bnd=bnd: e.matmul(
                        accb[rows[hd], a:bnd], lhsT=negones[:, 0:2], rhs=L_sb[hd][:, a:bnd], start=False, stop=False,
                        skip_group_check=True), r=[t_Lp[hd][a // 512], t_c], w=[t_accm])
            if j > 0:
                p.add("dve", lambda e, lo=lo: e.tensor_copy(out=hi_t[0:34, lo:NQ], in_=accb[0:34, lo:NQ]),
                      r=[t_accm], w=[t_him])
                p.add("dve", lambda e, lo=lo: e.scalar_tensor_tensor(
                    out=acc_hl[0:34, lo:NQ], in0=hi_t[0:34, lo:NQ], scalar=negm[0:34, 0:1],
                    in1=accb[0:34, lo:NQ], op0=ALU.mult, op1=ALU.add), r=[t_him, t_accm, t_c], w=[t_hlm])
                for hd in range(2):
                    emit_z(j - 1, hd)
            for (a, bnd) in pcs:
                for hd in range(2):
                    p.add("pe", lambda e, hd=hd, a=a, bnd=bnd, j=j: e.matmul(
                        ob[hd * 64:(hd + 1) * 64, a:bnd], lhsT=V[:, j, hd * 64:(hd + 1) * 64], rhs=W_sb[hd][:, a:bnd],
                        start=False, stop=False, skip_group_check=True), r=[t_Wp[hd][a // 512], t_v], w=[t_o])
        ob_i = qg % 2
        p.add("dve", lambda e, ob_i=ob_i: e.tensor_copy(out=obuf[ob_i][:, :], in_=ob[:, :]), r=[t_o], w=[t_ob[ob_i]])
        cx.dma(oT[:, qcol0:qcol0 + NQ], obuf[ob_i][:, :], r=[t_ob[ob_i]])
    p.emit()
    return nc


def _attn_consts():
    k = np.arange(128)
    neg_minc = np.where(k[:, None] >= k[None, :], -1.0, 0.0)
    diag = np.where(k[:, None] >= k[None, :], -30000.0, 0.0)
    cst = np.concatenate([neg_minc, diag, np.ones((128, 128)), np.eye(128), np.zeros((128, 512))], axis=1)
    negm = np.zeros((64, 1), np.float32)
    negm[1, 0] = -1.0
    negm[33, 0] = -1.0
    return np.ascontiguousarray(cst.astype(ml_dtypes.bfloat16)), negm


ATT_CST, ATT_NEGM = _attn_consts()


_PROGS = {}


def _prog(key, builder):
    if key not in _PROGS:
        _PROGS[key] = builder()
    return _PROGS[key]


def _run(nc, in_maps):
    return run_bass_kernel_spmd(nc, in_maps, core_ids=list(range(NCORES))).results


def kernel(x, ssm_w_in, ssm_conv_w, ssm_conv_b, ssm_dt_bias, ssm_a_log, ssm_d, ssm_norm_w, ssm_w_out,
           sb_w_k, sb_w_v, sb_w_q, sb_w_o, mlp_w1, mlp_w2, ln_mix_g, ln_mix_b, ln_mlp_g, ln_mlp_b):
    f32 = lambda a: np.ascontiguousarray(np.asarray(a, dtype=np.float32))
    h = f32(x)[0]
    ident = np.eye(128, dtype=np.float32)
    kv_shared = None
    for layer in range(DEPTH):
        hT = np.ascontiguousarray(h.T)
        if layer < 2:
            nc = _prog("ssd", build_ssd)
            ins = [ssd_inputs(hT, f32(ssm_w_in[layer]), f32(ssm_conv_w[layer]), f32(ssm_conv_b[layer]),
                              f32(ssm_dt_bias[layer]), f32(ssm_a_log[layer]), f32(ssm_d[layer]),
                              f32(ssm_norm_w[layer]), g) for g in range(NCORES)]
            res = _run(nc, ins)
            Y = np.concatenate([np.asarray(res[g]["yn"]) for g in range(NCORES)], axis=1)
            yTs = [np.ascontiguousarray(Y[c * TOK:(c + 1) * TOK].T) for c in range(NCORES)]
            w_o, kin = f32(ssm_w_out[layer]), D_INNER
        else:
            j = layer - 2
            same = layer == 2
            nc = _prog(("attn", same), lambda: build_attn(SEQ, same_kv=same))
            ins = []
            for c in range(NCORES):
                sl = slice(c * 128, (c + 1) * 128)
                wqkv = np.concatenate([f32(sb_w_q[j][:, sl]), f32(sb_w_k[:, sl]), f32(sb_w_v[:, sl])], axis=1)
                d = {"hTq": hT, "wqkv": np.ascontiguousarray(wqkv), "acst": ATT_CST, "negm": ATT_NEGM}
                if not same:
                    d["kT_in"], d["V_in"] = kv_shared[c]
                ins.append(d)
            res = _run(nc, ins)
            if same:
                kv_shared = [(np.asarray(res[c]["kT_out"]), np.asarray(res[c]["V_out"])) for c in range(NCORES)]
            OT = np.concatenate([np.asarray(res[c]["oT"]) for c in range(NCORES)], axis=0)
            yTs = [np.ascontiguousarray(OT[:, c * TOK:(c + 1) * TOK]) for c in range(NCORES)]
            w_o, kin = f32(sb_w_o[j]), D
        nc = _prog(("t", kin), lambda: build_tphase(kin))
        lnp = np.stack([f32(ln_mix_g[layer]), f32(ln_mix_b[layer]), f32(ln_mlp_g[layer]), f32(ln_mlp_b[layer])])
        lnp = np.ascontiguousarray(np.broadcast_to(lnp[None], (128, 4, D)))
        w1, w2 = f32(mlp_w1[layer]), f32(mlp_w2[layer])
        ins = [{"yT": yTs[c], "h": np.ascontiguousarray(h[c * TOK:(c + 1) * TOK]), "w_o": w_o, "w1": w1, "w2": w2,
                "lnp": lnp, "ident": ident} for c in range(NCORES)]
        res = _run(nc, ins)
        h = np.concatenate([np.asarray(res[c]["h_out"]) for c in range(NCORES)], axis=0)
    return h[None].astype(np.float32)
```

```python
import math
import numpy as np
import ml_dtypes
import concourse.bass as bass
import concourse.mybir as mybir
from concourse.bass_utils import run_bass_kernel_spmd

F32 = mybir.dt.float32
BF16 = mybir.dt.bfloat16
AF = mybir.ActivationFunctionType
ALU = mybir.AluOpType
AX = mybir.AxisListType

NCORES = 8
SEQ = 16384
D = 1024
DEPTH = 4
TOK = SEQ // NCORES
ALPHA = (2 * DEPTH) ** 0.25
LN_EPS = 1e-5
RMS_EPS = 1e-5
D_INNER = 2048
D_FF = 4096
NBLK = SEQ // 128

ENGS = ("pe", "act", "dve", "pool", "sp")
NRING = 6


class Tok:
    __slots__ = ("name", "w", "rs", "rd", "excl")

    def __init__(self, name="", excl=False):
        self.name = name
        self.w = None
        self.rs = {}
        self.rd = []
        self.excl = excl


class Op:
    __slots__ = ("eng", "fn", "deps", "dma", "sig", "sem", "val", "prev")


class Prog:
    def __init__(self, nc):
        self.nc = nc
        self.ops = {e: [] for e in ENGS}

    def add(self, eng, fn, r=(), w=(), dma=False):
        op = Op()
        op.eng, op.fn, op.dma, op.sig, op.sem, op.val, op.prev = eng, fn, dma, False, None, 0, 0
        deps = set()
        for t in r:
            if t.w is not None:
                deps.add(t.w)
            if t.excl:
                deps.update(o for en, o in t.rs.items() if en != eng)
        for t in w:
            if t.w is not None:
                deps.add(t.w)
            deps.update(t.rs.values())
            deps.update(t.rd)
        if eng == "pe" and not dma:
            deps = {d for d in deps if d.dma or d.eng != "pe"}
        deps.discard(op)
        op.deps = deps
        for t in r:
            if dma:
                t.rd.append(op)
            else:
                t.rs[eng] = op
        for t in w:
            t.w = op
            t.rs = {}
            t.rd = []
        self.ops[eng].append(op)
        return op

    def emit(self):
        nc = self.nc
        for e in ENGS:
            for op in self.ops[e]:
                for d in op.deps:
                    d.sig = True
        csem = {e: nc.alloc_semaphore(name=f"c_{e}") for e in ENGS if e != "sp"}
        ring = {e: [nc.alloc_semaphore(name=f"r_{e}{i}") for i in range(NRING)] for e in ENGS}
        final = {}
        for e in ENGS:
            cnt = 0
            nd = 0
            for op in self.ops[e]:
                if op.dma:
                    op.sem = ring[e][nd % NRING]
                    op.prev = 16 * (nd // NRING)
                    op.val = op.prev + 16
                    final[op.sem] = op.val
                    nd += 1
                elif op.sig:
                    cnt += 1
                    op.sem = csem[e]
                    op.val = cnt
        engobj = {"pe": "tensor", "act": "scalar", "dve": "vector", "pool": "gpsimd", "sp": "sync"}
        ops = self.ops

        def run(e, eng):
            waited = {}
            for op in ops[e]:
                needs = {}
                for d in op.deps:
                    if needs.get(d.sem, 0) < d.val:
                        needs[d.sem] = d.val
                if op.dma and op.prev > 0 and needs.get(op.sem, 0) < op.prev:
                    needs[op.sem] = op.prev
                for sem, val in needs.items():
                    if waited.get(sem, 0) < val:
                        eng.wait_ge(sem, val)
                        waited[sem] = val
                ins = op.fn(eng)
                if op.dma:
                    ins.then_inc(op.sem, 16)
                elif op.sig:
                    ins.then_inc(op.sem, 1)
            if e == "sp":
                for sem, val in final.items():
                    if waited.get(sem, 0) < val:
                        eng.wait_ge(sem, val)

        with nc.Block() as block:
            @block.tensor
            def _(eng):
                run("pe", eng)

            @block.scalar
            def _(eng):
                run("act", eng)

            @block.vector
            def _(eng):
                run("dve", eng)

            @block.gpsimd
            def _(eng):
                run("pool", eng)

            @block.sync
            def _(eng):
                run("sp", eng)


class PProxy:
    def __init__(self, real):
        self.real = real
        self.rec = None

    def add(self, *a, **k):
        if self.rec is not None:
            self.rec.append((a, k))
            return None
        return self.real.add(*a, **k)

    def record(self, fn, *args):
        self.rec = []
        fn(*args)
        items, self.rec = self.rec, None
        return items

    def merge(self, *lists):
        total = max(len(l) for l in lists)
        pos = [0] * len(lists)
        for step in range(1, total + 1):
            for i, l in enumerate(lists):
                upto = (len(l) * step) // total
                while pos[i] < upto:
                    a, k = l[pos[i]]
                    self.real.add(*a, **k)
                    pos[i] += 1

    def emit(self):
        self.real.emit()


class Ctx:
    def __init__(self, nbanks=8):
        self.nc = bass.Bass("TRN2", target_bir_lowering=False)
        self.p = Prog(self.nc)
        self.nb = nbanks
        self.ps = self.nc.alloc_psum_tensor("psum_all", [128, nbanks, 512], F32)
        self.ps_tok = [Tok(f"ps{i}", excl=True) for i in range(nbanks)]
        self.ps_next = 0
        self.nbuf = 0

    def dbuf(self, name, shape, dt, n=2):
        bufs = [self.nc.alloc_sbuf_tensor(f"{name}_{i}", shape, dt) for i in range(n)]
        toks = [Tok(f"{name}_{i}") for i in range(n)]
        return bufs, toks

    def sb(self, name, shape, dt):
        return self.nc.alloc_sbuf_tensor(name, shape, dt)

    def bank(self):
        b = self.ps_next
        self.ps_next = (b + 1) % self.nb
        return self.ps[:, b, :], self.ps_tok[b]

    def dram_in(self, name, shape, dt):
        return self.nc.dram_tensor(name, list(shape), dt, kind="ExternalInput").ap()

    def dram_out(self, name, shape, dt):
        return self.nc.dram_tensor(name, list(shape), dt, kind="ExternalOutput").ap()

    def dma(self, out, in_, r=(), w=(), eng=None):
        if eng is None:
            eng = "sp"
        return self.p.add(eng, lambda e, o=out, i=in_: e.dma_start(out=o, in_=i), r=r, w=w, dma=True)

    def dma_cast(self, out, in_, r=(), w=()):
        return self.p.add("pool", lambda e, o=out, i=in_: e.dma_start(out=o, in_=i), r=r, w=w, dma=True)


def load_cast_rows(cx, dst, src, nk, ncols, wtok, chunk=2048):
    v = src.rearrange("(k p) n -> p k n", p=128)
    for k in range(nk):
        for c0 in range(0, ncols, chunk):
            c1 = min(ncols, c0 + chunk)
            cx.dma_cast(dst[:, k, c0:c1], v[:, k, c0:c1], w=[wtok])


def layer_norm_tile(cx, r_sb, r_tok, out_sb, out_tok, g_sb, b_sb, eps_sb, scr, gtok):
    p = cx.p
    st, mv, rstd, st_tok = scr
    for c in range(2):
        p.add("dve", lambda e, c=c: e.bn_stats(out=st[:, c, :], in_=r_sb[:, c * 512:(c + 1) * 512]),
              r=[r_tok], w=[st_tok])
    p.add("dve", lambda e: e.bn_aggr(out=mv[:, :], in_=st[:, :, :].rearrange("p a b -> p (a b)")), r=[st_tok], w=[st_tok])
    p.add("act", lambda e: e.activation(out=rstd[:, :], in_=mv[:, 1:2], func=AF.Sqrt, bias=eps_sb[:, 0:1], scale=1.0),
          r=[st_tok], w=[st_tok])
    p.add("dve", lambda e: e.reciprocal(out=rstd[:, :], in_=rstd[:, :]), r=[st_tok], w=[st_tok])
    p.add("dve", lambda e: e.tensor_scalar(out=out_sb, in0=r_sb, scalar1=mv[:, 0:1], scalar2=rstd[:, 0:1],
                                           op0=ALU.subtract, op1=ALU.mult), r=[r_tok, st_tok], w=[out_tok])
    p.add("pool", lambda e: e.tensor_tensor(out=out_sb, in0=out_sb, in1=g_sb, op=ALU.mult), r=[out_tok, gtok], w=[out_tok])
    p.add("pool", lambda e: e.tensor_tensor(out=out_sb, in0=out_sb, in1=b_sb, op=ALU.add), r=[out_tok, gtok], w=[out_tok])


def build_tphase(kin):
    cx = Ctx()
    cx.p = PProxy(cx.p)
    nc, p = cx.nc, cx.p
    ctr = {"a": 0, "b": 0}

    def abank():
        i = ctr["a"] % 2
        ctr["a"] += 1
        return cx.ps[:, i, :], cx.ps_tok[i]

    def bbank():
        i = 2 + ctr["b"] % 6
        ctr["b"] += 1
        return cx.ps[:, i, :], cx.ps_tok[i]

    KC = kin // 128
    G = 128
    CPB = 512 // G
    NG = TOK // G
    yT = cx.dram_in("yT", [kin, TOK], BF16)
    h_in = cx.dram_in("h", [TOK, D], F32)
    w_o = cx.dram_in("w_o", [kin, D], F32)
    w1 = cx.dram_in("w1", [D, D_FF], F32)
    w2 = cx.dram_in("w2", [D_FF, D], F32)
    lnp = cx.dram_in("lnp", [128, 4, D], F32)
    h_out = cx.dram_out("h_out", [TOK, D], F32)

    wo_sb = cx.sb("wo_sb", [128, KC, D], BF16)
    w1_sb = cx.sb("w1_sb", [128, 8, D_FF], BF16)
    w2_sb = cx.sb("w2_sb", [128, 32, D], BF16)
    lnp_sb = cx.sb("lnp_sb", [128, 4, D], F32)
    ident = cx.sb("ident_sb", [128, 128], F32)
    eps_sb = cx.sb("eps_sb", [128, 1], F32)
    t_wo, t_w1, t_w2, t_lnp, t_const = Tok("wo"), Tok("w1"), Tok("w2"), Tok("lnp"), Tok("const")

    ident_d = cx.dram_in("ident", [128, 128], F32)
    cx.dma(ident[:, :], ident_d[:, :], w=[t_const])
    p.add("pool", lambda e: e.memset(eps_sb[:, :], LN_EPS), w=[t_const])
    cx.dma(lnp_sb[:, :, :], lnp[:, :, :], w=[t_lnp])

    yT_sb = cx.sb("yT_sb", [128, KC, G], BF16)
    h_sb = cx.sb("h_sb", [128, D], F32)
    t_yT, t_h = Tok("yT"), Tok("h")
    rbuf = [cx.sb(f"rbuf{i}", [128, D], F32) for i in range(2)]
    t_rb = [Tok("rb0"), Tok("rb1")]
    h1Ts = [cx.sb(f"h1T{i}", [128, 8, G], BF16) for i in range(2)]
    t_h1Ts = [Tok("h1T0"), Tok("h1T1")]
    uT = cx.sb("uT", [128, 32, G], BF16)
    t_uT = [Tok(f"uT{i}") for i in range(32 // CPB)]
    utmp = [cx.sb("utmp0", [128, CPB * G], F32)] * 2
    t_utmp = [Tok("utmp0")] * 2
    st = cx.sb("ln_st", [128, 2, 6], F32)
    mv = cx.sb("ln_mv", [128, 2], F32)
    rstd = cx.sb("ln_rstd", [128, 1], F32)
    scr = (st, mv, rstd, Tok("lnscr"))
    scrB = (cx.sb("ln_stB", [128, 2, 6], F32), cx.sb("ln_mvB", [128, 2], F32), cx.sb("ln_rstdB", [128, 1], F32), Tok("lnscrB"))

    yT_v = yT.rearrange("(k p) t -> p k t", p=128)
    h_v = h_in.rearrange("(n p) d -> p n d", p=128)
    ho_v = h_out.rearrange("(n p) d -> p n d", p=128)

    def load_group(g):
        cx.dma(yT_sb[:, :, :], yT_v[:, :, g * G:(g + 1) * G], w=[t_yT])
        cx.dma(h_sb[:, :], h_v[:, g, :], w=[t_h])

    load_group(0)
    load_cast_rows(cx, wo_sb, w_o, KC, D, t_wo)
    load_cast_rows(cx, w1_sb, w1, 8, D_FF, t_w1)
    load_cast_rows(cx, w2_sb, w2, 32, D, t_w2)

    def stageA(g):
        rb, trb = rbuf[g % 2], t_rb[g % 2]
        h1T, t_h1T = h1Ts[g % 2], t_h1Ts[g % 2]
        for n in range(2):
            bank, bt = abank()
            for k in range(KC):
                p.add("pe", lambda e, bank=bank, k=k, n=n: e.matmul(
                    bank, lhsT=yT_sb[:, k, :], rhs=wo_sb[:, k, n * 512:(n + 1) * 512],
                    start=(k == 0), stop=(k == KC - 1)), r=[t_yT, t_wo], w=[bt])
            p.add("dve", lambda e, bank=bank, n=n, rb=rb: e.scalar_tensor_tensor(
                out=rb[:, n * 512:(n + 1) * 512], in0=h_sb[:, n * 512:(n + 1) * 512], scalar=ALPHA,
                in1=bank, op0=ALU.mult, op1=ALU.add), r=[bt, t_h], w=[trb])
        if g + 1 < NG:
            load_group(g + 1)
        layer_norm_tile(cx, rb[:, :], trb, rb[:, :], trb, lnp_sb[:, 0, :], lnp_sb[:, 1, :], eps_sb, scr, t_lnp)
        for q in range(2):
            bank, bt = abank()
            for j in range(4):
                f = q * 4 + j
                p.add("pe", lambda e, bank=bank, j=j, f=f, rb=rb: e.transpose(
                    bank[:, j * 128:(j + 1) * 128], rb[:, f * 128:(f + 1) * 128], ident[:, :]),
                    r=[trb, t_const], w=[bt])
            p.add("act", lambda e, bank=bank, q=q: e.activation(
                out=h1T[:, q * 4:(q + 1) * 4, :], in_=bank.rearrange("p (j c) -> p j c", j=4), func=AF.Copy),
                r=[bt], w=[t_h1T])
    def stageB(g):
        rb, trb = rbuf[g % 2], t_rb[g % 2]
        h1T, t_h1T = h1Ts[g % 2], t_h1Ts[g % 2]
        for cp in range(32 // CPB):
            bank, bt = bbank()
            for j in range(CPB):
                ch = cp * CPB + j
                for k in range(8):
                    p.add("pe", lambda e, bank=bank, j=j, ch=ch, k=k: e.matmul(
                        bank[:, j * G:(j + 1) * G], lhsT=w1_sb[:, k, ch * 128:(ch + 1) * 128], rhs=h1T[:, k, :],
                        start=(k == 0), stop=(k == 7)), r=[t_w1, t_h1T], w=[bt])
            ub = cp % 2
            p.add("act", lambda e, bank=bank, ub=ub: e.activation(out=utmp[ub][:, :], in_=bank[:, 0:CPB * G], func=AF.Relu),
                  r=[bt], w=[t_utmp[ub]])
            p.add("pool", lambda e, cp=cp, ub=ub: e.tensor_tensor(
                out=uT[:, CPB * cp:CPB * cp + CPB, :], in0=utmp[ub][:, :].rearrange("p (j c) -> p j c", j=CPB),
                in1=utmp[ub][:, :].rearrange("p (j c) -> p j c", j=CPB), op=ALU.mult), r=[t_utmp[ub]], w=[t_uT[cp]])
        for n in range(2):
            bank, bt = bbank()
            for k in range(32):
                p.add("pe", lambda e, bank=bank, k=k, n=n: e.matmul(
                    bank, lhsT=uT[:, k, :], rhs=w2_sb[:, k, n * 512:(n + 1) * 512],
                    start=(k == 0), stop=(k == 31)), r=[t_uT[k // CPB], t_w2], w=[bt])
            p.add("dve", lambda e, bank=bank, n=n, rb=rb: e.scalar_tensor_tensor(
                out=rb[:, n * 512:(n + 1) * 512], in0=rb[:, n * 512:(n + 1) * 512], scalar=ALPHA,
                in1=bank, op0=ALU.mult, op1=ALU.add), r=[bt, trb], w=[trb])
        layer_norm_tile(cx, rb[:, :], trb, rb[:, :], trb, lnp_sb[:, 2, :], lnp_sb[:, 3, :], eps_sb, scrB, t_lnp)
        cx.dma(ho_v[:, g, :], rb[:, :], r=[trb])
    stageA(0)
    for g in range(NG):
        lists = []
        if g + 1 < NG:
            lists.append(p.record(stageA, g + 1))
        lists.append(p.record(stageB, g))
        p.merge(*lists)
    p.emit()
    return nc


def build_ssd(nblocks=SEQ // 512):
    cx = Ctx(nbanks=5)
    cx.p = PProxy(cx.p)
    pool_ctr = {"t": 0, "b": 0}

    def tbank():
        i = pool_ctr["t"] % 3
        pool_ctr["t"] += 1
        return cx.ps[:, i, :], cx.ps_tok[i]

    def bbank():
        i = 3 + pool_ctr["b"] % 2
        pool_ctr["b"] += 1
        return cx.ps[:, i, :], cx.ps_tok[i]

    nc, p = cx.nc, cx.p
    BLK = 512
    seq = nblocks * BLK
    psB = nc.alloc_psum_tensor("psum_bf", [128, 1024], BF16)
    t_psB = Tok("psB", excl=True)
    psYS = nc.alloc_psum_tensor("psum_ys", [128, 2, 512], F32)
    t_YS = [Tok("YS0", excl=True), Tok("YS1", excl=True)]

    hT = cx.dram_in("hT", [D, seq], F32)
    wfm = cx.dram_in("wfm", [D, 512], F32)
    wtm = cx.dram_in("wtm", [D, 260], F32)
    cw = cx.dram_in("cw", [128, 4, 4], F32)
    cb = cx.dram_in("cb", [128, 4], F32)
    hp = cx.dram_in("hp", [128, 3, 4], F32)
    nw = cx.dram_in("nw", [128, 256], F32)
    cst_f = cx.dram_in("cst_f", [128, 2, 128], F32)
    cst_b = cx.dram_in("cst_b", [128, 128 + 512], BF16)
    yn = cx.dram_out("yn", [seq, 256], BF16)

    wfm_sb = cx.sb("wfm_sb", [128, 8, 512], BF16)
    wtm_sb = cx.sb("wtm_sb", [128, 8, 260], BF16)
    cw_sb = cx.sb("cw_sb", [128, 4, 4], F32)
    cb_sb = cx.sb("cb_sb", [128, 4], F32)
    hp_sb = cx.sb("hp_sb", [128, 3, 4], F32)
    nw_sb = cx.sb("nw_sb", [128, 256], F32)
    cf_sb = cx.sb("cf_sb", [128, 2, 128], F32)
    cbf_sb = cx.sb("cbf_sb", [128, 640], BF16)
    aneg = cx.sb("aneg", [128, 4], F32)
    onecol = cx.sb("onecol", [128, 1], F32)
    neghalf = cx.sb("neghalf", [128, 1], F32)
    S = cx.sb("S_state", [128, 256], F32)
    S_bf = cx.sb("S_bf", [128, 256], BF16)
    t_w, t_c, t_S, t_Sbf = Tok("w"), Tok("c"), Tok("S"), Tok("Sbf")
    triu, ones_f = cf_sb[:, 0, :], cf_sb[:, 1, :]
    ident_b, negmask = cbf_sb[:, 0:128], cbf_sb[:, 128:640]

    for dst, src in ((cw_sb, cw), (cb_sb, cb), (hp_sb, hp), (nw_sb, nw), (cf_sb, cst_f), (cbf_sb, cst_b)):
        nd = len(dst.shape)
        idx = tuple([slice(None)] * nd)
        cx.dma(dst[idx], src[idx], w=[t_c])
    load_cast_rows(cx, wfm_sb, wfm, 8, 512, t_w)
    load_cast_rows(cx, wtm_sb, wtm, 8, 260, t_w)
    p.add("pool", lambda e: e.memset(onecol[:, :], 1.0), w=[t_c])
    p.add("pool", lambda e: e.memset(neghalf[:, :], -0.5), w=[t_c])
    p.add("pool", lambda e: e.memset(S[:, :], 0.0), w=[t_S])
    p.add("pool", lambda e: e.memset(S_bf[:, :], 0.0), w=[t_Sbf])
    p.add("act", lambda e: e.activation(out=aneg[:, :], in_=hp_sb[:, 1, :], func=AF.Exp), r=[t_c], w=[t_c])
    p.add("dve", lambda e: e.tensor_scalar(out=aneg[:, :], in0=aneg[:, :], scalar1=-1.0, scalar2=None, op0=ALU.mult),
          r=[t_c], w=[t_c])

    hTf, t_hTf = cx.dbuf("hTf", [128, 8, BLK], F32)
    hTb, t_hTb = cx.dbuf("hTb", [128, 8, BLK], BF16, n=3)
    xpre, t_xpre = cx.dbuf("xpre", [128, 4, 3 + BLK], F32)
    acc, t_acc = cx.dbuf("cacc", [128, BLK], F32)
    xbcT, t_xbcT = cx.dbuf("xbcT", [128, 4, BLK], BF16)
    zs, t_zs = cx.dbuf("zs", [128, 4, 256], F32)
    dtr, t_dtr = cx.dbuf("dtr", [128, 4, 4], F32)
    dte, t_dte = cx.dbuf("dte", [128, 4, 4], F32)
    dts, t_dts = cx.dbuf("dts", [128, 4, 4], F32)
    das, t_das = cx.dbuf("das", [128, 4, 4], F32)
    xdt, t_xdt = cx.dbuf("xdt", [128, 256], BF16)
    ysk, t_ysk = cx.dbuf("ysk", [128, 256], F32)
    Btm, t_Btm = cx.dbuf("Btm", [128, 128], BF16)
    daT, t_daT = cx.dbuf("daT", [128, 4, 128], F32)
    nacs, t_nacs = cx.dbuf("nacs", [128, 4], F32)
    e_sb, t_e = cx.dbuf("e_sb", [128, 4], F32)
    w_sb, t_wd = cx.dbuf("w_sb", [128, 4], F32)
    dec, t_dec = cx.dbuf("dec", [128, 4], F32)
    Dm, t_Dm = cx.dbuf("Dm", [128, 4, 128], F32)
    Gm, t_G = cx.dbuf("Gm", [128, 4, 128], BF16)
    yc, t_yc = cx.dbuf("yc", [128, 256], F32)
    sq, t_sq = cx.dbuf("sq", [128, 256], F32)
    ss, t_ss = cx.dbuf("ss", [128, 2], F32)
    yno, t_yno = cx.dbuf("yno", [128, 256], BF16)
    xdw, t_xdw = cx.dbuf("xdw", [128, 256], BF16)

    hT_v = hT.rearrange("(k p) t -> p k t", p=128)
    p.add("pool", lambda e: e.memset(xpre[1][:, :, :], 0.0), w=[t_xpre[1]])

    def load_block(b):
        fb, hb = b % 2, b % 3
        for half in range(2):
            cx.dma(hTf[fb][:, half * 4:(half + 1) * 4, :], hT_v[:, half * 4:(half + 1) * 4, b * BLK:(b + 1) * BLK],
                   w=[t_hTf[fb]])
        p.add("act", lambda e, fb=fb, hb=hb: e.activation(out=hTb[hb][:, 0:4, :], in_=hTf[fb][:, 0:4, :], func=AF.Copy),
              r=[t_hTf[fb]], w=[t_hTb[hb]])
        p.add("dve", lambda e, fb=fb, hb=hb: e.tensor_copy(out=hTb[hb][:, 4:8, :], in_=hTf[fb][:, 4:8, :]),
              r=[t_hTf[fb]], w=[t_hTb[hb]])

    load_block(0)
    if nblocks > 1:
        load_block(1)
    bc4 = lambda ap: ap.unsqueeze(2).broadcast_to([128, 4, 64])
    v464 = lambda ap: ap.rearrange("p (h c) -> p h c", h=4)
    def blockpro(b):
        pb = b % 2
        hb = b % 3
        if b + 2 < nblocks:
            load_block(b + 2)
        p.add("pool", lambda e, pb=pb: e.tensor_copy(out=xpre[pb][:, :, 0:3], in_=xpre[1 - pb][:, :, BLK:BLK + 3]),
              r=[t_xpre[1 - pb]], w=[t_xpre[pb]])
        for ct in range(4):
            bank, bt = bbank()
            for k in range(8):
                p.add("pe", lambda e, bank=bank, k=k, ct=ct, hb=hb: e.matmul(
                    bank, lhsT=wfm_sb[:, k, ct * 128:(ct + 1) * 128], rhs=hTb[hb][:, k, :],
                    start=(k == 0), stop=(k == 7)), r=[t_w, t_hTb[hb]], w=[bt])
            p.add("act", lambda e, bank=bank, ct=ct, pb=pb: e.activation(
                out=xpre[pb][:, ct, 3:3 + BLK], in_=bank, func=AF.Copy), r=[bt], w=[t_xpre[pb]])
        for ct in range(4):
            a = ct % 2
            p.add("act", lambda e, ct=ct, pb=pb, a=a: e.activation(
                out=acc[a][:, :], in_=xpre[pb][:, ct, 0:BLK], func=AF.Identity,
                scale=cw_sb[:, ct, 0:1], bias=cb_sb[:, ct:ct + 1]), r=[t_xpre[pb], t_c], w=[t_acc[a]])
            for k in range(1, 4):
                p.add("dve", lambda e, ct=ct, pb=pb, a=a, k=k: e.scalar_tensor_tensor(
                    out=acc[a][:, :], in0=xpre[pb][:, ct, k:k + BLK], scalar=cw_sb[:, ct, k:k + 1], in1=acc[a][:, :],
                    op0=ALU.mult, op1=ALU.add), r=[t_xpre[pb], t_acc[a], t_c], w=[t_acc[a]])
            p.add("act", lambda e, ct=ct, pb=pb, a=a: e.activation(
                out=xbcT[pb][:, ct, :], in_=acc[a][:, :], func=AF.Silu), r=[t_acc[a]], w=[t_xbcT[pb]])
        for t in range(4):
            bank, bt = bbank()
            for k in range(8):
                p.add("pe", lambda e, bank=bank, k=k, t=t, hb=hb: e.matmul(
                    bank[:, 0:260], lhsT=hTb[hb][:, k, t * 128:(t + 1) * 128], rhs=wtm_sb[:, k, :],
                    start=(k == 0), stop=(k == 7)), r=[t_w, t_hTb[hb]], w=[bt])
            p.add("act", lambda e, bank=bank, t=t, pb=pb: e.activation(
                out=zs[pb][:, t, :], in_=bank[:, 0:256], func=AF.Silu), r=[bt], w=[t_zs[pb]])
            p.add("dve", lambda e, bank=bank, t=t, pb=pb: e.tensor_tensor(
                out=dtr[pb][:, t, :], in0=bank[:, 256:260], in1=hp_sb[:, 0, :], op=ALU.add),
                r=[bt, t_c], w=[t_dtr[pb]])
        p.add("act", lambda e, pb=pb: e.activation(out=dte[pb][:, :, :], in_=dtr[pb][:, :, :], func=AF.Exp),
              r=[t_dtr[pb]], w=[t_dte[pb]])
        p.add("act", lambda e, pb=pb: e.activation(out=dts[pb][:, :, :], in_=dte[pb][:, :, :], func=AF.Ln,
                                                   bias=onecol[:, 0:1], scale=1.0), r=[t_dte[pb], t_c], w=[t_dts[pb]])
        p.add("dve", lambda e, pb=pb: e.tensor_tensor(
            out=das[pb][:, :, :], in0=dts[pb][:, :, :], in1=aneg[:, :].unsqueeze(1).broadcast_to([128, 4, 4]),
            op=ALU.mult), r=[t_dts[pb], t_c], w=[t_das[pb]])
    def front(n):
        if True:
            b, t = divmod(n, 4)
            pb, q, c0 = b % 2, n % 2, t * 128
            for j in range(3):
                p.add("pe", lambda e, j=j, pb=pb, c0=c0: e.transpose(
                    psB[:, j * 128:(j + 1) * 128], xbcT[pb][:, j, c0:c0 + 128], ident_b),
                    r=[t_xbcT[pb], t_c], w=[t_psB])
            p.add("dve", lambda e, q=q, pb=pb, t=t: e.tensor_tensor(
                out=v464(xdt[q][:, :]), in0=v464(psB[:, 0:256]), in1=bc4(dts[pb][:, t, :]), op=ALU.mult),
                r=[t_psB, t_dts[pb]], w=[t_xdt[q]])
            p.add("dve", lambda e, q=q: e.tensor_tensor(
                out=v464(ysk[q][:, :]), in0=v464(psB[:, 0:256]), in1=bc4(hp_sb[:, 2, :]), op=ALU.mult),
                r=[t_psB, t_c], w=[t_ysk[q]])
            p.add("act", lambda e, q=q: e.activation(out=Btm[q][:, :], in_=psB[:, 256:384], func=AF.Copy),
                  r=[t_psB], w=[t_Btm[q]])
            bankA, btA = tbank()
            p.add("pe", lambda e, bankA=bankA, pb=pb, t=t: e.matmul(
                bankA[:, 0:4], lhsT=triu, rhs=das[pb][:, t, :], start=True, stop=True), r=[t_c, t_das[pb]], w=[btA])
            p.add("dve", lambda e, q=q, pb=pb, t=t: e.tensor_tensor(
                out=daT[q][:, :, :], in0=triu.unsqueeze(1).broadcast_to([128, 4, 128]),
                in1=das[pb][:, t, :].unsqueeze(2).broadcast_to([128, 4, 128]), op=ALU.mult),
                r=[t_c, t_das[pb]], w=[t_daT[q]])
            bankR, btR = tbank()
            p.add("pe", lambda e, bankR=bankR, q=q: e.matmul(
                bankR, lhsT=ones_f, rhs=daT[q][:, :, :].rearrange("p h l -> p (h l)"), start=True, stop=False),
                r=[t_c, t_daT[q]], w=[btR])
            p.add("pe", lambda e, bankR=bankR: e.matmul(bankR, lhsT=ident_b, rhs=negmask, start=False, stop=True),
                  r=[t_c], w=[btR])
            p.add("act", lambda e, bankA=bankA, q=q: e.activation(
                out=nacs[q][:, :], in_=bankA[:, 0:4], func=AF.Identity, scale=-1.0), r=[btA], w=[t_nacs[q]])
            p.add("act", lambda e, bankA=bankA, q=q: e.activation(out=e_sb[q][:, :], in_=bankA[:, 0:4], func=AF.Exp),
                  r=[btA], w=[t_e[q]])
            Rv = bankR.rearrange("p (h l) -> p h l", h=4)
            for h in range(4):
                p.add("act", lambda e, bankR=bankR, q=q, h=h: e.activation(
                    out=Dm[q][:, h, :], in_=bankR[:, h * 128:(h + 1) * 128], func=AF.Exp,
                    bias=nacs[q][:, h:h + 1], scale=1.0), r=[btR, t_nacs[q]], w=[t_Dm[q]])
            p.add("dve", lambda e, Rv=Rv, q=q: e.tensor_tensor(
                out=w_sb[q][:, :], in0=Rv[:, :, 127], in1=nacs[q][:, :], op=ALU.add),
                r=[btR, t_nacs[q]], w=[t_wd[q]])
            p.add("act", lambda e, q=q: e.activation(out=w_sb[q][:, :], in_=w_sb[q][:, :], func=AF.Exp),
                  r=[t_wd[q]], w=[t_wd[q]])
            p.add("act", lambda e, Rv=Rv, q=q: e.activation(out=dec[q][:, :], in_=Rv[:, :, 127], func=AF.Exp),
                  r=[btR], w=[t_dec[q]])
            bankC, btC = bankA[:, 128:256], btA
            p.add("pe", lambda e, bankC=bankC, pb=pb, c0=c0: e.matmul(
                bankC, lhsT=xbcT[pb][:, 2, c0:c0 + 128], rhs=xbcT[pb][:, 3, c0:c0 + 128],
                start=True, stop=True), r=[t_xbcT[pb]], w=[btC])
            p.add("dve", lambda e, bankC=bankC, q=q: e.tensor_tensor(
                out=Gm[q][:, :, :], in0=Dm[q][:, :, :], in1=bankC.unsqueeze(1).broadcast_to([128, 4, 128]),
                op=ALU.mult), r=[btC, t_Dm[q]], w=[t_G[q]])
            bankY, btY = psYS[:, q, :], t_YS[q]
            for h in range(4):
                p.add("pe", lambda e, bankY=bankY, q=q, h=h: e.matmul(
                    bankY[:, h * 64:(h + 1) * 64], lhsT=Gm[q][:, h, :], rhs=xdt[q][:, h * 64:(h + 1) * 64],
                    start=True, stop=True), r=[t_G[q], t_xdt[q]], w=[btY])
            p.add("pool", lambda e, q=q: e.tensor_tensor(
                out=v464(xdw[q][:, :]), in0=v464(xdt[q][:, :]), in1=bc4(w_sb[q][:, :]), op=ALU.mult),
                r=[t_xdt[q], t_wd[q]], w=[t_xdw[q]])
            p.add("pe", lambda e, q=q: e.matmul(
                psYS[:, q, 256:512], lhsT=Btm[q][:, :], rhs=xdw[q][:, :], start=True, stop=True),
                r=[t_Btm[q], t_xdw[q]], w=[t_YS[q]])
    def back(n):
        if True:
            b, t = divmod(n, 4)
            pb, q, c0 = b % 2, n % 2, t * 128
            row0 = b * BLK + c0
            bankY, btY = psYS[:, q, :], t_YS[q]
            bankO, btO = tbank()
            p.add("pe", lambda e, bankO=bankO, pb=pb, c0=c0: e.matmul(
                bankO[:, 0:256], lhsT=xbcT[pb][:, 3, c0:c0 + 128], rhs=S_bf[:, :], start=True, stop=True),
                r=[t_xbcT[pb], t_Sbf], w=[btO])
            p.add("dve", lambda e, bankO=bankO, q=q: e.tensor_tensor(
                out=v464(yc[q][:, :]), in0=v464(bankO[:, 0:256]), in1=bc4(e_sb[q][:, :]), op=ALU.mult),
                r=[btO, t_e[q]], w=[t_yc[q]])
            p.add("dve", lambda e, bankY=bankY, q=q: e.tensor_tensor(
                out=yc[q][:, :], in0=yc[q][:, :], in1=bankY[:, 0:256], op=ALU.add), r=[btY, t_yc[q]], w=[t_yc[q]])
            p.add("pool", lambda e, q=q: e.tensor_tensor(out=yc[q][:, :], in0=yc[q][:, :], in1=ysk[q][:, :], op=ALU.add),
                  r=[t_yc[q], t_ysk[q]], w=[t_yc[q]])
            p.add("pool", lambda e, q=q, pb=pb, t=t: e.tensor_tensor(
                out=yc[q][:, :], in0=yc[q][:, :], in1=zs[pb][:, t, :], op=ALU.mult),
                r=[t_yc[q], t_zs[pb]], w=[t_yc[q]])
            p.add("act", lambda e, q=q: e.activation(out=sq[q][:, :], in_=yc[q][:, :], func=AF.Square,
                                                     accum_out=ss[q][:, 0:1]), r=[t_yc[q]], w=[t_sq[q], t_ss[q]])
            p.add("dve", lambda e, q=q: e.tensor_scalar(out=ss[q][:, 1:2], in0=ss[q][:, 0:1], scalar1=1.0 / 256.0,
                                                        scalar2=RMS_EPS, op0=ALU.mult, op1=ALU.add),
                  r=[t_ss[q]], w=[t_ss[q]])
            p.add("pool", lambda e, q=q: e.tensor_tensor(out=ss[q][:, 1:2], in0=ss[q][:, 1:2], in1=neghalf[:, 0:1],
                                                         op=ALU.pow), r=[t_ss[q], t_c], w=[t_ss[q]])
            p.add("dve", lambda e, q=q: e.scalar_tensor_tensor(
                out=yno[q][:, :], in0=yc[q][:, :], scalar=ss[q][:, 1:2], in1=nw_sb[:, :], op0=ALU.mult, op1=ALU.mult),
                r=[t_yc[q], t_ss[q], t_c], w=[t_yno[q]])
            cx.dma(yn[row0:row0 + 128, :], yno[q][:, :], r=[t_yno[q]])
            p.add("pool", lambda e, q=q: e.tensor_tensor(
                out=v464(S[:, :]), in0=v464(S[:, :]), in1=bc4(dec[q][:, :]), op=ALU.mult),
                r=[t_S, t_dec[q]], w=[t_S])
            p.add("dve", lambda e, q=q: e.tensor_tensor(out=S[:, :], in0=S[:, :], in1=psYS[:, q, 256:512], op=ALU.add),
                  r=[t_S, t_YS[q]], w=[t_S])
            p.add("act", lambda e: e.activation(out=S_bf[:, :], in_=S[:, :], func=AF.Copy), r=[t_S], w=[t_Sbf])
    ntiles = nblocks * 4
    blockpro(0)
    front(0)
    chunks = []
    for n in range(ntiles):
        b, t = divmod(n, 4)
        lists = []
        if t == 0 and b + 1 < nblocks:
            bp = p.record(blockpro, b + 1)
            c = (len(bp) + 2) // 3
            chunks = [bp[0:c], bp[c:2 * c], bp[2 * c:]]
        if t < 3 and chunks:
            lists.append(chunks[t])
            if t == 2:
                chunks = []
        if n + 1 < ntiles:
            lists.append(p.record(front, n + 1))
        lists.append(p.record(back, n))
        p.merge(*lists)
    p.emit()
    return nc


def ssd_inputs(hT, w_in, conv_w, conv_b, dt_bias, a_log, d_skip, norm_w, g):
    xo, bo, co, do = 2048, 4096, 5120, 6144
    wfm = np.concatenate([w_in[:, xo + g * 256: xo + (g + 1) * 256], w_in[:, bo + g * 128: bo + (g + 1) * 128],
                          w_in[:, co + g * 128: co + (g + 1) * 128]], axis=1)
    wtm = np.concatenate([w_in[:, g * 256:(g + 1) * 256], w_in[:, do + 4 * g: do + 4 * g + 4]], axis=1)
    cidx = np.concatenate([np.arange(g * 256, (g + 1) * 256), 2048 + np.arange(g * 128, (g + 1) * 128),
                           3072 + np.arange(g * 128, (g + 1) * 128)])
    cw = np.ascontiguousarray(conv_w[:, cidx].reshape(4, 4, 128).transpose(2, 1, 0))
    cb = np.ascontiguousarray(conv_b[cidx].reshape(4, 128).T)
    hp = np.stack([dt_bias[4 * g:4 * g + 4], a_log[4 * g:4 * g + 4], d_skip[4 * g:4 * g + 4]])
    hp = np.ascontiguousarray(np.broadcast_to(hp[None], (128, 3, 4)))
    nw = np.ascontiguousarray(np.broadcast_to(norm_w[None, g * 256:(g + 1) * 256], (128, 256)))
    return {"hT": hT, "wfm": np.ascontiguousarray(wfm), "wtm": np.ascontiguousarray(wtm), "cw": cw, "cb": cb,
            "hp": hp.astype(np.float32), "nw": nw.astype(np.float32), "cst_f": SSD_CST_F, "cst_b": SSD_CST_B}


def _ssd_consts():
    k = np.arange(128)
    triu = (k[:, None] <= k[None, :]).astype(np.float32)
    cst_f = np.ascontiguousarray(np.stack([triu, np.ones((128, 128), np.float32)], axis=1))
    neg = np.where(k[:, None] > k[None, :], -30000.0, 0.0).astype(np.float32)
    cst_b = np.concatenate([np.eye(128, dtype=np.float32), np.tile(neg, (1, 4))], axis=1).astype(ml_dtypes.bfloat16)
    return cst_f, np.ascontiguousarray(cst_b)


SSD_CST_F, SSD_CST_B = _ssd_consts()


def build_attn(seq=SEQ, same_kv=True, nd1=0, nd2=0):
    cx = Ctx(nbanks=8)
    nc, p = cx.nc, cx.p
    BLK = 512
    nblocks = seq // BLK
    nkb = seq // 128
    QG = 8
    nqg = nkb // QG

    hTq = cx.dram_in("hTq", [D, seq], F32)
    wqkv = cx.dram_in("wqkv", [D, 384], F32)
    if same_kv:
        kT_out = cx.dram_out("kT_out", [128, seq], BF16)
        V_out = cx.dram_out("V_out", [128, nkb * 128], BF16)
    else:
        kT_in = cx.dram_in("kT_in", [128, seq], BF16)
        V_in = cx.dram_in("V_in", [128, nkb * 128], BF16)
    cst = cx.dram_in("acst", [128, 1024], BF16)
    negm_d = cx.dram_in("negm", [64, 1], F32)
    oT = cx.dram_out("oT", [128, seq], BF16)

    wqkv_sb = cx.sb("wqkv_sb", [128, 8, 384], BF16)
    wq_sb, wk_sb, wv_sb = wqkv_sb[:, :, 0:128], wqkv_sb[:, :, 128:256], wqkv_sb[:, :, 256:384]
    cst_sb = cx.sb("cst_sb", [128, 1024], BF16)
    negm = cx.sb("negm_sb", [64, 1], F32)
    qT = cx.sb("qT", [128, seq], BF16)
    kT = cx.sb("kT", [128, seq], BF16)
    V = cx.sb("V", [128, nkb, 128], BF16)
    t_w, t_c, t_q, t_k, t_v = Tok("w"), Tok("c"), Tok("q"), Tok("k"), Tok("v")
    negMinc, diagmask = cst_sb[:, 0:128], cst_sb[:, 128:256]
    ones_b, ident_b, zero_b = cst_sb[:, 256:384], cst_sb[:, 384:512], cst_sb[:, 512:1024]
    ones_col = cx.sb("ones_col", [128, 1], F32)
    negones = cx.sb("negones", [128, 2], BF16)
    p.add("pool", lambda e: e.memset(ones_col[:, :], 1.0), w=[t_c])
    p.add("pool", lambda e: e.memset(negones[:, :], -1.0), w=[t_c])

    cx.dma(cst_sb[:, :], cst[:, :], w=[t_c])
    cx.dma(negm[:, :], negm_d[:, :], w=[t_c])
    wv_ = wqkv.rearrange("(k p) n -> p k n", p=128)
    for half in range(2):
        cx.dma_cast(wqkv_sb[:, half * 4:(half + 1) * 4, :], wv_[:, half * 4:(half + 1) * 4, :], w=[t_w])

    hTf, t_hTf = cx.dbuf("hTf", [128, 8, BLK], F32)
    hTb, t_hTb = cx.dbuf("hTb", [128, 8, BLK], BF16)

    passes = [("qkv", hTq)] if same_kv else [("q", hTq)]
    if not same_kv:
        for c4 in range(4):
            cs = slice(c4 * (seq // 4), (c4 + 1) * (seq // 4))
            cx.dma(kT[:, cs], kT_in[:, cs], w=[t_k])
        Vf = V[:, :, :].rearrange("p b c -> p (b c)")
        for c4 in range(4):
            cs = slice(c4 * (nkb * 32), (c4 + 1) * (nkb * 32))
            cx.dma(Vf[:, cs], V_in[:, cs], w=[t_v])
    cnt = 0
    for what, src in passes:
        src_v = src.rearrange("(k p) t -> p k t", p=128)

        def load_block(b, cnt, src_v=src_v):
            pb = cnt % 2
            for half in range(2):
                cx.dma(hTf[pb][:, half * 4:(half + 1) * 4, :], src_v[:, half * 4:(half + 1) * 4, b * BLK:(b + 1) * BLK],
                       w=[t_hTf[pb]])

        load_block(0, cnt)
        for b in range(nblocks):
            pb = cnt % 2
            if b + 1 < nblocks:
                load_block(b + 1, cnt + 1)
            cnt += 1
            p.add("act", lambda e, pb=pb: e.activation(out=hTb[pb][:, 0:3, :], in_=hTf[pb][:, 0:3, :], func=AF.Copy),
                  r=[t_hTf[pb]], w=[t_hTb[pb]])
            p.add("dve", lambda e, pb=pb: e.tensor_copy(out=hTb[pb][:, 3:6, :], in_=hTf[pb][:, 3:6, :]),
                  r=[t_hTf[pb]], w=[t_hTb[pb]])
            p.add("pool", lambda e, pb=pb: e.tensor_copy(out=hTb[pb][:, 6:8, :], in_=hTf[pb][:, 6:8, :]),
                  r=[t_hTf[pb]], w=[t_hTb[pb]])
            cols = slice(b * BLK, (b + 1) * BLK)
            if "q" in what:
                bank, bt = cx.bank()
                for k in range(8):
                    p.add("pe", lambda e, bank=bank, k=k, pb=pb: e.matmul(
                        bank, lhsT=wq_sb[:, k, :], rhs=hTb[pb][:, k, :], start=(k == 0), stop=(k == 7)),
                        r=[t_w, t_hTb[pb]], w=[bt])
                p.add("act", lambda e, bank=bank, cols=cols: e.activation(out=qT[:, cols], in_=bank, func=AF.Copy, scale=0.125),
                      r=[bt], w=[t_q])
            if "k" in what:
                bank, bt = cx.bank()
                for k in range(8):
                    p.add("pe", lambda e, bank=bank, k=k, pb=pb: e.matmul(
                        bank, lhsT=wk_sb[:, k, :], rhs=hTb[pb][:, k, :], start=(k == 0), stop=(k == 7)),
                        r=[t_w, t_hTb[pb]], w=[bt])
                p.add("dve", lambda e, bank=bank, cols=cols: e.tensor_copy(out=kT[:, cols], in_=bank), r=[bt], w=[t_k])
                bank, bt = cx.bank()
                for t in range(4):
                    for k in range(8):
                        p.add("pe", lambda e, bank=bank, k=k, t=t, pb=pb: e.matmul(
                            bank[:, t * 128:(t + 1) * 128], lhsT=hTb[pb][:, k, t * 128:(t + 1) * 128], rhs=wv_sb[:, k, :],
                            start=(k == 0), stop=(k == 7)), r=[t_w, t_hTb[pb]], w=[bt])
                p.add("act", lambda e, bank=bank, b=b: e.activation(
                    out=V[:, 4 * b:4 * b + 4, :], in_=bank.rearrange("p (t c) -> p t c", t=4), func=AF.Copy),
                    r=[bt], w=[t_v])

    if same_kv:
        Vf = V[:, :, :].rearrange("p b c -> p (b c)")
        for c4 in range(4):
            cs = slice(c4 * (seq // 4), (c4 + 1) * (seq // 4))
            cx.dma(kT_out[:, cs], kT[:, cs], r=[t_k])
            cs = slice(c4 * (nkb * 32), (c4 + 1) * (nkb * 32))
            cx.dma(V_out[:, cs], Vf[:, cs], r=[t_v])

    NQ = QG * 128
    zb = [cx.ps[:, 0:2, :].rearrange("p b c -> p (b c)"), cx.ps[:, 2:4, :].rearrange("p b c -> p (b c)")]
    ob = cx.ps[:, 4:6, :].rearrange("p b c -> p (b c)")
    accb = cx.ps[:, 6:8, :].rearrange("p b c -> p (b c)")
    t_z = [Tok("zA", excl=True), Tok("zB", excl=True)]
    t_zp = [[Tok(f"z{h}{i}", excl=True) for i in range(2)] for h in range(2)]
    t_Wp = [[Tok(f"W{h}{i}") for i in range(2)] for h in range(2)]
    t_Ep = [[Tok(f"E{h}{i}") for i in range(2)] for h in range(2)]
    t_Lp = [[Tok(f"L{h}{i}") for i in range(2)] for h in range(2)]
    t_o, t_acc = Tok("o", excl=True), [Tok("accA", excl=True), Tok("accB", excl=True)]
    for hd in range(2):
        for bnk in (2 * hd, 2 * hd + 1):
            t_zp[hd][bnk - 2 * hd].rs.update(cx.ps_tok[bnk].rs)
    for bnk in (4, 5):
        t_o.rs.update(cx.ps_tok[bnk].rs)
    for hd in range(2):
        for bnk in (6, 7):
            t_acc[hd].rs.update(cx.ps_tok[bnk].rs)
    E_sb, t_E = cx.dbuf("E_sb", [128, NQ], F32)
    L_sb, t_L = cx.dbuf("L_sb", [128, NQ], BF16)
    W_sb, t_W = cx.dbuf("W_sb", [128, NQ], BF16)
    hi_t = cx.sb("hi_t", [64, NQ], BF16)
    acc_hl = cx.sb("acc_hl", [64, NQ], BF16)
    t_him, t_hlm, t_accm = Tok("hi"), Tok("hl"), Tok("accm", excl=True)
    for bnk in (6, 7):
        t_accm.rs.update(cx.ps_tok[bnk].rs)
    obuf, t_ob = cx.dbuf("obuf", [128, NQ], BF16)
    rows = [slice(0, 2), slice(32, 34)]

    def dummies(n):
        for i in range(n):
            c0 = 512 * (i % 2)
            p.add("pe", lambda e, c0=c0: e.matmul(ob[:, c0:c0 + 512], lhsT=zero_b[:, 0:128], rhs=zero_b[:, 0:512],
                                                  start=False, stop=False, skip_group_check=True), r=[t_c], w=[t_o])

    def pieces(lo):
        out = []
        for c0 in (0, 512):
            a, bnd = max(lo, c0), c0 + 512
            if a < bnd:
                out.append((a, bnd))
        return out

    for qg in range(nqg):
        i0 = qg * QG
        qcol0 = i0 * 128
        for c0 in (0, 512):
            p.add("pe", lambda e, c0=c0: e.matmul(ob[:, c0:c0 + 512], lhsT=zero_b[:, 0:128], rhs=zero_b[:, 0:512],
                                                  start=True, stop=False, skip_group_check=True), r=[t_c], w=[t_o])
            p.add("pe", lambda e, c0=c0: e.matmul(
                accb[0:34, c0:c0 + 512], lhsT=zero_b[:, 0:34], rhs=zero_b[:, 0:512],
                start=True, stop=False, skip_group_check=True), r=[t_c], w=[t_accm])
        p.add("pool", lambda e: e.memset(acc_hl[0:34, :], 0.0), w=[t_hlm])
        def emit_z(j, hd):
            lo = max(j - i0, 0) * 128
            ks = slice(j * 128, (j + 1) * 128)
            hp_ = slice(hd * 64, (hd + 1) * 64)
            for (a, bnd) in pieces(lo):
                p.add("pe", lambda e, hd=hd, hp_=hp_, a=a, bnd=bnd, ks=ks, qcol0=qcol0: e.matmul(
                    zb[hd][:, a:bnd], lhsT=kT[hp_, ks], rhs=qT[hp_, qcol0 + a:qcol0 + bnd],
                    start=True, stop=False, skip_group_check=True), r=[t_k, t_q], w=[t_zp[hd][a // 512]])
            if j >= i0:
                p.add("pe", lambda e, hd=hd, lo=lo: e.matmul(
                    zb[hd][:, lo:lo + 128], lhsT=ident_b, rhs=diagmask, start=False, stop=False,
                    skip_group_check=True), r=[t_c], w=[t_zp[hd][lo // 512]])

        jtop = i0 + QG - 1
        for hd in range(2):
            emit_z(jtop, hd)
        for j in range(jtop, -1, -1):
            lo = max(j - i0, 0) * 128
            pcs = pieces(lo)
            for hd in range(2):
                for (a, bnd) in pcs:
                    h2 = a // 512
                    p.add("act", lambda e, hd=hd, a=a, bnd=bnd: e.activation(
                        out=E_sb[hd][:, a:bnd], in_=zb[hd][:, a:bnd], func=AF.Exp),
                        r=[t_zp[hd][h2]], w=[t_Ep[hd][h2]])
                    p.add("act", lambda e, hd=hd, a=a, bnd=bnd: e.activation(
                        out=L_sb[hd][:, a:bnd], in_=E_sb[hd][:, a:bnd], func=AF.Ln, bias=ones_col[:, 0:1], scale=1.0),
                        r=[t_Ep[hd][h2], t_c], w=[t_Lp[hd][h2]])
            for hd in range(2):
                for (a, bnd) in pcs:
                    h2 = a // 512
                    p.add("pe", lambda e, hd=hd, a=a, bnd=bnd: e.matmul(
                        zb[hd][:, a:bnd], lhsT=negMinc, rhs=L_sb[hd][:, a:bnd], start=False, stop=False,
                        skip_group_check=True), r=[t_Lp[hd][h2], t_c], w=[t_zp[hd][h2]])
                    p.add("pe", lambda e, hd=hd, a=a, bnd=bnd: e.matmul(
                        zb[hd][:, a:bnd], lhsT=ones_b[rows[hd], :], rhs=acc_hl[rows[hd], a:bnd], start=False, stop=True,
                        skip_group_check=True), r=[t_hlm, t_c], w=[t_zp[hd][h2]])
            for hd in range(2):
                for (a, bnd) in pcs:
                    h2 = a // 512
                    p.add("act", lambda e, hd=hd, a=a, bnd=bnd: e.activation(
                        out=W_sb[hd][:, a:bnd], in_=zb[hd][:, a:bnd], func=AF.Exp),
                        r=[t_zp[hd][h2]], w=[t_Wp[hd][h2]])
            for (a, bnd) in pcs:
                for hd in range(2):
                    p.add("pe", lambda e, hd=hd, a=a, bnd=bnd: e.matmul(
                        accb[rows[hd], a:bnd], lhsT=negones[:, 0:2], rhs=L_sb[hd][:, a:bnd], start=False, stop=False,
                        skip_group_check=True), r=[t_Lp[hd][a // 512], t_c], w=[t_accm])
            if j > 0:
                p.add("dve", lambda e, lo=lo: e.tensor_copy(out=hi_t[0:34, lo:NQ], in_=accb[0:34, lo:NQ]),
                      r=[t_accm], w=[t_him])
                p.add("dve", lambda e, lo=lo: e.scalar_tensor_tensor(
                    out=acc_hl[0:34, lo:NQ], in0=hi_t[0:34, lo:NQ], scalar=negm[0:34, 0:1],
                    in1=accb[0:34, lo:NQ], op0=ALU.mult, op1=ALU.add), r=[t_him, t_accm, t_c], w=[t_hlm])
                for hd in range(2):
                    emit_z(j - 1, hd)
            for (a, bnd) in pcs:
                for hd in range(2):
                    p.add("pe", lambda e, hd=hd, a=a, bnd=bnd, j=j: e.matmul(
                        ob[hd * 64:(hd + 1) * 64, a:bnd], lhsT=V[:, j, hd * 64:(hd + 1) * 64], rhs=W_sb[hd][:, a:bnd],
                        start=False, stop=False, skip_group_check=True), r=[t_Wp[hd][a // 512], t_v], w=[t_o])
        ob_i = qg % 2
        p.add("dve", lambda e, ob_i=ob_i: e.tensor_copy(out=obuf[ob_i][:, :], in_=ob[:, :]), r=[t_o], w=[t_ob[ob_i]])
        cx.dma(oT[:, qcol0:qcol0 + NQ], obuf[ob_i][:, :], r=[t_ob[ob_i]])
    p.emit()
    return nc


def _attn_consts():
    k = np.arange(128)
    neg_minc = np.where(k[:, None] >= k[None, :], -1.0, 0.0)
    diag = np.where(k[:, None] >= k[None, :], -30000.0, 0.0)
    cst = np.concatenate([neg_minc, diag, np.ones((128, 128)), np.eye(128), np.zeros((128, 512))], axis=1)
    negm = np.zeros((64, 1), np.float32)
    negm[1, 0] = -1.0
    negm[33, 0] = -1.0
    return np.ascontiguousarray(cst.astype(ml_dtypes.bfloat16)), negm


ATT_CST, ATT_NEGM = _attn_consts()


_PROGS = {}


def _prog(key, builder):
    if key not in _PROGS:
        _PROGS[key] = builder()
    return _PROGS[key]


def _run(nc, in_maps):
    return run_bass_kernel_spmd(nc, in_maps, core_ids=list(range(NCORES))).results


def kernel(x, ssm_w_in, ssm_conv_w, ssm_conv_b, ssm_dt_bias, ssm_a_log, ssm_d, ssm_norm_w, ssm_w_out,
           sb_w_k, sb_w_v, sb_w_q, sb_w_o, mlp_w1, mlp_w2, ln_mix_g, ln_mix_b, ln_mlp_g, ln_mlp_b):
    f32 = lambda a: np.ascontiguousarray(np.asarray(a, dtype=np.float32))
    h = f32(x)[0]
    ident = np.eye(128, dtype=np.float32)
    kv_shared = None
    for layer in range(DEPTH):
        hT = np.ascontiguousarray(h.T)
        if layer < 2:
            nc = _prog("ssd", build_ssd)
            ins = [ssd_inputs(hT, f32(ssm_w_in[layer]), f32(ssm_conv_w[layer]), f32(ssm_conv_b[layer]),
                              f32(ssm_dt_bias[layer]), f32(ssm_a_log[layer]), f32(ssm_d[layer]),
                              f32(ssm_norm_w[layer]), g) for g in range(NCORES)]
            res = _run(nc, ins)
            Y = np.concatenate([np.asarray(res[g]["yn"]) for g in range(NCORES)], axis=1)
            yTs = [np.ascontiguousarray(Y[c * TOK:(c + 1) * TOK].T) for c in range(NCORES)]
            w_o, kin = f32(ssm_w_out[layer]), D_INNER
        else:
            j = layer - 2
            same = layer == 2
            nc = _prog(("attn", same), lambda: build_attn(SEQ, same_kv=same))
            ins = []
            for c in range(NCORES):
                sl = slice(c * 128, (c + 1) * 128)
                wqkv = np.concatenate([f32(sb_w_q[j][:, sl]), f32(sb_w_k[:, sl]), f32(sb_w_v[:, sl])], axis=1)
                d = {"hTq": hT, "wqkv": np.ascontiguousarray(wqkv), "acst": ATT_CST, "negm": ATT_NEGM}
                if not same:
                    d["kT_in"], d["V_in"] = kv_shared[c]
                ins.append(d)
            res = _run(nc, ins)
            if same:
                kv_shared = [(np.asarray(res[c]["kT_out"]), np.asarray(res[c]["V_out"])) for c in range(NCORES)]
            OT = np.concatenate([np.asarray(res[c]["oT"]) for c in range(NCORES)], axis=0)
            yTs = [np.ascontiguousarray(OT[:, c * TOK:(c + 1) * TOK]) for c in range(NCORES)]
            w_o, kin = f32(sb_w_o[j]), D
        nc = _prog(("t", kin), lambda: build_tphase(kin))
        lnp = np.stack([f32(ln_mix_g[layer]), f32(ln_mix_b[layer]), f32(ln_mlp_g[layer]), f32(ln_mlp_b[layer])])
        lnp = np.ascontiguousarray(np.broadcast_to(lnp[None], (128, 4, D)))
        w1, w2 = f32(mlp_w1[layer]), f32(mlp_w2[layer])
        ins = [{"yT": yTs[c], "h": np.ascontiguousarray(h[c * TOK:(c + 1) * TOK]), "w_o": w_o, "w1": w1, "w2": w2,
                "lnp": lnp, "ident": ident} for c in range(NCORES)]
        res = _run(nc, ins)
        h = np.concatenate([np.asarray(res[c]["h_out"]) for c in range(NCORES)], axis=0)
    return h[None].astype(np.float32)
```

```python
import math
import numpy as np
import ml_dtypes
import concourse.bass as bass
import concourse.mybir as mybir
from concourse.bass_utils import run_bass_kernel_spmd

F32 = mybir.dt.float32
BF16 = mybir.dt.bfloat16
AF = mybir.ActivationFunctionType
ALU = mybir.AluOpType
AX = mybir.AxisListType

NCORES = 8
SEQ = 16384
D = 1024
DEPTH = 4
TOK = SEQ // NCORES
ALPHA = (2 * DEPTH) ** 0.25
LN_EPS = 1e-5
RMS_EPS = 1e-5
D_INNER = 2048
D_FF = 4096
NBLK = SEQ // 128

ENGS = ("pe", "act", "dve", "pool", "sp")
NRING = 6


class Tok:
    __slots__ = ("name", "w", "rs", "rd", "excl")

    def __init__(self, name="", excl=False):
        self.name = name
        self.w = None
        self.rs = {}
        self.rd = []
        self.excl = excl


class Op:
    __slots__ = ("eng", "fn", "deps", "dma", "sig", "sem", "val", "prev")


class Prog:
    def __init__(self, nc):
        self.nc = nc
        self.ops = {e: [] for e in ENGS}

    def add(self, eng, fn, r=(), w=(), dma=False):
        op = Op()
        op.eng, op.fn, op.dma, op.sig, op.sem, op.val, op.prev = eng, fn, dma, False, None, 0, 0
        deps = set()
        for t in r:
            if t.w is not None:
                deps.add(t.w)
            if t.excl:
                deps.update(o for en, o in t.rs.items() if en != eng)
        for t in w:
            if t.w is not None:
                deps.add(t.w)
            deps.update(t.rs.values())
            deps.update(t.rd)
        if eng == "pe" and not dma:
            deps = {d for d in deps if d.dma or d.eng != "pe"}
        deps.discard(op)
        op.deps = deps
        for t in r:
            if dma:
                t.rd.append(op)
            else:
                t.rs[eng] = op
        for t in w:
            t.w = op
            t.rs = {}
            t.rd = []
        self.ops[eng].append(op)
        return op

    def emit(self):
        nc = self.nc
        for e in ENGS:
            for op in self.ops[e]:
                for d in op.deps:
                    d.sig = True
        csem = {e: nc.alloc_semaphore(name=f"c_{e}") for e in ENGS if e != "sp"}
        ring = {e: [nc.alloc_semaphore(name=f"r_{e}{i}") for i in range(NRING)] for e in ENGS}
        final = {}
        for e in ENGS:
            cnt = 0
            nd = 0
            for op in self.ops[e]:
                if op.dma:
                    op.sem = ring[e][nd % NRING]
                    op.prev = 16 * (nd // NRING)
                    op.val = op.prev + 16
                    final[op.sem] = op.val
                    nd += 1
                elif op.sig:
                    cnt += 1
                    op.sem = csem[e]
                    op.val = cnt
        engobj = {"pe": "tensor", "act": "scalar", "dve": "vector", "pool": "gpsimd", "sp": "sync"}
        ops = self.ops

        def run(e, eng):
            waited = {}
            for op in ops[e]:
                needs = {}
                for d in op.deps:
                    if needs.get(d.sem, 0) < d.val:
                        needs[d.sem] = d.val
                if op.dma and op.prev > 0 and needs.get(op.sem, 0) < op.prev:
                    needs[op.sem] = op.prev
                for sem, val in needs.items():
                    if waited.get(sem, 0) < val:
                        eng.wait_ge(sem, val)
                        waited[sem] = val
                ins = op.fn(eng)
                if op.dma:
                    ins.then_inc(op.sem, 16)
                elif op.sig:
                    ins.then_inc(op.sem, 1)
            if e == "sp":
                for sem, val in final.items():
                    if waited.get(sem, 0) < val:
                        eng.wait_ge(sem, val)

        with nc.Block() as block:
            @block.tensor
            def _(eng):
                run("pe", eng)

            @block.scalar
            def _(eng):
                run("act", eng)

            @block.vector
            def _(eng):
                run("dve", eng)

            @block.gpsimd
            def _(eng):
                run("pool", eng)

            @block.sync
            def _(eng):
                run("sp", eng)


class PProxy:
    def __init__(self, real):
        self.real = real
        self.rec = None

    def add(self, *a, **k):
        if self.rec is not None:
            self.rec.append((a, k))
            return None
        return self.real.add(*a, **k)

    def record(self, fn, *args):
        self.rec = []
        fn(*args)
        items, self.rec = self.rec, None
        return items

    def merge(self, *lists):
        total = max(len(l) for l in lists)
        pos = [0] * len(lists)
        for step in range(1, total + 1):
            for i, l in enumerate(lists):
                upto = (len(l) * step) // total
                while pos[i] < upto:
                    a, k = l[pos[i]]
                    self.real.add(*a, **k)
                    pos[i] += 1

    def emit(self):
        self.real.emit()


class Ctx:
    def __init__(self, nbanks=8):
        self.nc = bass.Bass("TRN2", target_bir_lowering=False)
        self.p = Prog(self.nc)
        self.nb = nbanks
        self.ps = self.nc.alloc_psum_tensor("psum_all", [128, nbanks, 512], F32)
        self.ps_tok = [Tok(f"ps{i}", excl=True) for i in range(nbanks)]
        self.ps_next = 0
        self.nbuf = 0

    def dbuf(self, name, shape, dt, n=2):
        bufs = [self.nc.alloc_sbuf_tensor(f"{name}_{i}", shape, dt) for i in range(n)]
        toks = [Tok(f"{name}_{i}") for i in range(n)]
        return bufs, toks

    def sb(self, name, shape, dt):
        return self.nc.alloc_sbuf_tensor(name, shape, dt)

    def bank(self):
        b = self.ps_next
        self.ps_next = (b + 1) % self.nb
        return self.ps[:, b, :], self.ps_tok[b]

    def dram_in(self, name, shape, dt):
        return self.nc.dram_tensor(name, list(shape), dt, kind="ExternalInput").ap()

    def dram_out(self, name, shape, dt):
        return self.nc.dram_tensor(name, list(shape), dt, kind="ExternalOutput").ap()

    def dma(self, out, in_, r=(), w=(), eng=None):
        if eng is None:
            eng = "sp"
        return self.p.add(eng, lambda e, o=out, i=in_: e.dma_start(out=o, in_=i), r=r, w=w, dma=True)

    def dma_cast(self, out, in_, r=(), w=()):
        return self.p.add("pool", lambda e, o=out, i=in_: e.dma_start(out=o, in_=i), r=r, w=w, dma=True)


def load_cast_rows(cx, dst, src, nk, ncols, wtok, chunk=2048):
    v = src.rearrange("(k p) n -> p k n", p=128)
    for k in range(nk):
        for c0 in range(0, ncols, chunk):
            c1 = min(ncols, c0 + chunk)
            cx.dma_cast(dst[:, k, c0:c1], v[:, k, c0:c1], w=[wtok])


def layer_norm_tile(cx, r_sb, r_tok, out_sb, out_tok, g_sb, b_sb, eps_sb, scr, gtok):
    p = cx.p
    st, mv, rstd, st_tok = scr
    for c in range(2):
        p.add("dve", lambda e, c=c: e.bn_stats(out=st[:, c, :], in_=r_sb[:, c * 512:(c + 1) * 512]),
              r=[r_tok], w=[st_tok])
    p.add("dve", lambda e: e.bn_aggr(out=mv[:, :], in_=st[:, :, :].rearrange("p a b -> p (a b)")), r=[st_tok], w=[st_tok])
    p.add("act", lambda e: e.activation(out=rstd[:, :], in_=mv[:, 1:2], func=AF.Sqrt, bias=eps_sb[:, 0:1], scale=1.0),
          r=[st_tok], w=[st_tok])
    p.add("dve", lambda e: e.reciprocal(out=rstd[:, :], in_=rstd[:, :]), r=[st_tok], w=[st_tok])
    p.add("dve", lambda e: e.tensor_scalar(out=out_sb, in0=r_sb, scalar1=mv[:, 0:1], scalar2=rstd[:, 0:1],
                                           op0=ALU.subtract, op1=ALU.mult), r=[r_tok, st_tok], w=[out_tok])
    p.add("pool", lambda e: e.tensor_tensor(out=out_sb, in0=out_sb, in1=g_sb, op=ALU.mult), r=[out_tok, gtok], w=[out_tok])
    p.add("pool", lambda e: e.tensor_tensor(out=out_sb, in0=out_sb, in1=b_sb, op=ALU.add), r=[out_tok, gtok], w=[out_tok])


def build_tphase(kin):
    cx = Ctx()
    cx.p = PProxy(cx.p)
    nc, p = cx.nc, cx.p
    ctr = {"a": 0, "b": 0}

    def abank():
        i = ctr["a"] % 2
        ctr["a"] += 1
        return cx.ps[:, i, :], cx.ps_tok[i]

    def bbank():
        i = 2 + ctr["b"] % 6
        ctr["b"] += 1
        return cx.ps[:, i, :], cx.ps_tok[i]

    KC = kin // 128
    G = 128
    CPB = 512 // G
    NG = TOK // G
    yT = cx.dram_in("yT", [kin, TOK], BF16)
    h_in = cx.dram_in("h", [TOK, D], F32)
    w_o = cx.dram_in("w_o", [kin, D], F32)
    w1 = cx.dram_in("w1", [D, D_FF], F32)
    w2 = cx.dram_in("w2", [D_FF, D], F32)
    lnp = cx.dram_in("lnp", [128, 4, D], F32)
    h_out = cx.dram_out("h_out", [TOK, D], F32)

    wo_sb = cx.sb("wo_sb", [128, KC, D], BF16)
    w1_sb = cx.sb("w1_sb", [128, 8, D_FF], BF16)
    w2_sb = cx.sb("w2_sb", [128, 32, D], BF16)
    lnp_sb = cx.sb("lnp_sb", [128, 4, D], F32)
    ident = cx.sb("ident_sb", [128, 128], F32)
    eps_sb = cx.sb("eps_sb", [128, 1], F32)
    t_wo, t_w1, t_w2, t_lnp, t_const = Tok("wo"), Tok("w1"), Tok("w2"), Tok("lnp"), Tok("const")

    ident_d = cx.dram_in("ident", [128, 128], F32)
    cx.dma(ident[:, :], ident_d[:, :], w=[t_const])
    p.add("pool", lambda e: e.memset(eps_sb[:, :], LN_EPS), w=[t_const])
    cx.dma(lnp_sb[:, :, :], lnp[:, :, :], w=[t_lnp])

    yT_sb = cx.sb("yT_sb", [128, KC, G], BF16)
    h_sb = cx.sb("h_sb", [128, D], F32)
    t_yT, t_h = Tok("yT"), Tok("h")
    rbuf = [cx.sb(f"rbuf{i}", [128, D], F32) for i in range(2)]
    t_rb = [Tok("rb0"), Tok("rb1")]
    h1Ts = [cx.sb(f"h1T{i}", [128, 8, G], BF16) for i in range(2)]
    t_h1Ts = [Tok("h1T0"), Tok("h1T1")]
    uT = cx.sb("uT", [128, 32, G], BF16)
    t_uT = [Tok(f"uT{i}") for i in range(32 // CPB)]
    utmp = [cx.sb("utmp0", [128, CPB * G], F32)] * 2
    t_utmp = [Tok("utmp0")] * 2
    st = cx.sb("ln_st", [128, 2, 6], F32)
    mv = cx.sb("ln_mv", [128, 2], F32)
    rstd = cx.sb("ln_rstd", [128, 1], F32)
    scr = (st, mv, rstd, Tok("lnscr"))
    scrB = (cx.sb("ln_stB", [128, 2, 6], F32), cx.sb("ln_mvB", [128, 2], F32), cx.sb("ln_rstdB", [128, 1], F32), Tok("lnscrB"))

    yT_v = yT.rearrange("(k p) t -> p k t", p=128)
    h_v = h_in.rearrange("(n p) d -> p n d", p=128)
    ho_v = h_out.rearrange("(n p) d -> p n d", p=128)

    def load_group(g):
        cx.dma(yT_sb[:, :, :], yT_v[:, :, g * G:(g + 1) * G], w=[t_yT])
        cx.dma(h_sb[:, :], h_v[:, g, :], w=[t_h])

    load_group(0)
    load_cast_rows(cx, wo_sb, w_o, KC, D, t_wo)
    load_cast_rows(cx, w1_sb, w1, 8, D_FF, t_w1)
    load_cast_rows(cx, w2_sb, w2, 32, D, t_w2)

    def stageA(g):
        rb, trb = rbuf[g % 2], t_rb[g % 2]
        h1T, t_h1T = h1Ts[g % 2], t_h1Ts[g % 2]
        for n in range(2):
            bank, bt = abank()
            for k in range(KC):
                p.add("pe", lambda e, bank=bank, k=k, n=n: e.matmul(
                    bank, lhsT=yT_sb[:, k, :], rhs=wo_sb[:, k, n * 512:(n + 1) * 512],
                    start=(k == 0), stop=(k == KC - 1)), r=[t_yT, t_wo], w=[bt])
            p.add("dve", lambda e, bank=bank, n=n, rb=rb: e.scalar_tensor_tensor(
                out=rb[:, n * 512:(n + 1) * 512], in0=h_sb[:, n * 512:(n + 1) * 512], scalar=ALPHA,
                in1=bank, op0=ALU.mult, op1=ALU.add), r=[bt, t_h], w=[trb])
        if g + 1 < NG:
            load_group(g + 1)
        layer_norm_tile(cx, rb[:, :], trb, rb[:, :], trb, lnp_sb[:, 0, :], lnp_sb[:, 1, :], eps_sb, scr, t_lnp)
        for q in range(2):
            bank, bt = abank()
            for j in range(4):
                f = q * 4 + j
                p.add("pe", lambda e, bank=bank, j=j, f=f, rb=rb: e.transpose(
                    bank[:, j * 128:(j + 1) * 128], rb[:, f * 128:(f + 1) * 128], ident[:, :]),
                    r=[trb, t_const], w=[bt])
            p.add("act", lambda e, bank=bank, q=q: e.activation(
                out=h1T[:, q * 4:(q + 1) * 4, :], in_=bank.rearrange("p (j c) -> p j c", j=4), func=AF.Copy),
                r=[bt], w=[t_h1T])
    def stageB(g):
        rb, trb = rbuf[g % 2], t_rb[g % 2]
        h1T, t_h1T = h1Ts[g % 2], t_h1Ts[g % 2]
        for cp in range(32 // CPB):
            bank, bt = bbank()
            for j in range(CPB):
                ch = cp * CPB + j
                for k in range(8):
                    p.add("pe", lambda e, bank=bank, j=j, ch=ch, k=k: e.matmul(
                        bank[:, j * G:(j + 1) * G], lhsT=w1_sb[:, k, ch * 128:(ch + 1) * 128], rhs=h1T[:, k, :],
                        start=(k == 0), stop=(k == 7)), r=[t_w1, t_h1T], w=[bt])
            ub = cp % 2
            p.add("act", lambda e, bank=bank, ub=ub: e.activation(out=utmp[ub][:, :], in_=bank[:, 0:CPB * G], func=AF.Relu),
                  r=[bt], w=[t_utmp[ub]])
            p.add("pool", lambda e, cp=cp, ub=ub: e.tensor_tensor(
                out=uT[:, CPB * cp:CPB * cp + CPB, :], in0=utmp[ub][:, :].rearrange("p (j c) -> p j c", j=CPB),
                in1=utmp[ub][:, :].rearrange("p (j c) -> p j c", j=CPB), op=ALU.mult), r=[t_utmp[ub]], w=[t_uT[cp]])
        for n in range(2):
            bank, bt = bbank()
            for k in range(32):
                p.add("pe", lambda e, bank=bank, k=k, n=n: e.matmul(
                    bank, lhsT=uT[:, k, :], rhs=w2_sb[:, k, n * 512:(n + 1) * 512],
                    start=(k == 0), stop=(k == 31)), r=[t_uT[k // CPB], t_w2], w=[bt])
            p.add("dve", lambda e, bank=bank, n=n, rb=rb: e.scalar_tensor_tensor(
                out=rb[:, n * 512:(n + 1) * 512], in0=rb[:, n * 512:(n + 1) * 512], scalar=ALPHA,
                in1=bank, op0=ALU.mult, op1=ALU.add), r=[bt, trb], w=[trb])
        layer_norm_tile(cx, rb[:, :], trb, rb[:, :], trb, lnp_sb[:, 2, :], lnp_sb[:, 3, :], eps_sb, scrB, t_lnp)
        cx.dma(ho_v[:, g, :], rb[:, :], r=[trb])
    stageA(0)
    for g in range(NG):
        lists = []
        if g + 1 < NG:
            lists.append(p.record(stageA, g + 1))
        lists.append(p.record(stageB, g))
        p.merge(*lists)
    p.emit()
    return nc


def build_ssd(nblocks=SEQ // 512):
    cx = Ctx(nbanks=5)
    cx.p = PProxy(cx.p)
    pool_ctr = {"t": 0, "b": 0}

    def tbank():
        i = pool_ctr["t"] % 3
        pool_ctr["t"] += 1
        return cx.ps[:, i, :], cx.ps_tok[i]

    def bbank():
        i = 3 + pool_ctr["b"] % 2
        pool_ctr["b"] += 1
        return cx.ps[:, i, :], cx.ps_tok[i]

    nc, p = cx.nc, cx.p
    BLK = 512
    seq = nblocks * BLK
    psB = nc.alloc_psum_tensor("psum_bf", [128, 1024], BF16)
    t_psB = Tok("psB", excl=True)
    psYS = nc.alloc_psum_tensor("psum_ys", [128, 2, 512], F32)
    t_YS = [Tok("YS0", excl=True), Tok("YS1", excl=True)]

    hT = cx.dram_in("hT", [D, seq], F32)
    wfm = cx.dram_in("wfm", [D, 512], F32)
    wtm = cx.dram_in("wtm", [D, 260], F32)
    cw = cx.dram_in("cw", [128, 4, 4], F32)
    cb = cx.dram_in("cb", [128, 4], F32)
    hp = cx.dram_in("hp", [128, 3, 4], F32)
    nw = cx.dram_in("nw", [128, 256], F32)
    cst_f = cx.dram_in("cst_f", [128, 2, 128], F32)
    cst_b = cx.dram_in("cst_b", [128, 128 + 512], BF16)
    yn = cx.dram_out("yn", [seq, 256], BF16)

    wfm_sb = cx.sb("wfm_sb", [128, 8, 512], BF16)
    wtm_sb = cx.sb("wtm_sb", [128, 8, 260], BF16)
    cw_sb = cx.sb("cw_sb", [128, 4, 4], F32)
    cb_sb = cx.sb("cb_sb", [128, 4], F32)
    hp_sb = cx.sb("hp_sb", [128, 3, 4], F32)
    nw_sb = cx.sb("nw_sb", [128, 256], F32)
    cf_sb = cx.sb("cf_sb", [128, 2, 128], F32)
    cbf_sb = cx.sb("cbf_sb", [128, 640], BF16)
    aneg = cx.sb("aneg", [128, 4], F32)
    onecol = cx.sb("onecol", [128, 1], F32)
    neghalf = cx.sb("neghalf", [128, 1], F32)
    S = cx.sb("S_state", [128, 256], F32)
    S_bf = cx.sb("S_bf", [128, 256], BF16)
    t_w, t_c, t_S, t_Sbf = Tok("w"), Tok("c"), Tok("S"), Tok("Sbf")
    triu, ones_f = cf_sb[:, 0, :], cf_sb[:, 1, :]
    ident_b, negmask = cbf_sb[:, 0:128], cbf_sb[:, 128:640]

    for dst, src in ((cw_sb, cw), (cb_sb, cb), (hp_sb, hp), (nw_sb, nw), (cf_sb, cst_f), (cbf_sb, cst_b)):
        nd = len(dst.shape)
        idx = tuple([slice(None)] * nd)
        cx.dma(dst[idx], src[idx], w=[t_c])
    load_cast_rows(cx, wfm_sb, wfm, 8, 512, t_w)
    load_cast_rows(cx, wtm_sb, wtm, 8, 260, t_w)
    p.add("pool", lambda e: e.memset(onecol[:, :], 1.0), w=[t_c])
    p.add("pool", lambda e: e.memset(neghalf[:, :], -0.5), w=[t_c])
    p.add("pool", lambda e: e.memset(S[:, :], 0.0), w=[t_S])
    p.add("pool", lambda e: e.memset(S_bf[:, :], 0.0), w=[t_Sbf])
    p.add("act", lambda e: e.activation(out=aneg[:, :], in_=hp_sb[:, 1, :], func=AF.Exp), r=[t_c], w=[t_c])
    p.add("dve", lambda e: e.tensor_scalar(out=aneg[:, :], in0=aneg[:, :], scalar1=-1.0, scalar2=None, op0=ALU.mult),
          r=[t_c], w=[t_c])

    hTf, t_hTf = cx.dbuf("hTf", [128, 8, BLK], F32)
    hTb, t_hTb = cx.dbuf("hTb", [128, 8, BLK], BF16, n=3)
    xpre, t_xpre = cx.dbuf("xpre", [128, 4, 3 + BLK], F32)
    acc, t_acc = cx.dbuf("cacc", [128, BLK], F32)
    xbcT, t_xbcT = cx.dbuf("xbcT", [128, 4, BLK], BF16)
    zs, t_zs = cx.dbuf("zs", [128, 4, 256], F32)
    dtr, t_dtr = cx.dbuf("dtr", [128, 4, 4], F32)
    dte, t_dte = cx.dbuf("dte", [128, 4, 4], F32)
    dts, t_dts = cx.dbuf("dts", [128, 4, 4], F32)
    das, t_das = cx.dbuf("das", [128, 4, 4], F32)
    xdt, t_xdt = cx.dbuf("xdt", [128, 256], BF16)
    ysk, t_ysk = cx.dbuf("ysk", [128, 256], F32)
    Btm, t_Btm = cx.dbuf("Btm", [128, 128], BF16)
    daT, t_daT = cx.dbuf("daT", [128, 4, 128], F32)
    nacs, t_nacs = cx.dbuf("nacs", [128, 4], F32)
    e_sb, t_e = cx.dbuf("e_sb", [128, 4], F32)
    w_sb, t_wd = cx.dbuf("w_sb", [128, 4], F32)
    dec, t_dec = cx.dbuf("dec", [128, 4], F32)
    Dm, t_Dm = cx.dbuf("Dm", [128, 4, 128], F32)
    Gm, t_G = cx.dbuf("Gm", [128, 4, 128], BF16)
    yc, t_yc = cx.dbuf("yc", [128, 256], F32)
    sq, t_sq = cx.dbuf("sq", [128, 256], F32)
    ss, t_ss = cx.dbuf("ss", [128, 2], F32)
    yno, t_yno = cx.dbuf("yno", [128, 256], BF16)
    xdw, t_xdw = cx.dbuf("xdw", [128, 256], BF16)

    hT_v = hT.rearrange("(k p) t -> p k t", p=128)
    p.add("pool", lambda e: e.memset(xpre[1][:, :, :], 0.0), w=[t_xpre[1]])

    def load_block(b):
        fb, hb = b % 2, b % 3
        for half in range(2):
            cx.dma(hTf[fb][:, half * 4:(half + 1) * 4, :], hT_v[:, half * 4:(half + 1) * 4, b * BLK:(b + 1) * BLK],
                   w=[t_hTf[fb]])
        p.add("act", lambda e, fb=fb, hb=hb: e.activation(out=hTb[hb][:, 0:4, :], in_=hTf[fb][:, 0:4, :], func=AF.Copy),
              r=[t_hTf[fb]], w=[t_hTb[hb]])
        p.add("dve", lambda e, fb=fb, hb=hb: e.tensor_copy(out=hTb[hb][:, 4:8, :], in_=hTf[fb][:, 4:8, :]),
              r=[t_hTf[fb]], w=[t_hTb[hb]])

    load_block(0)
    if nblocks > 1:
        load_block(1)
    bc4 = lambda ap: ap.unsqueeze(2).broadcast_to([128, 4, 64])
    v464 = lambda ap: ap.rearrange("p (h c) -> p h c", h=4)
    def blockpro(b):
        pb = b % 2
        hb = b % 3
        if b + 2 < nblocks:
            load_block(b + 2)
        p.add("pool", lambda e, pb=pb: e.tensor_copy(out=xpre[pb][:, :, 0:3], in_=xpre[1 - pb][:, :, BLK:BLK + 3]),
              r=[t_xpre[1 - pb]], w=[t_xpre[pb]])
        for ct in range(4):
            bank, bt = bbank()
            for k in range(8):
                p.add("pe", lambda e, bank=bank, k=k, ct=ct, hb=hb: e.matmul(
                    bank, lhsT=wfm_sb[:, k, ct * 128:(ct + 1) * 128], rhs=hTb[hb][:, k, :],
                    start=(k == 0), stop=(k == 7)), r=[t_w, t_hTb[hb]], w=[bt])
            p.add("act", lambda e, bank=bank, ct=ct, pb=pb: e.activation(
                out=xpre[pb][:, ct, 3:3 + BLK], in_=bank, func=AF.Copy), r=[bt], w=[t_xpre[pb]])
        for ct in range(4):
            a = ct % 2
            p.add("act", lambda e, ct=ct, pb=pb, a=a: e.activation(
                out=acc[a][:, :], in_=xpre[pb][:, ct, 0:BLK], func=AF.Identity,
                scale=cw_sb[:, ct, 0:1], bias=cb_sb[:, ct:ct + 1]), r=[t_xpre[pb], t_c], w=[t_acc[a]])
            for k in range(1, 4):
                p.add("dve", lambda e, ct=ct, pb=pb, a=a, k=k: e.scalar_tensor_tensor(
                    out=acc[a][:, :], in0=xpre[pb][:, ct, k:k + BLK], scalar=cw_sb[:, ct, k:k + 1], in1=acc[a][:, :],
                    op0=ALU.mult, op1=ALU.add), r=[t_xpre[pb], t_acc[a], t_c], w=[t_acc[a]])
            p.add("act", lambda e, ct=ct, pb=pb, a=a: e.activation(
                out=xbcT[pb][:, ct, :], in_=acc[a][:, :], func=AF.Silu), r=[t_acc[a]], w=[t_xbcT[pb]])
        for t in range(4):
            bank, bt = bbank()
            for k in range(8):
                p.add("pe", lambda e, bank=bank, k=k, t=t, hb=hb: e.matmul(
                    bank[:, 0:260], lhsT=hTb[hb][:, k, t * 128:(t + 1) * 128], rhs=wtm_sb[:, k, :],
                    start=(k == 0), stop=(k == 7)), r=[t_w, t_hTb[hb]], w=[bt])
            p.add("act", lambda e, bank=bank, t=t, pb=pb: e.activation(
                out=zs[pb][:, t, :], in_=bank[:, 0:256], func=AF.Silu), r=[bt], w=[t_zs[pb]])
            p.add("dve", lambda e, bank=bank, t=t, pb=pb: e.tensor_tensor(
                out=dtr[pb][:, t, :], in0=bank[:, 256:260], in1=hp_sb[:, 0, :], op=ALU.add),
                r=[bt, t_c], w=[t_dtr[pb]])
        p.add("act", lambda e, pb=pb: e.activation(out=dte[pb][:, :, :], in_=dtr[pb][:, :, :], func=AF.Exp),
              r=[t_dtr[pb]], w=[t_dte[pb]])
        p.add("act", lambda e, pb=pb: e.activation(out=dts[pb][:, :, :], in_=dte[pb][:, :, :], func=AF.Ln,
                                                   bias=onecol[:, 0:1], scale=1.0), r=[t_dte[pb], t_c], w=[t_dts[pb]])
        p.add("dve", lambda e, pb=pb: e.tensor_tensor(
            out=das[pb][:, :, :], in0=dts[pb][:, :, :], in1=aneg[:, :].unsqueeze(1).broadcast_to([128, 4, 4]),
            op=ALU.mult), r=[t_dts[pb], t_c], w=[t_das[pb]])
    def front(n):
        if True:
            b, t = divmod(n, 4)
            pb, q, c0 = b % 2, n % 2, t * 128
            for j in range(3):
                p.add("pe", lambda e, j=j, pb=pb, c0=c0: e.transpose(
                    psB[:, j * 128:(j + 1) * 128], xbcT[pb][:, j, c0:c0 + 128], ident_b),
                    r=[t_xbcT[pb], t_c], w=[t_psB])
            p.add("dve", lambda e, q=q, pb=pb, t=t: e.tensor_tensor(
                out=v464(xdt[q][:, :]), in0=v464(psB[:, 0:256]), in1=bc4(dts[pb][:, t, :]), op=ALU.mult),
                r=[t_psB, t_dts[pb]], w=[t_xdt[q]])
            p.add("dve", lambda e, q=q: e.tensor_tensor(
                out=v464(ysk[q][:, :]), in0=v464(psB[:, 0:256]), in1=bc4(hp_sb[:, 2, :]), op=ALU.mult),
                r=[t_psB, t_c], w=[t_ysk[q]])
            p.add("act", lambda e, q=q: e.activation(out=Btm[q][:, :], in_=psB[:, 256:384], func=AF.Copy),
                  r=[t_psB], w=[t_Btm[q]])
            bankA, btA = tbank()
            p.add("pe", lambda e, bankA=bankA, pb=pb, t=t: e.matmul(
                bankA[:, 0:4], lhsT=triu, rhs=das[pb][:, t, :], start=True, stop=True), r=[t_c, t_das[pb]], w=[btA])
            p.add("dve", lambda e, q=q, pb=pb, t=t: e.tensor_tensor(
                out=daT[q][:, :, :], in0=triu.unsqueeze(1).broadcast_to([128, 4, 128]),
                in1=das[pb][:, t, :].unsqueeze(2).broadcast_to([128, 4, 128]), op=ALU.mult),
                r=[t_c, t_das[pb]], w=[t_daT[q]])
            bankR, btR = tbank()
            p.add("pe", lambda e, bankR=bankR, q=q: e.matmul(
                bankR, lhsT=ones_f, rhs=daT[q][:, :, :].rearrange("p h l -> p (h l)"), start=True, stop=False),
                r=[t_c, t_daT[q]], w=[btR])
            p.add("pe", lambda e, bankR=bankR: e.matmul(bankR, lhsT=ident_b, rhs=negmask, start=False, stop=True),
                  r=[t_c], w=[btR])
            p.add("act", lambda e, bankA=bankA, q=q: e.activation(
                out=nacs[q][:, :], in_=bankA[:, 0:4], func=AF.Identity, scale=-1.0), r=[btA], w=[t_nacs[q]])
            p.add("act", lambda e, bankA=bankA, q=q: e.activation(out=e_sb[q][:, :], in_=bankA[:, 0:4], func=AF.Exp),
                  r=[btA], w=[t_e[q]])
            Rv = bankR.rearrange("p (h l) -> p h l", h=4)
            for h in range(4):
                p.add("act", lambda e, bankR=bankR, q=q, h=h: e.activation(
                    out=Dm[q][:, h, :], in_=bankR[:, h * 128:(h + 1) * 128], func=AF.Exp,
                    bias=nacs[q][:, h:h + 1], scale=1.0), r=[btR, t_nacs[q]], w=[t_Dm[q]])
            p.add("dve", lambda e, Rv=Rv, q=q: e.tensor_tensor(
                out=w_sb[q][:, :], in0=Rv[:, :, 127], in1=nacs[q][:, :], op=ALU.add),
                r=[btR, t_nacs[q]], w=[t_wd[q]])
            p.add("act", lambda e, q=q: e.activation(out=w_sb[q][:, :], in_=w_sb[q][:, :], func=AF.Exp),
                  r=[t_wd[q]], w=[t_wd[q]])
            p.add("act", lambda e, Rv=Rv, q=q: e.activation(out=dec[q][:, :], in_=Rv[:, :, 127], func=AF.Exp),
                  r=[btR], w=[t_dec[q]])
            bankC, btC = bankA[:, 128:256], btA
            p.add("pe", lambda e, bankC=bankC, pb=pb, c0=c0: e.matmul(
                bankC, lhsT=xbcT[pb][:, 2, c0:c0 + 128], rhs=xbcT[pb][:, 3, c0:c0 + 128],
                start=True, stop=True), r=[t_xbcT[pb]], w=[btC])
            p.add("dve", lambda e, bankC=bankC, q=q: e.tensor_tensor(
                out=Gm[q][:, :, :], in0=Dm[q][:, :, :], in1=bankC.unsqueeze(1).broadcast_to([128, 4, 128]),
                op=ALU.mult), r=[btC, t_Dm[q]], w=[t_G[q]])
            bankY, btY = psYS[:, q, :], t_YS[q]
            for h in range(4):
                p.add("pe", lambda e, bankY=bankY, q=q, h=h: e.matmul(
                    bankY[:, h * 64:(h + 1) * 64], lhsT=Gm[q][:, h, :], rhs=xdt[q][:, h * 64:(h + 1) * 64],
                    start=True, stop=True), r=[t_G[q], t_xdt[q]], w=[btY])
            p.add("pool", lambda e, q=q: e.tensor_tensor(
                out=v464(xdw[q][:, :]), in0=v464(xdt[q][:, :]), in1=bc4(w_sb[q][:, :]), op=ALU.mult),
                r=[t_xdt[q], t_wd[q]], w=[t_xdw[q]])
            p.add("pe", lambda e, q=q: e.matmul(
                psYS[:, q, 256:512], lhsT=Btm[q][:, :], rhs=xdw[q][:, :], start=True, stop=True),
                r=[t_Btm[q], t_xdw[q]], w=[t_YS[q]])
    def back(n):
        if True:
            b, t = divmod(n, 4)
            pb, q, c0 = b % 2, n % 2, t * 128
            row0 = b * BLK + c0
            bankY, btY = psYS[:, q, :], t_YS[q]
            bankO, btO = tbank()
            p.add("pe", lambda e, bankO=bankO, pb=pb, c0=c0: e.matmul(
                bankO[:, 0:256], lhsT=xbcT[pb][:, 3, c0:c0 + 128], rhs=S_bf[:, :], start=True, stop=True),
                r=[t_xbcT[pb], t_Sbf], w=[btO])
            p.add("dve", lambda e, bankO=bankO, q=q: e.tensor_tensor(
                out=v464(yc[q][:, :]), in0=v464(bankO[:, 0:256]), in1=bc4(e_sb[q][:, :]), op=ALU.mult),
                r=[btO, t_e[q]], w=[t_yc[q]])
            p.add("dve", lambda e, bankY=bankY, q=q: e.tensor_tensor(
                out=yc[q][:, :], in0=yc[q][:, :], in1=bankY[:, 0:256], op=ALU.add), r=[btY, t_yc[q]], w=[t_yc[q]])
            p.add("pool", lambda e, q=q: e.tensor_tensor(out=yc[q][:, :], in0=yc[q][:, :], in1=ysk[q][:, :], op=ALU.add),
                  r=[t_yc[q], t_ysk[q]], w=[t_yc[q]])
            p.add("pool", lambda e, q=q, pb=pb, t=t: e.tensor_tensor(
                out=yc[q][:, :], in0=yc[q][:, :], in1=zs[pb][:, t, :], op=ALU.mult),
                r=[t_yc[q], t_zs[pb]], w=[t_yc[q]])
            p.add("act", lambda e, q=q: e.activation(out=sq[q][:, :], in_=yc[q][:, :], func=AF.Square,
                                                     accum_out=ss[q][:, 0:1]), r=[t_yc[q]], w=[t_sq[q], t_ss[q]])
            p.add("dve", lambda e, q=q: e.tensor_scalar(out=ss[q][:, 1:2], in0=ss[q][:, 0:1], scalar1=1.0 / 256.0,
                                                        scalar2=RMS_EPS, op0=ALU.mult, op1=ALU.add),
                  r=[t_ss[q]], w=[t_ss[q]])
            p.add("pool", lambda e, q=q: e.tensor_tensor(out=ss[q][:, 1:2], in0=ss[q][:, 1:2], in1=neghalf[:, 0:1],
                                                         op=ALU.pow), r=[t_ss[q], t_c], w=[t_ss[q]])
            p.add("dve", lambda e, q=q: e.scalar_tensor_tensor(
                out=yno[q][:, :], in0=yc[q][:, :], scalar=ss[q][:, 1:2], in1=nw_sb[:, :], op0=ALU.mult, op1=ALU.mult),
                r=[t_yc[q], t_ss[q], t_c], w=[t_yno[q]])
            cx.dma(yn[row0:row0 + 128, :], yno[q][:, :], r=[t_yno[q]])
            p.add("pool", lambda e, q=q: e.tensor_tensor(
                out=v464(S[:, :]), in0=v464(S[:, :]), in1=bc4(dec[q][:, :]), op=ALU.mult),
                r=[t_S, t_dec[q]], w=[t_S])
            p.add("dve", lambda e, q=q: e.tensor_tensor(out=S[:, :], in0=S[:, :], in1=psYS[:, q, 256:512], op=ALU.add),
                  r=[t_S, t_YS[q]], w=[t_S])
            p.add("act", lambda e: e.activation(out=S_bf[:, :], in_=S[:, :], func=AF.Copy), r=[t_S], w=[t_Sbf])
    ntiles = nblocks * 4
    blockpro(0)
    front(0)
    chunks = []
    for n in range(ntiles):
        b, t = divmod(n, 4)
        lists = []
        if t == 0 and b + 1 < nblocks:
            bp = p.record(blockpro, b + 1)
            c = (len(bp) + 2) // 3
            chunks = [bp[0:c], bp[c:2 * c], bp[2 * c:]]
        if t < 3 and chunks:
            lists.append(chunks[t])
            if t == 2:
                chunks = []
        if n + 1 < ntiles:
            lists.append(p.record(front, n + 1))
        lists.append(p.record(back, n))
        p.merge(*lists)
    p.emit()
    return nc


def ssd_inputs(hT, w_in, conv_w, conv_b, dt_bias, a_log, d_skip, norm_w, g):
    xo, bo, co, do = 2048, 4096, 5120, 6144
    wfm = np.concatenate([w_in[:, xo + g * 256: xo + (g + 1) * 256], w_in[:, bo + g * 128: bo + (g + 1) * 128],
                          w_in[:, co + g * 128: co + (g + 1) * 128]], axis=1)
    wtm = np.concatenate([w_in[:, g * 256:(g + 1) * 256], w_in[:, do + 4 * g: do + 4 * g + 4]], axis=1)
    cidx = np.concatenate([np.arange(g * 256, (g + 1) * 256), 2048 + np.arange(g * 128, (g + 1) * 128),
                           3072 + np.arange(g * 128, (g + 1) * 128)])
    cw = np.ascontiguousarray(conv_w[:, cidx].reshape(4, 4, 128).transpose(2, 1, 0))
    cb = np.ascontiguousarray(conv_b[cidx].reshape(4, 128).T)
    hp = np.stack([dt_bias[4 * g:4 * g + 4], a_log[4 * g:4 * g + 4], d_skip[4 * g:4 * g + 4]])
    hp = np.ascontiguousarray(np.broadcast_to(hp[None], (128, 3, 4)))
    nw = np.ascontiguousarray(np.broadcast_to(norm_w[None, g * 256:(g + 1) * 256], (128, 256)))
    return {"hT": hT, "wfm": np.ascontiguousarray(wfm), "wtm": np.ascontiguousarray(wtm), "cw": cw, "cb": cb,
            "hp": hp.astype(np.float32), "nw": nw.astype(np.float32), "cst_f": SSD_CST_F, "cst_b": SSD_CST_B}


def _ssd_consts():
    k = np.arange(128)
    triu = (k[:, None] <= k[None, :]).astype(np.float32)
    cst_f = np.ascontiguousarray(np.stack([triu, np.ones((128, 128), np.float32)], axis=1))
    neg = np.where(k[:, None] > k[None, :], -30000.0, 0.0).astype(np.float32)
    cst_b = np.concatenate([np.eye(128, dtype=np.float32), np.tile(neg, (1, 4))], axis=1).astype(ml_dtypes.bfloat16)
    return cst_f, np.ascontiguousarray(cst_b)


SSD_CST_F, SSD_CST_B = _ssd_consts()


def build_attn(seq=SEQ, same_kv=True, nd1=8, nd2=0):
    cx = Ctx(nbanks=8)
    nc, p = cx.nc, cx.p
    BLK = 512
    nblocks = seq // BLK
    nkb = seq // 128
    QG = 8
    nqg = nkb // QG

    hTq = cx.dram_in("hTq", [D, seq], F32)
    wqkv = cx.dram_in("wqkv", [D, 384], F32)
    if same_kv:
        kT_out = cx.dram_out("kT_out", [128, seq], BF16)
        V_out = cx.dram_out("V_out", [128, nkb * 128], BF16)
    else:
        kT_in = cx.dram_in("kT_in", [128, seq], BF16)
        V_in = cx.dram_in("V_in", [128, nkb * 128], BF16)
    cst = cx.dram_in("acst", [128, 1024], BF16)
    negm_d = cx.dram_in("negm", [64, 1], F32)
    oT = cx.dram_out("oT", [128, seq], BF16)

    wqkv_sb = cx.sb("wqkv_sb", [128, 8, 384], BF16)
    wq_sb, wk_sb, wv_sb = wqkv_sb[:, :, 0:128], wqkv_sb[:, :, 128:256], wqkv_sb[:, :, 256:384]
    cst_sb = cx.sb("cst_sb", [128, 1024], BF16)
    negm = cx.sb("negm_sb", [64, 1], F32)
    qT = cx.sb("qT", [128, seq], BF16)
    kT = cx.sb("kT", [128, seq], BF16)
    V = cx.sb("V", [128, nkb, 128], BF16)
    t_w, t_c, t_q, t_k, t_v = Tok("w"), Tok("c"), Tok("q"), Tok("k"), Tok("v")
    negMinc, diagmask = cst_sb[:, 0:128], cst_sb[:, 128:256]
    ones_b, ident_b, zero_b = cst_sb[:, 256:384], cst_sb[:, 384:512], cst_sb[:, 512:1024]
    ones_col = cx.sb("ones_col", [128, 1], F32)
    negones = cx.sb("negones", [128, 2], BF16)
    p.add("pool", lambda e: e.memset(ones_col[:, :], 1.0), w=[t_c])
    p.add("pool", lambda e: e.memset(negones[:, :], -1.0), w=[t_c])

    cx.dma(cst_sb[:, :], cst[:, :], w=[t_c])
    cx.dma(negm[:, :], negm_d[:, :], w=[t_c])
    wv_ = wqkv.rearrange("(k p) n -> p k n", p=128)
    for half in range(2):
        cx.dma_cast(wqkv_sb[:, half * 4:(half + 1) * 4, :], wv_[:, half * 4:(half + 1) * 4, :], w=[t_w])

    hTf, t_hTf = cx.dbuf("hTf", [128, 8, BLK], F32)
    hTb, t_hTb = cx.dbuf("hTb", [128, 8, BLK], BF16)

    passes = [("qkv", hTq)] if same_kv else [("q", hTq)]
    if not same_kv:
        for c4 in range(4):
            cs = slice(c4 * (seq // 4), (c4 + 1) * (seq // 4))
            cx.dma(kT[:, cs], kT_in[:, cs], w=[t_k])
        Vf = V[:, :, :].rearrange("p b c -> p (b c)")
        for c4 in range(4):
            cs = slice(c4 * (nkb * 32), (c4 + 1) * (nkb * 32))
            cx.dma(Vf[:, cs], V_in[:, cs], w=[t_v])
    cnt = 0
    for what, src in passes:
        src_v = src.rearrange("(k p) t -> p k t", p=128)

        def load_block(b, cnt, src_v=src_v):
            pb = cnt % 2
            for half in range(2):
                cx.dma(hTf[pb][:, half * 4:(half + 1) * 4, :], src_v[:, half * 4:(half + 1) * 4, b * BLK:(b + 1) * BLK],
                       w=[t_hTf[pb]])

        load_block(0, cnt)
        for b in range(nblocks):
            pb = cnt % 2
            if b + 1 < nblocks:
                load_block(b + 1, cnt + 1)
            cnt += 1
            p.add("act", lambda e, pb=pb: e.activation(out=hTb[pb][:, 0:3, :], in_=hTf[pb][:, 0:3, :], func=AF.Copy),
                  r=[t_hTf[pb]], w=[t_hTb[pb]])
            p.add("dve", lambda e, pb=pb: e.tensor_copy(out=hTb[pb][:, 3:6, :], in_=hTf[pb][:, 3:6, :]),
                  r=[t_hTf[pb]], w=[t_hTb[pb]])
            p.add("pool", lambda e, pb=pb: e.tensor_copy(out=hTb[pb][:, 6:8, :], in_=hTf[pb][:, 6:8, :]),
                  r=[t_hTf[pb]], w=[t_hTb[pb]])
            cols = slice(b * BLK, (b + 1) * BLK)
            if "q" in what:
                bank, bt = cx.bank()
                for k in range(8):
                    p.add("pe", lambda e, bank=bank, k=k, pb=pb: e.matmul(
                        bank, lhsT=wq_sb[:, k, :], rhs=hTb[pb][:, k, :], start=(k == 0), stop=(k == 7)),
                        r=[t_w, t_hTb[pb]], w=[bt])
                p.add("act", lambda e, bank=bank, cols=cols: e.activation(out=qT[:, cols], in_=bank, func=AF.Copy, scale=0.125),
                      r=[bt], w=[t_q])
            if "k" in what:
                bank, bt = cx.bank()
                for k in range(8):
                    p.add("pe", lambda e, bank=bank, k=k, pb=pb: e.matmul(
                        bank, lhsT=wk_sb[:, k, :], rhs=hTb[pb][:, k, :], start=(k == 0), stop=(k == 7)),
                        r=[t_w, t_hTb[pb]], w=[bt])
                p.add("dve", lambda e, bank=bank, cols=cols: e.tensor_copy(out=kT[:, cols], in_=bank), r=[bt], w=[t_k])
                bank, bt = cx.bank()
                for t in range(4):
                    for k in range(8):
                        p.add("pe", lambda e, bank=bank, k=k, t=t, pb=pb: e.matmul(
                            bank[:, t * 128:(t + 1) * 128], lhsT=hTb[pb][:, k, t * 128:(t + 1) * 128], rhs=wv_sb[:, k, :],
                            start=(k == 0), stop=(k == 7)), r=[t_w, t_hTb[pb]], w=[bt])
                p.add("act", lambda e, bank=bank, b=b: e.activation(
                    out=V[:, 4 * b:4 * b + 4, :], in_=bank.rearrange("p (t c) -> p t c", t=4), func=AF.Copy),
                    r=[bt], w=[t_v])

    if same_kv:
        Vf = V[:, :, :].rearrange("p b c -> p (b c)")
        for c4 in range(4):
            cs = slice(c4 * (seq // 4), (c4 + 1) * (seq // 4))
            cx.dma(kT_out[:, cs], kT[:, cs], r=[t_k])
            cs = slice(c4 * (nkb * 32), (c4 + 1) * (nkb * 32))
            cx.dma(V_out[:, cs], Vf[:, cs], r=[t_v])

    NQ = QG * 128
    zb = [cx.ps[:, 0:2, :].rearrange("p b c -> p (b c)"), cx.ps[:, 2:4, :].rearrange("p b c -> p (b c)")]
    ob = cx.ps[:, 4:6, :].rearrange("p b c -> p (b c)")
    accb = cx.ps[:, 6:8, :].rearrange("p b c -> p (b c)")
    t_z = [Tok("zA", excl=True), Tok("zB", excl=True)]
    t_zp = [[Tok(f"z{h}{i}", excl=True) for i in range(2)] for h in range(2)]
    t_Wp = [[Tok(f"W{h}{i}") for i in range(2)] for h in range(2)]
    t_Ep = [[Tok(f"E{h}{i}") for i in range(2)] for h in range(2)]
    t_Lp = [[Tok(f"L{h}{i}") for i in range(2)] for h in range(2)]
    t_o, t_acc = Tok("o", excl=True), [Tok("accA", excl=True), Tok("accB", excl=True)]
    for hd in range(2):
        for bnk in (2 * hd, 2 * hd + 1):
            t_zp[hd][bnk - 2 * hd].rs.update(cx.ps_tok[bnk].rs)
    for bnk in (4, 5):
        t_o.rs.update(cx.ps_tok[bnk].rs)
    for hd in range(2):
        for bnk in (6, 7):
            t_acc[hd].rs.update(cx.ps_tok[bnk].rs)
    E_sb, t_E = cx.dbuf("E_sb", [128, NQ], F32)
    L_sb, t_L = cx.dbuf("L_sb", [128, NQ], BF16)
    W_sb, t_W = cx.dbuf("W_sb", [128, NQ], BF16)
    hi_t = cx.sb("hi_t", [64, NQ], BF16)
    acc_hl = cx.sb("acc_hl", [64, NQ], BF16)
    t_him, t_hlm, t_accm = Tok("hi"), Tok("hl"), Tok("accm", excl=True)
    for bnk in (6, 7):
        t_accm.rs.update(cx.ps_tok[bnk].rs)
    obuf, t_ob = cx.dbuf("obuf", [128, NQ], BF16)
    rows = [slice(0, 2), slice(32, 34)]

    def dummies(n):
        for i in range(n):
            c0 = 512 * (i % 2)
            p.add("pe", lambda e, c0=c0: e.matmul(ob[:, c0:c0 + 512], lhsT=zero_b[:, 0:128], rhs=zero_b[:, 0:512],
                                                  start=False, stop=False, skip_group_check=True), r=[t_c], w=[t_o])

    def pieces(lo):
        out = []
        for c0 in (0, 512):
            a, bnd = max(lo, c0), c0 + 512
            if a < bnd:
                out.append((a, bnd))
        return out

    for qg in range(nqg):
        i0 = qg * QG
        qcol0 = i0 * 128
        for c0 in (0, 512):
            p.add("pe", lambda e, c0=c0: e.matmul(ob[:, c0:c0 + 512], lhsT=zero_b[:, 0:128], rhs=zero_b[:, 0:512],
                                                  start=True, stop=False, skip_group_check=True), r=[t_c], w=[t_o])
            p.add("pe", lambda e, c0=c0: e.matmul(
                accb[0:34, c0:c0 + 512], lhsT=zero_b[:, 0:34], rhs=zero_b[:, 0:512],
                start=True, stop=False, skip_group_check=True), r=[t_c], w=[t_accm])
        p.add("pool", lambda e: e.memset(acc_hl[0:34, :], 0.0), w=[t_hlm])
        def emit_z(j, hd):
            lo = max(j - i0, 0) * 128
            ks = slice(j * 128, (j + 1) * 128)
            hp_ = slice(hd * 64, (hd + 1) * 64)
            for (a, bnd) in pieces(lo):
                p.add("pe", lambda e, hd=hd, hp_=hp_, a=a, bnd=bnd, ks=ks, qcol0=qcol0: e.matmul(
                    zb[hd][:, a:bnd], lhsT=kT[hp_, ks], rhs=qT[hp_, qcol0 + a:qcol0 + bnd],
                    start=True, stop=False, skip_group_check=True), r=[t_k, t_q], w=[t_zp[hd][a // 512]])
            if j >= i0:
                p.add("pe", lambda e, hd=hd, lo=lo: e.matmul(
                    zb[hd][:, lo:lo + 128], lhsT=ident_b, rhs=diagmask, start=False, stop=False,
                    skip_group_check=True), r=[t_c], w=[t_zp[hd][lo // 512]])

        jtop = i0 + QG - 1
        for hd in range(2):
            emit_z(jtop, hd)
        for j in range(jtop, -1, -1):
            lo = max(j - i0, 0) * 128
            pcs = pieces(lo)
            for hd in range(2):
                for (a, bnd) in pcs:
                    h2 = a // 512
                    p.add("act", lambda e, hd=hd, a=a, bnd=bnd: e.activation(
                        out=E_sb[hd][:, a:bnd], in_=zb[hd][:, a:bnd], func=AF.Exp),
                        r=[t_zp[hd][h2]], w=[t_Ep[hd][h2]])
                    p.add("act", lambda e, hd=hd, a=a, bnd=bnd: e.activation(
                        out=L_sb[hd][:, a:bnd], in_=E_sb[hd][:, a:bnd], func=AF.Ln, bias=ones_col[:, 0:1], scale=1.0),
                        r=[t_Ep[hd][h2], t_c], w=[t_Lp[hd][h2]])
            for hd in range(2):
                for (a, bnd) in pcs:
                    h2 = a // 512
                    p.add("pe", lambda e, hd=hd, a=a, bnd=bnd: e.matmul(
                        zb[hd][:, a:bnd], lhsT=negMinc, rhs=L_sb[hd][:, a:bnd], start=False, stop=False,
                        skip_group_check=True), r=[t_Lp[hd][h2], t_c], w=[t_zp[hd][h2]])
                    p.add("pe", lambda e, hd=hd, a=a, bnd=bnd: e.matmul(
                        zb[hd][:, a:bnd], lhsT=ones_b[rows[hd], :], rhs=acc_hl[rows[hd], a:bnd], start=False, stop=True,
                        skip_group_check=True), r=[t_hlm, t_c], w=[t_zp[hd][h2]])
            for hd in range(2):
                for (a, bnd) in pcs:
                    h2 = a // 512
                    p.add("act", lambda e, hd=hd, a=a, bnd=bnd: e.activation(
                        out=W_sb[hd][:, a:bnd], in_=zb[hd][:, a:bnd], func=AF.Exp),
                        r=[t_zp[hd][h2]], w=[t_Wp[hd][h2]])
            for (a, bnd) in pcs:
                for hd in range(2):
                    p.add("pe", lambda e, hd=hd, a=a, bnd=bnd: e.matmul(
                        accb[rows[hd], a:bnd], lhsT=negones[:, 0:2], rhs=L_sb[hd][:, a:bnd], start=False, stop=False,
                        skip_group_check=True), r=[t_Lp[hd][a // 512], t_c], w=[t_accm])
            if j > 0:
                p.add("dve", lambda e, lo=lo: e.tensor_copy(out=hi_t[0:34, lo:NQ], in_=accb[0:34, lo:NQ]),
                      r=[t_accm], w=[t_him])
                p.add("dve", lambda e, lo=lo: e.scalar_tensor_tensor(
                    out=acc_hl[0:34, lo:NQ], in0=hi_t[0:34, lo:NQ], scalar=negm[0:34, 0:1],
                    in1=accb[0:34, lo:NQ], op0=ALU.mult, op1=ALU.add), r=[t_him, t_accm, t_c], w=[t_hlm])
                for hd in range(2):
                    emit_z(j - 1, hd)
            for (a, bnd) in pcs:
                for hd in range(2):
                    p.add("pe", lambda e, hd=hd, a=a, bnd=bnd, j=j: e.matmul(
                        ob[hd * 64:(hd + 1) * 64, a:bnd], lhsT=V[:, j, hd * 64:(hd + 1) * 64], rhs=W_sb[hd][:, a:bnd],
                        start=False, stop=False, skip_group_check=True), r=[t_Wp[hd][a // 512], t_v], w=[t_o])
            dummies(nd1)
        ob_i = qg % 2
        p.add("dve", lambda e, ob_i=ob_i: e.tensor_copy(out=obuf[ob_i][:, :], in_=ob[:, :]), r=[t_o], w=[t_ob[ob_i]])
        cx.dma(oT[:, qcol0:qcol0 + NQ], obuf[ob_i][:, :], r=[t_ob[ob_i]])
    p.emit()
    return nc


def _attn_consts():
    k = np.arange(128)
    neg_minc = np.where(k[:, None] >= k[None, :], -1.0, 0.0)
    diag = np.where(k[:, None] >= k[None, :], -30000.0, 0.0)
    cst = np.concatenate([neg_minc, diag, np.ones((128, 128)), np.eye(128), np.zeros((128, 512))], axis=1)
    negm = np.zeros((64, 1), np.float32)
    negm[1, 0] = -1.0
    negm[33, 0] = -1.0
    return np.ascontiguousarray(cst.astype(ml_dtypes.bfloat16)), negm


ATT_CST, ATT_NEGM = _attn_consts()


_PROGS = {}


def _prog(key, builder):
    if key not in _PROGS:
        _PROGS[key] = builder()
    return _PROGS[key]


def _run(nc, in_maps):
    return run_bass_kernel_spmd(nc, in_maps, core_ids=list(range(NCORES))).results


def kernel(x, ssm_w_in, ssm_conv_w, ssm_conv_b, ssm_dt_bias, ssm_a_log, ssm_d, ssm_norm_w, ssm_w_out,
           sb_w_k, sb_w_v, sb_w_q, sb_w_o, mlp_w1, mlp_w2, ln_mix_g, ln_mix_b, ln_mlp_g, ln_mlp_b):
    f32 = lambda a: np.ascontiguousarray(np.asarray(a, dtype=np.float32))
    h = f32(x)[0]
    ident = np.eye(128, dtype=np.float32)
    kv_shared = None
    for layer in range(DEPTH):
        hT = np.ascontiguousarray(h.T)
        if layer < 2:
            nc = _prog("ssd", build_ssd)
            ins = [ssd_inputs(hT, f32(ssm_w_in[layer]), f32(ssm_conv_w[layer]), f32(ssm_conv_b[layer]),
                              f32(ssm_dt_bias[layer]), f32(ssm_a_log[layer]), f32(ssm_d[layer]),
                              f32(ssm_norm_w[layer]), g) for g in range(NCORES)]
            res = _run(nc, ins)
            Y = np.concatenate([np.asarray(res[g]["yn"]) for g in range(NCORES)], axis=1)
            yTs = [np.ascontiguousarray(Y[c * TOK:(c + 1) * TOK].T) for c in range(NCORES)]
            w_o, kin = f32(ssm_w_out[layer]), D_INNER
        else:
            j = layer - 2
            same = layer == 2
            nc = _prog(("attn", same), lambda: build_attn(SEQ, same_kv=same))
            ins = []
            for c in range(NCORES):
                sl = slice(c * 128, (c + 1) * 128)
                wqkv = np.concatenate([f32(sb_w_q[j][:, sl]), f32(sb_w_k[:, sl]), f32(sb_w_v[:, sl])], axis=1)
                d = {"hTq": hT, "wqkv": np.ascontiguousarray(wqkv), "acst": ATT_CST, "negm": ATT_NEGM}
                if not same:
                    d["kT_in"], d["V_in"] = kv_shared[c]
                ins.append(d)
            res = _run(nc, ins)
            if same:
                kv_shared = [(np.asarray(res[c]["kT_out"]), np.asarray(res[c]["V_out"])) for c in range(NCORES)]
            OT = np.concatenate([np.asarray(res[c]["oT"]) for c in range(NCORES)], axis=0)
            yTs = [np.ascontiguousarray(OT[:, c * TOK:(c + 1) * TOK]) for c in range(NCORES)]
            w_o, kin = f32(sb_w_o[j]), D
        nc = _prog(("t", kin), lambda: build_tphase(kin))
        lnp = np.stack([f32(ln_mix_g[layer]), f32(ln_mix_b[layer]), f32(ln_mlp_g[layer]), f32(ln_mlp_b[layer])])
        lnp = np.ascontiguousarray(np.broadcast_to(lnp[None], (128, 4, D)))
        w1, w2 = f32(mlp_w1[layer]), f32(mlp_w2[layer])
        ins = [{"yT": yTs[c], "h": np.ascontiguousarray(h[c * TOK:(c + 1) * TOK]), "w_o": w_o, "w1": w1, "w2": w2,
                "lnp": lnp, "ident": ident} for c in range(NCORES)]
        res = _run(nc, ins)
        h = np.concatenate([np.asarray(res[c]["h_out"]) for c in range(NCORES)], axis=0)
    return h[None].astype(np.float32)
```
